# Optimizing a Trainium2 kernel written in Bass

```python
import math
import jax, jax.numpy as jnp
from jax import lax
import numpy as np

D_MODEL = 1024
BATCH = 4
SEQ = 8192
DEPTH = 2

N_A_LAYERS = DEPTH // 2
N_B_LAYERS = DEPTH - N_A_LAYERS
CONV_WIDTH = 3
FFN_HIDDEN = -(-8 * D_MODEL // (3 * 256)) * 256
N_HEADS = 16
N_KV_GROUPS = 4
HEADS_PER_GROUP = N_HEADS // N_KV_GROUPS
HEAD_DIM = 64
L_CMP = 32
D_CMP = 16
L_SLC = 64
N_SEL = 16
WINDOW = 512
CMP_HIDDEN = 256
Q_BLOCK = 64
N_KV_SETS = 6
RMS_EPS = 1e-5
NEG_INF = -1e30
FORCE_BONUS = 1e4

kernel_name = "yoco_shortconv_nsa_hybrid"


def rms_norm(x, g):
    xf = x.astype(jnp.float32)
    y = xf * lax.rsqrt(jnp.mean(xf * xf, axis=-1, keepdims=True) + RMS_EPS)
    return (y * g.astype(jnp.float32)).astype(x.dtype)


def short_conv_mixer(h, w_in, conv_w, w_out):
    b_gate, c_gate, v = jnp.split(h @ w_in, 3, axis=-1)
    u = lax.conv_general_dilated(
        c_gate * v, conv_w[:, None, :], window_strides=(1,),
        padding=((CONV_WIDTH - 1, 0),), dimension_numbers=("NWC", "WIO", "NWC"),
        feature_group_count=D_MODEL)
    return (b_gate * u) @ w_out


def swiglu(h, w_gu, w_down):
    g, u = jnp.split(h @ w_gu, 2, axis=-1)
    return (jax.nn.silu(g) * u) @ w_down


def compress(kv_raw, pe, w1, w2):
    bsz, seq = kv_raw.shape[0], kv_raw.shape[1]
    n_cmp = (seq - L_CMP) // D_CMP + 1
    idx = jnp.arange(n_cmp)[:, None] * D_CMP + jnp.arange(L_CMP)[None, :]
    blocks = kv_raw[:, idx] + pe[None, None, :, None, :]
    blocks = blocks.transpose(0, 1, 3, 2, 4).reshape(bsz, n_cmp, N_KV_GROUPS, L_CMP * HEAD_DIM)
    return jax.nn.silu(blocks @ w1) @ w2


def shared_kv(h, kv_norm, w_kv, pe_k, w1_k, w2_k, pe_v, w1_v, w2_v):
    bsz, seq, _ = h.shape
    kv = (rms_norm(h, kv_norm) @ w_kv).reshape(bsz, seq, N_KV_SETS, N_KV_GROUPS, HEAD_DIM)
    k_cmp = compress(kv[:, :, 0], pe_k, w1_k, w2_k)
    v_cmp = compress(kv[:, :, 1], pe_v, w1_v, w2_v)
    return (k_cmp, v_cmp, kv[:, :, 2], kv[:, :, 3], kv[:, :, 4], kv[:, :, 5])


def nsa_mixer(h, w_qg, w_o, k_cmp, v_cmp, k_slc, v_slc, k_win, v_win):
    bsz, seq, _ = h.shape
    G, HPG, dh = N_KV_GROUPS, HEADS_PER_GROUP, HEAD_DIM
    qg = h @ w_qg
    q = (qg[..., :N_HEADS * dh] * (dh ** -0.5)).reshape(bsz, seq, G, HPG, dh)
    gates = jax.nn.sigmoid(qg[..., N_HEADS * dh:].astype(jnp.float32)).reshape(bsz, seq, G, HPG, 3)

    slopes = jnp.exp2(-8.0 * jnp.arange(1, N_HEADS + 1, dtype=jnp.float32) / N_HEADS).reshape(G, HPG)

    n_cmp = k_cmp.shape[1]
    cmp_start = jnp.arange(n_cmp) * D_CMP
    cmp_end = cmp_start + L_CMP - 1
    cmp_center = cmp_end.astype(jnp.float32) - (L_CMP - 1) / 2.0
    n_slc = seq // L_SLC
    n_sel = min(N_SEL, n_slc)
    slc_start = jnp.arange(n_slc) * L_SLC
    overlap = ((cmp_start[:, None] < slc_start[None, :] + L_SLC)
               & (cmp_start[:, None] + L_CMP > slc_start[None, :])).astype(jnp.float32)

    k_blk = k_slc.reshape(bsz, n_slc, L_SLC, G, dh).transpose(0, 3, 1, 2, 4)
    v_blk = v_slc.reshape(bsz, n_slc, L_SLC, G, dh).transpose(0, 3, 1, 2, 4)
    pad = ((0, 0), (WINDOW, 0), (0, 0), (0, 0))
    k_win_pad = jnp.pad(k_win, pad)
    v_win_pad = jnp.pad(v_win, pad)
    b_ix = jnp.arange(bsz)[:, None, None, None]
    g_ix = jnp.arange(G)[None, :, None, None]
    j_blk = jnp.arange(n_slc)

    def one_block(qb):
        q0 = qb * Q_BLOCK
        t = q0 + jnp.arange(Q_BLOCK)
        tf = t.astype(jnp.float32)
        qblk = lax.dynamic_slice_in_dim(q, q0, Q_BLOCK, axis=1)
        gblk = lax.dynamic_slice_in_dim(gates, q0, Q_BLOCK, axis=1)

        s = jnp.einsum("bqghd,bngd->bghqn", qblk, k_cmp).astype(jnp.float32)
        vis = cmp_end[None, :] <= t[:, None]
        s = s - slopes[:, :, None, None] * (tf[:, None] - cmp_center[None, :])
        p_cmp = jax.nn.softmax(jnp.where(vis, s, NEG_INF), axis=-1) * vis
        o_cmp = jnp.einsum("bghqn,bngd->bqghd", p_cmp.astype(v_cmp.dtype), v_cmp)

        imp = jnp.einsum("bghqn,nj->bgqj", p_cmp, overlap)
        cur = t // L_SLC
        valid = j_blk[None, :] <= cur[:, None]
        forced = ((j_blk[None, :] == 0) | (j_blk[None, :] == cur[:, None])
                  | (j_blk[None, :] == cur[:, None] - 1))
        imp = jnp.where(valid, imp + FORCE_BONUS * forced, NEG_INF)
        _, sel = lax.top_k(imp, n_sel)
        k_sel = k_blk[b_ix, g_ix, sel]
        v_sel = v_blk[b_ix, g_ix, sel]
        s = jnp.einsum("bqghd,bgqnld->bghqnl", qblk, k_sel).astype(jnp.float32)
        dist = t[None, None, :, None, None] - (sel[..., None] * L_SLC + jnp.arange(L_SLC))
        dist = dist[:, :, None]
        s = s - slopes[:, :, None, None, None] * dist.astype(jnp.float32)
        s = jnp.where(dist >= 0, s, NEG_INF)
        p = jax.nn.softmax(s.reshape(s.shape[:4] + (n_sel * L_SLC,)), axis=-1).reshape(s.shape)
        o_slc = jnp.einsum("bghqnl,bgqnld->bqghd", p.astype(v_sel.dtype), v_sel)

        kw = lax.dynamic_slice_in_dim(k_win_pad, q0, WINDOW + Q_BLOCK, axis=1)
        vw = lax.dynamic_slice_in_dim(v_win_pad, q0, WINDOW + Q_BLOCK, axis=1)
        spos = q0 - WINDOW + jnp.arange(WINDOW + Q_BLOCK)
        d = t[:, None] - spos[None, :]
        m = (d >= 0) & (d < WINDOW) & (spos >= 0)[None, :]
        s = jnp.einsum("bqghd,bkgd->bghqk", qblk, kw).astype(jnp.float32)
        s = s - slopes[:, :, None, None] * d.astype(jnp.float32)
        p = jax.nn.softmax(jnp.where(m, s, NEG_INF), axis=-1)
        o_win = jnp.einsum("bghqk,bkgd->bqghd", p.astype(vw.dtype), vw)

        o = (gblk[..., 0:1] * o_cmp + gblk[..., 1:2] * o_slc + gblk[..., 2:3] * o_win)
        return o.reshape(bsz, Q_BLOCK, N_HEADS * dh).astype(h.dtype)

    out = lax.map(one_block, jnp.arange(seq // Q_BLOCK))
    out = out.transpose(1, 0, 2, 3).reshape(bsz, seq, N_HEADS * dh)
    return out @ w_o


def setup_inputs(seed: int = 0) -> dict:
    key = jax.random.key(seed)
    ks = jax.random.split(key, 24)

    def w(k, shape, fan_in):
        return jax.random.normal(k, shape, jnp.float32) * (fan_in ** -0.5)

    def gain(k, shape):
        return 1.0 + 0.02 * jax.random.normal(k, shape, jnp.float32)

    D, F = D_MODEL, FFN_HIDDEN
    kv_w = N_KV_GROUPS * HEAD_DIM
    qg_w = N_HEADS * HEAD_DIM + 3 * N_HEADS
    return {
        "x": jax.random.normal(ks[0], (BATCH, SEQ, D), jnp.float32),
        "a_norm": gain(ks[1], (N_A_LAYERS, D)),
        "a_w_in": w(ks[2], (N_A_LAYERS, D, 3 * D), D),
        "a_conv": w(ks[3], (N_A_LAYERS, CONV_WIDTH, D), CONV_WIDTH),
        "a_w_out": w(ks[4], (N_A_LAYERS, D, D), D),
        "kv_norm": gain(ks[5], (D,)),
        "w_kv": w(ks[6], (D, N_KV_SETS * kv_w), D),
        "cmp_pe_k": 0.02 * jax.random.normal(ks[7], (L_CMP, HEAD_DIM), jnp.float32),
        "cmp_w1_k": w(ks[8], (L_CMP * HEAD_DIM, CMP_HIDDEN), L_CMP * HEAD_DIM),
        "cmp_w2_k": w(ks[9], (CMP_HIDDEN, HEAD_DIM), CMP_HIDDEN),
        "cmp_pe_v": 0.02 * jax.random.normal(ks[10], (L_CMP, HEAD_DIM), jnp.float32),
        "cmp_w1_v": w(ks[11], (L_CMP * HEAD_DIM, CMP_HIDDEN), L_CMP * HEAD_DIM),
        "cmp_w2_v": w(ks[12], (CMP_HIDDEN, HEAD_DIM), CMP_HIDDEN),
        "b_norm": gain(ks[13], (N_B_LAYERS, D)),
        "b_w_qg": w(ks[14], (N_B_LAYERS, D, qg_w), D),
        "b_w_o": w(ks[15], (N_B_LAYERS, N_HEADS * HEAD_DIM, D), N_HEADS * HEAD_DIM),
        "f_norm": gain(ks[16], (DEPTH, D)),
        "f_w_gu": w(ks[17], (DEPTH, D, 2 * F), D),
        "f_w_down": w(ks[18], (DEPTH, F, D), F),
        "final_norm": gain(ks[19], (D,)),
    }


def reference(x, a_norm, a_w_in, a_conv, a_w_out, kv_norm, w_kv, cmp_pe_k, cmp_w1_k, cmp_w2_k,
              cmp_pe_v, cmp_w1_v, cmp_w2_v, b_norm, b_w_qg, b_w_o, f_norm, f_w_gu, f_w_down,
              final_norm):
    h = x
    shared = None
    for layer in range(DEPTH):
        if layer < N_A_LAYERS:
            i = layer
            h = h + short_conv_mixer(rms_norm(h, a_norm[i]), a_w_in[i], a_conv[i], a_w_out[i])
        else:
            i = layer - N_A_LAYERS
            if shared is None:
                shared = shared_kv(h, kv_norm, w_kv, cmp_pe_k, cmp_w1_k, cmp_w2_k,
                                   cmp_pe_v, cmp_w1_v, cmp_w2_v)
            k_cmp, v_cmp, k_slc, v_slc, k_win, v_win = shared
            h = h + nsa_mixer(rms_norm(h, b_norm[i]), b_w_qg[i], b_w_o[i],
                              k_cmp, v_cmp, k_slc, v_slc, k_win, v_win)
        h = h + swiglu(rms_norm(h, f_norm[layer]), f_w_gu[layer], f_w_down[layer])
    return rms_norm(h, final_norm)
```

```python
import contextlib
import numpy as np
import concourse.bass as bass
import concourse.mybir as mybir
from concourse.bass_utils import run_bass_kernel_spmd

F32 = mybir.dt.float32
BF = mybir.dt.bfloat16
ACT = mybir.ActivationFunctionType
ALU = mybir.AluOpType

D = 1024
S = 8192
FF = 2816
NH = 16
NG = 4
DH = 64
TT = 512
NT = S // TT
NQ = S // 128
NEGM = -30000.0
EPS = 1e-5

ENGS = ("pe", "act", "dve", "pool", "sp")


class Buf:
    __slots__ = ("name", "last_w", "readers")

    def __init__(self, name):
        self.name = name
        self.last_w = None
        self.readers = []


class Op:
    __slots__ = ("eng", "fn", "deps", "dma", "sig", "sem", "has_dep", "prewait")

    def __init__(self, eng, fn, dma):
        self.eng = eng
        self.fn = fn
        self.dma = dma
        self.deps = []
        self.sig = None
        self.sem = None
        self.has_dep = False
        self.prewait = None


class Prog:
    def __init__(self, nc, n_dma_sems=24):
        self.nc = nc
        self.ops = {e: [] for e in ENGS}
        self.n_dma_sems = n_dma_sems
        self.nbuf = 0
        self.dmas_since_barrier = []

    def buf(self, name=None):
        self.nbuf += 1
        return Buf(name or f"b{self.nbuf}")

    def add(self, eng, fn, reads=(), writes=(), dma=False):
        op = Op(eng, fn, dma)
        deps = {}

        def dep(p, kind):
            if p is None or p is op:
                return
            if p.eng == eng and not p.dma:
                if eng == "pe":
                    return
                if kind in ("war", "waw"):
                    return
            deps[id(p)] = p

        for r in reads:
            dep(r.last_w, "raw")
        for w in writes:
            dep(w.last_w, "waw")
            for rd in w.readers:
                dep(rd, "war")
        for r in reads:
            if dma:
                r.readers.append(op)
            else:
                r.readers = [x for x in r.readers if x.dma or x.eng != eng]
                r.readers.append(op)
        for w in writes:
            w.last_w = op
            w.readers = []
        op.deps = list(deps.values())
        for p in op.deps:
            p.has_dep = True
        self.ops[eng].append(op)
        if dma:
            self.dmas_since_barrier.append(op)
        return op

    def barrier(self):
        lasts = []
        for e in ENGS:
            for op in reversed(self.ops[e]):
                if not op.dma:
                    lasts.append(op)
                    break
        dm = list(self.dmas_since_barrier)
        self.dmas_since_barrier = []
        for e in ENGS:
            op = Op(e, lambda eng: eng.nop(), False)
            op.deps = [p for p in lasts if p.eng != e] + dm
            for p in op.deps:
                p.has_dep = True
            self.ops[e].append(op)

    def emit(self, final_dma_ops=()):
        nc = self.nc
        if final_dma_ops:
            fin = Op("sp", lambda e: e.nop(), False)
            fin.deps = list(final_dma_ops)
            for p in fin.deps:
                p.has_dep = True
            self.ops["sp"].append(fin)
        with contextlib.ExitStack() as st:
            esem = {e: st.enter_context(nc.semaphore(f"s_{e}")) for e in ENGS}
            dsem = {e: [st.enter_context(nc.semaphore(f"d_{e}{i}")) for i in range(self.n_dma_sems)]
                    for e in ("sp", "pool", "act")}
            for e in ENGS:
                cnt = 0
                dcnt = 0
                for op in self.ops[e]:
                    if op.dma:
                        R = self.n_dma_sems
                        op.sem = dsem[e][dcnt % R]
                        op.sig = 16 * (dcnt // R + 1)
                        if op.sig > 16:
                            op.prewait = (op.sem, op.sig - 16)
                        dcnt += 1
                    elif op.has_dep:
                        cnt += 1
                        op.sem = esem[e]
                        op.sig = cnt
            block = st.enter_context(nc.Block())

            def run(e, eng_obj):
                waited = {}
                for op in self.ops[e]:
                    ws = {}
                    if op.prewait:
                        ws[id(op.prewait[0])] = op.prewait
                    for p in op.deps:
                        k = id(p.sem)
                        if k not in ws or ws[k][1] < p.sig:
                            ws[k] = (p.sem, p.sig)
                    for k, (s, v) in ws.items():
                        if waited.get(k, 0) >= v:
                            continue
                        waited[k] = v
                        eng_obj.wait_ge(s, v)
                    inst = op.fn(eng_obj)
                    if op.dma:
                        inst.then_inc(op.sem, 16)
                    elif op.has_dep:
                        inst.then_inc(op.sem, 1)

            @block.sync
            def _(sync):
                run("sp", sync)

            @block.tensor
            def _(t):
                run("pe", t)

            @block.vector
            def _(v):
                run("dve", v)

            @block.scalar
            def _(a):
                run("act", a)

            @block.gpsimd
            def _(g):
                run("pool", g)


class Arena:
    def __init__(self, nc, base, top):
        self.nc = nc
        self.base = (base + 31) // 32 * 32
        self.top = top
        self.cur = self.base
        self.n = 0

    def mark(self):
        return self.cur

    def reset(self, m):
        self.cur = m

    def alloc(self, shape, dtype):
        nb = 4 if dtype == F32 else 2
        sz = nb
        for s in shape[1:]:
            sz *= s
        sz = (sz + 31) // 32 * 32
        off = self.cur
        assert off + sz <= self.top, ("SBUF overflow", off + sz - self.base, self.top - self.base)
        self.cur += sz
        self.n += 1
        return self.nc.alloc_sbuf_tensor_at(f"t{self.n}", list(shape), dtype, offset=off).ap()


def dram_ap(t, offset, pat):
    return bass.AP(t, offset, [list(p) for p in pat])


def build(debug=False):
    nc = bass.Bass("TRN2", target_bir_lowering=False)
    P = Prog(nc)
    A = Arena(nc, nc.sbuf_base, nc.sbuf_top)

    def din(name, shape):
        return nc.dram_tensor(name, list(shape), F32, kind="ExternalInput")

    x_t = din("x", [S, D])
    a_norm_t = din("a_norm", [D]); a_w_in_t = din("a_w_in", [D, 3 * D]); a_conv_t = din("a_conv", [3, D])
    a_w_out_t = din("a_w_out", [D, D]); kv_norm_t = din("kv_norm", [D]); w_kv_t = din("w_kv", [D, 1536])
    pe_k_t = din("cmp_pe_k", [32, 64]); w1_k_t = din("cmp_w1_k", [2048, 256]); w2_k_t = din("cmp_w2_k", [256, 64])
    pe_v_t = din("cmp_pe_v", [32, 64]); w1_v_t = din("cmp_w1_v", [2048, 256]); w2_v_t = din("cmp_w2_v", [256, 64])
    b_norm_t = din("b_norm", [D]); b_w_qg_t = din("b_w_qg", [D, 1072]); b_w_o_t = din("b_w_o", [D, D])
    f_norm_t = din("f_norm", [2, D]); f_w_gu_t = din("f_w_gu", [2, D, 2 * FF]); f_w_down_t = din("f_w_down", [2, FF, D])
    final_norm_t = din("final_norm", [D])
    out_t = nc.dram_tensor("out", [S, D], F32, kind="ExternalOutput")

    def scr(name, shape, dt=BF):
        return nc.dram_tensor(name, list(shape), dt)

    win_s = scr("win_s", [128, 8, 3072]); wout_s = scr("wout_s", [128, 8, 1024])
    wgu_s = [scr(f"wgu_s{l}", [128, 8, 2 * FF]) for l in range(2)]
    wdn_s = [scr(f"wdn_s{l}", [128, 22, 1024]) for l in range(2)]
    wfm_s = scr("wfm_s", [128, 8, 1536]); wtm_s = scr("wtm_s", [128, 8, 512])
    wq_s = scr("wq_s", [128, 8, 1024]); wg_s = scr("wg_s", [128, 8, 48]); wo_s = scr("wo_s", [128, 8, 1024])
    w1_s = [scr(f"w1_s{i}", [128, 16, 256]) for i in range(2)]
    w2_s = [scr(f"w2_s{i}", [128, 2, 64]) for i in range(2)]
    kT_s = scr("kT_s", [8, 64, S])
    kc2_s = scr("kc2_s", [8, 128, S // 2])
    va_s = scr("va_s", [128, 8, 64, 65])
    qT_s = scr("qT_s", [NH, 64, S])
    gate_s = scr("gate_s", [S, 48], F32)
    h1_s = scr("h1_s", [S, D], F32)
    o_s = scr("o_s", [S, D])

    NBANK = 6
    banks = [nc.alloc_psum_tensor(f"pb{i}", [128, 512], F32).ap() for i in range(NBANK)]
    bbufs = [P.buf(f"pb{i}") for i in range(NBANK)]
    tbanks = [nc.alloc_psum_tensor(f"tb{i}", [128, 1024], BF).ap() for i in range(2)]
    tbufs = [P.buf(f"tb{i}") for i in range(2)]
    bstate = {"b": 0, "t": 0}

    def bank():
        i = bstate["b"]
        bstate["b"] = (i + 1) % NBANK
        return banks[i], bbufs[i]

    def tbank():
        i = bstate["t"]
        bstate["t"] = (i + 1) % 2
        return tbanks[i], tbufs[i]

    ident = A.alloc([128, 128], BF); b_ident = P.buf("ident")
    identf = A.alloc([128, 128], F32)
    gP = A.alloc([128, 6, 8], F32); b_gP = P.buf("gP")
    convP = A.alloc([128, 8, 3], F32); b_convP = P.buf("convP")
    finalg = A.alloc([128, D], F32); b_finalg = P.buf("finalg")
    ss = A.alloc([128, 4], F32); b_ss = P.buf("ss")
    rstd = A.alloc([128, 4], F32); b_rstd = P.buf("rstd")
    cvh = A.alloc([128, 8, 2], F32); b_cvh = [P.buf(f"cvh{f}") for f in range(8)]
    cbias = A.alloc([128, 2, 2], F32); b_cbias = P.buf("cbias")
    frame0 = A.mark()

    xs = A.alloc([128, 4, D], F32); b_xs = [P.buf(f"xs{j}") for j in range(4)]
    junk = A.alloc([128, D], BF); b_junk = P.buf("junk")
    hn = A.alloc([128, 4, D], BF); b_hn = [P.buf(f"hn{j}") for j in range(4)]
    hnT = A.alloc([128, 8, TT], BF); b_hnT = [P.buf(f"hnT{j}") for j in range(4)]
    actT = A.alloc([128, 22, TT], BF); b_actT = [P.buf(f"actT{k}") for k in range(22)]
    silu_t = [A.alloc([128, TT], F32) for _ in range(2)]; b_silu = [P.buf() for _ in range(2)]
    NSLOT = 5
    ring = [A.alloc([128, 8, 512], BF) for _ in range(NSLOT)]; b_ring = [P.buf(f"ring{i}") for i in range(NSLOT)]
    c_sb = [A.alloc([128, TT], F32) for _ in range(2)]; b_csb = [P.buf() for _ in range(2)]
    cv = [A.alloc([128, TT + 2], F32) for _ in range(2)]; b_cv = [P.buf() for _ in range(2)]
    u_t = [A.alloc([128, TT], F32) for _ in range(2)]; b_u = [P.buf() for _ in range(2)]
    buT = A.alloc([128, 8, TT], BF); b_buT = [P.buf(f"buT{k}") for k in range(8)]
    st_k = [A.alloc([128, TT], BF) for _ in range(2)]; b_stk = [P.buf() for _ in range(2)]
    st_kc = A.alloc([128, 8, 256], BF); b_stkc = P.buf("stkc")
    st_v = A.alloc([128, 4, 8, 65], BF); b_stv = P.buf("stv")
    st_g = A.alloc([128, 4, 48], F32); b_stg = P.buf("stg")
    dense_end = A.mark()

    def mk_ident(e):
        e.memset(identf, 0.0)
        return e.affine_select(out=identf, in_=identf, pattern=[[-1, 128]], compare_op=ALU.not_equal,
                               fill=1.0, base=0, channel_multiplier=1)
    P.add("pool", mk_ident, writes=[b_ident])
    P.add("dve", lambda e: e.tensor_copy(out=ident, in_=identf), reads=[b_ident], writes=[b_ident])
    gsrc = [a_norm_t.ap(), f_norm_t.ap()[0, :], kv_norm_t.ap(), b_norm_t.ap(), f_norm_t.ap()[1, :]]
    for i, g in enumerate(gsrc):
        P.add("sp", lambda e, i=i, g=g: e.dma_start(out=gP[:, i, :], in_=g.rearrange("(c p) -> p c", p=128),
                                                    allow_slow_non_contiguous=True), writes=[b_gP], dma=True)
    P.add("dve", lambda e: e.tensor_scalar_mul(out=gP[:, 5, :], in0=gP[:, 3, :], scalar1=0.125),
          reads=[b_gP], writes=[b_gP])
    for c in range(8):
        P.add("sp", lambda e, c=c: e.dma_start(
            out=convP[:, c, :], in_=a_conv_t.ap()[:, c * 128:(c + 1) * 128].rearrange("k p -> p k"),
            allow_slow_non_contiguous=True), writes=[b_convP], dma=True)
    P.add("sp", lambda e: e.dma_start(out=finalg, in_=final_norm_t.ap().unsqueeze(0).to_broadcast([128, D])),
          writes=[b_finalg], dma=True)
    P.add("pool", lambda e: e.memset(cvh, 0.0), writes=b_cvh)
    P.add("pool", lambda e: e.memset(st_v, 1.0), writes=[b_stv])

    stg_f = [xs[:, 0:2, :].rearrange("p a (b c) -> p (a b) c", c=512), xs[:, 2:4, :].rearrange("p a (b c) -> p (a b) c", c=512)]
    stg_b = [actT[:, 0:4, :], actT[:, 4:8, :]]
    b_sf = [P.buf("sf0"), P.buf("sf1")]
    b_sb = [P.buf("sb0"), P.buf("sb1")]
    prep_state = {"i": 0}
    wbufs = {}

    def wbuf(t):
        k = t.name
        if k not in wbufs:
            wbufs[k] = P.buf(k)
        return wbufs[k]

    def prep(src, k0, nk, c0, ncol, dsts, gain=None):
        i = prep_state["i"]
        prep_state["i"] += 1
        sl = i % 2
        sf = stg_f[sl][:, 0:nk, 0:ncol]
        sb = stg_b[sl][:, 0:nk, 0:ncol]
        srcv = src[k0 * 128:(k0 + nk) * 128, c0:c0 + ncol].rearrange("(k p) n -> p k n", p=128)
        P.add("sp", lambda e: e.dma_start(out=sf, in_=srcv), writes=[b_sf[sl]], dma=True)
        eng = ("dve", "act")[i % 2]
        if gain is None:
            if eng == "act":
                P.add("act", lambda e: e.activation(out=sb, in_=sf, func=ACT.Copy), reads=[b_sf[sl]], writes=[b_sb[sl]])
            else:
                P.add(eng, lambda e: e.tensor_copy(out=sb, in_=sf), reads=[b_sf[sl]], writes=[b_sb[sl]])
        else:
            def cast(e):
                for k in range(nk):
                    gcol = gP[:, gain, k0 + k:k0 + k + 1]
                    if eng == "act":
                        r = e.activation(out=sb[:, k, :], in_=sf[:, k, :], func=ACT.Copy, scale=gcol)
                    else:
                        r = e.tensor_scalar(out=sb[:, k, :], in0=sf[:, k, :], scalar1=gcol, scalar2=None, op0=ALU.mult)
                return r
            P.add(eng, cast, reads=[b_sf[sl], b_gP], writes=[b_sb[sl]])
        for (dt_, dk0, dc0) in dsts:
            dv = dt_.ap()[:, dk0:dk0 + nk, dc0:dc0 + ncol]
            P.add("pool", lambda e, dv=dv: e.dma_start(out=dv, in_=sb), reads=[b_sb[sl]], writes=[wbuf(dt_)], dma=True)

    def prep_mat(src, K, c0, ncols, dst, dc0, gain=None):
        nkc = K // 128
        for cc in range(0, ncols, 512):
            w = min(512, ncols - cc)
            for k0 in range(0, nkc, 4):
                nk = min(4, nkc - k0)
                prep(src, k0, nk, c0 + cc, w, [(dst, k0, dc0 + cc)], gain)

    for f in range(8):
        for k0 in (0, 4):
            prep(a_w_in_t.ap(), k0, 4, 1024 + f * 128, 128, [(win_s, k0, f * 384)], gain=0)
            prep(a_w_in_t.ap(), k0, 4, 2048 + f * 128, 128, [(win_s, k0, f * 384 + 128)], gain=0)
            prep(a_w_in_t.ap(), k0, 4, f * 128, 128, [(win_s, k0, f * 384 + 256)], gain=0)
    prep_mat(a_w_out_t.ap(), D, 0, 1024, wout_s, 0)
    for l in range(2):
        prep_mat(f_w_gu_t.ap()[l], D, 0, 2 * FF, wgu_s[l], 0, gain=(1 if l == 0 else 4))
        prep_mat(f_w_down_t.ap()[l], FF, 0, 1024, wdn_s[l], 0)
    wkv = w_kv_t.ap()
    for st_i, src_set in enumerate((0, 1)):
        for g in range(4):
            for k0 in (0, 4):
                prep(wkv, k0, 4, src_set * 256 + g * 64, 64,
                     [(wfm_s, k0, (st_i * 4 + g) * 128), (wfm_s, k0, (st_i * 4 + g) * 128 + 64)], gain=2)
    prep_mat(wkv, D, 2 * 256, 256, wfm_s, 1024, gain=2)
    prep_mat(wkv, D, 4 * 256, 256, wfm_s, 1280, gain=2)
    prep_mat(wkv, D, 3 * 256, 256, wtm_s, 0, gain=2)
    prep_mat(wkv, D, 5 * 256, 256, wtm_s, 256, gain=2)
    prep_mat(b_w_qg_t.ap(), D, 0, 1024, wq_s, 0, gain=5)
    prep_mat(b_w_qg_t.ap(), D, 1024, 48, wg_s, 0, gain=3)
    prep_mat(b_w_o_t.ap(), D, 0, 1024, wo_s, 0)
    for i, (w1, w2) in enumerate(((w1_k_t, w2_k_t), (w1_v_t, w2_v_t))):
        prep_mat(w1.ap(), 2048, 0, 256, w1_s[i], 0)
        prep_mat(w2.ap(), 256, 0, 64, w2_s[i], 0)

    ring_state = {"i": 0}

    def wload(dt_, k0, nk, c0, ncol):
        i = ring_state["i"]
        ring_state["i"] += 1
        sl = i % NSLOT
        dst = ring[sl][:, 0:nk, 0:ncol]
        srcv = dt_.ap()[:, k0:k0 + nk, c0:c0 + ncol]
        P.add("sp", lambda e: e.dma_start(out=dst, in_=srcv), reads=[wbuf(dt_)], writes=[b_ring[sl]], dma=True)
        return ring[sl], b_ring[sl]

    class WStream:
        def __init__(self, plan, depth=NSLOT - 1):
            self.plan = plan
            self.depth = depth
            self.loaded = []
            self.pos = 0
            for _ in range(min(depth, len(plan))):
                self._issue()

        def _issue(self):
            p = self.plan[len(self.loaded)]
            self.loaded.append(wload(*p))

        def get(self, expect=None):
            r = self.loaded[self.pos]
            if expect is not None:
                assert self.plan[self.pos][0] is expect, (self.plan[self.pos][0].name, expect.name)
            self.pos += 1
            return r

        def advance(self):
            if len(self.loaded) < len(self.plan):
                self._issue()

    def ffn_plan(l):
        pl = []
        for i in range(6):
            w = 512 if i < 5 else 256
            pl.append((wgu_s[l], 0, 8, i * 512, w))
            pl.append((wgu_s[l], 0, 8, FF + i * 512, w))
        for nh in range(2):
            for (k0, nk) in ((0, 8), (8, 8), (16, 6)):
                pl.append((wdn_s[l], k0, nk, nh * 512, 512))
        return pl

    def tile_plan1():
        pl = [(win_s, 0, 8, f * 384, 384) for f in range(8)]
        pl += [(wout_s, 0, 8, i * 512, 512) for i in range(2)]
        pl += ffn_plan(0)
        pl += [(wfm_s, 0, 8, i * 512, 512) for i in range(3)]
        pl += [(wtm_s, 0, 8, 0, 512)]
        pl += [(wq_s, 0, 8, i * 512, 512) for i in range(2)]
        pl += [(wg_s, 0, 8, 0, 48)]
        return pl

    def mm_group(ps, pb, lhs_fn, rhs_fn, nk, reads, n=None):
        def f(e):
            for k in range(nk):
                r = e.matmul(ps, lhsT=lhs_fn(k), rhs=rhs_fn(k), start=(k == 0), stop=(k == nk - 1))
            return r
        P.add("pe", f, reads=reads, writes=[pb])

    evict_rr = {"i": 0}

    def rstd_only():
        P.add("pool", lambda e: e.memset(ss, 0.0), writes=[b_ss])
        for j in range(4):
            P.add("act", lambda e, j=j: e.activation(out=junk, in_=xs[:, j, :], func=ACT.Square,
                                                      accum_out=ss[:, j:j + 1]),
                  reads=[b_xs[j]], writes=[b_ss, b_junk])
        P.add("dve", lambda e: e.tensor_scalar(out=rstd, in0=ss, scalar1=1.0 / D, scalar2=EPS, op0=ALU.mult, op1=ALU.add),
              reads=[b_ss], writes=[b_rstd])
        P.add("act", lambda e: e.activation(out=rstd, in_=rstd, func=ACT.Sqrt), reads=[b_rstd], writes=[b_rstd])
        P.add("dve", lambda e: e.reciprocal(out=rstd, in_=rstd), reads=[b_rstd], writes=[b_rstd])

    def transpose_hn():
        for j in range(4):
            tb, tbb = tbank()

            def tr(e, j=j, tb=tb):
                for k in range(8):
                    r = e.transpose(out=tb[:, k * 128:(k + 1) * 128], in_=hn[:, j, k * 128:(k + 1) * 128], identity=ident)
                return r
            P.add("pe", tr, reads=[b_hn[j], b_ident], writes=[tbb])
            eng = "act" if j % 2 == 0 else "dve"
            src = tb.rearrange("p (k t) -> p k t", k=8)
            dst = hnT[:, :, j * 128:(j + 1) * 128]
            if eng == "act":
                P.add("act", lambda e, src=src, dst=dst: e.activation(out=dst, in_=src, func=ACT.Copy),
                      reads=[tbb], writes=[b_hnT[j]])
            else:
                P.add("dve", lambda e, src=src, dst=dst: e.tensor_copy(out=dst, in_=src), reads=[tbb], writes=[b_hnT[j]])

    def norm_T():
        rstd_only()
        for j in range(4):
            if j % 2 == 0:
                P.add("act", lambda e, j=j: e.activation(out=hn[:, j, :], in_=xs[:, j, :], func=ACT.Copy, scale=rstd[:, j:j + 1]),
                      reads=[b_xs[j], b_rstd], writes=[b_hn[j]])
            else:
                P.add("dve", lambda e, j=j: e.tensor_scalar(out=hn[:, j, :], in0=xs[:, j, :], scalar1=rstd[:, j:j + 1],
                                                            scalar2=None, op0=ALU.mult),
                      reads=[b_xs[j], b_rstd], writes=[b_hn[j]])
        transpose_hn()

    def ffn(l, ws):
        for i in range(6):
            nch = 4 if i < 5 else 2
            gw, gb = ws.get(wgu_s[l])
            uw, ub = ws.get(wgu_s[l])
            for c in range(nch):
                fc = i * 4 + c
                pg, pgb = bank()
                mm_group(pg, pgb, lambda k, c=c, gw=gw: gw[:, k, c * 128:(c + 1) * 128], lambda k: hnT[:, k, :], 8,
                         [gb] + b_hnT)
                pu, pub = bank()
                mm_group(pu, pub, lambda k, c=c, uw=uw: uw[:, k, c * 128:(c + 1) * 128], lambda k: hnT[:, k, :], 8,
                         [ub] + b_hnT)
                sl = fc % 2
                P.add("act", lambda e, pg=pg, sl=sl: e.activation(out=silu_t[sl], in_=pg, func=ACT.Silu),
                      reads=[pgb], writes=[b_silu[sl]])
                P.add("dve", lambda e, pu=pu, sl=sl, fc=fc: e.tensor_tensor(out=actT[:, fc, :], in0=pu, in1=silu_t[sl], op=ALU.mult),
                      reads=[pub, b_silu[sl]], writes=[b_actT[fc]])
            ws.advance()
            ws.advance()
        for nh in range(2):
            pss = [bank() for _ in range(4)]
            for (k0, nk) in ((0, 8), (8, 8), (16, 6)):
                dw, db = ws.get(wdn_s[l]); ws.advance()
                for j in range(4):
                    ps, pb = pss[j]

                    def f(e, j=j, ps=ps, dw=dw, k0=k0, nk=nk):
                        for k in range(nk):
                            r = e.matmul(ps, lhsT=actT[:, k0 + k, j * 128:(j + 1) * 128], rhs=dw[:, k, :],
                                         start=(k0 + k == 0), stop=(k0 + k == 21))
                        return r
                    P.add("pe", f, reads=[db] + b_actT[k0:k0 + nk], writes=[pb])
            for j in range(4):
                ps, pb = pss[j]
                P.add("dve", lambda e, j=j, ps=ps, nh=nh: e.tensor_tensor(
                    out=xs[:, j, nh * 512:(nh + 1) * 512], in0=ps, in1=xs[:, j, nh * 512:(nh + 1) * 512], op=ALU.add),
                    reads=[pb, b_xs[j]], writes=[b_xs[j]])

    def down_proj_tm(ws, wt, srcT, b_src):
        for nh in range(2):
            w, wb = ws.get(wt); ws.advance()
            for j in range(4):
                ps, pb = bank()
                mm_group(ps, pb, lambda k, j=j: srcT[:, k, j * 128:(j + 1) * 128], lambda k, w=w: w[:, k, :], 8,
                         [wb] + b_src)
                P.add("dve", lambda e, j=j, ps=ps, nh=nh: e.tensor_tensor(
                    out=xs[:, j, nh * 512:(nh + 1) * 512], in0=ps, in1=xs[:, j, nh * 512:(nh + 1) * 512], op=ALU.add),
                    reads=[pb, b_xs[j]], writes=[b_xs[j]])

    b_kT = [P.buf(f"kT{t}") for t in range(NT)]
    b_kc2 = [P.buf(f"kc2{t}") for t in range(NT)]
    b_va = [P.buf(f"va{t}") for t in range(NT)]
    b_qT = [P.buf(f"qT{t}") for t in range(NT)]
    b_gate = [P.buf(f"gate{t}") for t in range(NT)]
    b_h1 = [P.buf(f"h1{t}") for t in range(NT)]
    b_o = [P.buf(f"o{t}") for t in range(NT)]
    out_ops = []

    x_ap = x_t.ap()

    P.barrier()

    kT_flat = kT_s.ap().rearrange("s d t -> (s d) t")
    qT_flat = qT_s.ap().rearrange("h d t -> (h d) t")

    def phase1(ntiles, final_stub=False):
        ws = WStream([p for _ in range(ntiles) for p in tile_plan1()])
        for t in range(ntiles):
            t0 = t * TT
            for j in range(4):
                P.add("sp", lambda e, t=t, t0=t0, j=j: e.dma_start(out=xs[:, j, :], in_=x_ap[t0 + j * 128:t0 + (j + 1) * 128, :]),
                      writes=[b_xs[j]], dma=True)
            norm_T()
            if debug and t == 0:
                d1 = nc.dram_tensor("dbg_hnT", [128, 8, TT], BF, kind="ExternalOutput")
                P.add("pool", lambda e: e.dma_start(out=d1.ap(), in_=hnT), reads=b_hnT, writes=[P.buf()], dma=True)
                d0 = nc.dram_tensor("dbg_rstd", [128, 4], F32, kind="ExternalOutput")
                P.add("pool", lambda e: e.dma_start(out=d0.ap(), in_=rstd), reads=[b_rstd], writes=[P.buf()], dma=True)
            for f in range(8):
                w, wb = ws.get(win_s); ws.advance()
                pc, pcb = bank()
                mm_group(pc, pcb, lambda k, w=w: w[:, k, 0:128], lambda k: hnT[:, k, :], 8, [wb] + b_hnT)
                pv, pvb = bank()
                mm_group(pv, pvb, lambda k, w=w: w[:, k, 128:256], lambda k: hnT[:, k, :], 8, [wb] + b_hnT)
                pq, pqb = bank()
                mm_group(pq, pqb, lambda k, w=w: w[:, k, 256:384], lambda k: hnT[:, k, :], 8, [wb] + b_hnT)
                sl = f % 2
                P.add("act", lambda e, pc=pc, sl=sl: e.activation(out=c_sb[sl], in_=pc, func=ACT.Copy),
                      reads=[pcb], writes=[b_csb[sl]])
                P.add("dve", lambda e, sl=sl, f=f: e.tensor_copy(out=cv[sl][:, 0:2], in_=cvh[:, f, :]),
                      reads=[b_cvh[f]], writes=[b_cv[sl]])
                P.add("dve", lambda e, pv=pv, sl=sl: e.tensor_tensor(out=cv[sl][:, 2:TT + 2], in0=pv, in1=c_sb[sl], op=ALU.mult),
                      reads=[pvb, b_csb[sl], b_cv[sl]], writes=[b_cv[sl]])
                P.add("dve", lambda e, sl=sl, f=f: e.tensor_copy(out=cvh[:, f, :], in_=cv[sl][:, TT:TT + 2]),
                      reads=[b_cv[sl]], writes=[b_cvh[f]])
                P.add("act", lambda e, sl=sl, f=f: e.activation(out=u_t[sl], in_=cv[sl][:, 2:TT + 2], func=ACT.Copy, scale=convP[:, f, 2:3]),
                      reads=[b_cv[sl], b_convP], writes=[b_u[sl]])
                P.add("dve", lambda e, sl=sl, f=f: e.scalar_tensor_tensor(out=u_t[sl], in0=cv[sl][:, 1:TT + 1], scalar=convP[:, f, 1:2],
                                                                           in1=u_t[sl], op0=ALU.mult, op1=ALU.add),
                      reads=[b_cv[sl], b_convP, b_u[sl]], writes=[b_u[sl]])
                P.add("dve", lambda e, sl=sl, f=f: e.scalar_tensor_tensor(out=u_t[sl], in0=cv[sl][:, 0:TT], scalar=convP[:, f, 0:1],
                                                                           in1=u_t[sl], op0=ALU.mult, op1=ALU.add),
                      reads=[b_cv[sl], b_convP, b_u[sl]], writes=[b_u[sl]])
                P.add("dve", lambda e, pq=pq, sl=sl, f=f: e.tensor_tensor(out=buT[:, f, :], in0=pq, in1=u_t[sl], op=ALU.mult),
                      reads=[pqb, b_u[sl]], writes=[b_buT[f]])
            if debug and t == 0:
                d6 = nc.dram_tensor("dbg_cvh", [128, 8, 2], F32, kind="ExternalOutput")
                P.add("pool", lambda e: e.dma_start(out=d6.ap(), in_=cvh), reads=b_cvh, writes=[P.buf()], dma=True)
            if debug and t == 1:
                d7 = nc.dram_tensor("dbg_buT1", [128, 8, TT], BF, kind="ExternalOutput")
                P.add("pool", lambda e: e.dma_start(out=d7.ap(), in_=buT), reads=b_buT, writes=[P.buf()], dma=True)
            if debug and t == 0:
                d2 = nc.dram_tensor("dbg_buT", [128, 8, TT], BF, kind="ExternalOutput")
                P.add("pool", lambda e: e.dma_start(out=d2.ap(), in_=buT), reads=b_buT, writes=[P.buf()], dma=True)
            down_proj_tm(ws, wout_s, buT, b_buT)
            if debug and t == 0:
                d3 = nc.dram_tensor("dbg_ha", [128, 4, D], F32, kind="ExternalOutput")
                P.add("pool", lambda e: e.dma_start(out=d3.ap(), in_=xs), reads=b_xs, writes=[P.buf()], dma=True)
            norm_T()
            if debug and t == 0:
                d4 = nc.dram_tensor("dbg_hnT2", [128, 8, TT], BF, kind="ExternalOutput")
                P.add("pool", lambda e: e.dma_start(out=d4.ap(), in_=hnT), reads=b_hnT, writes=[P.buf()], dma=True)
            ffn(0, ws)
            if debug and t == 0:
                d5 = nc.dram_tensor("dbg_actT", [128, 22, TT], BF, kind="ExternalOutput")
                P.add("pool", lambda e: e.dma_start(out=d5.ap(), in_=actT), reads=b_actT, writes=[P.buf()], dma=True)
            norm_T()
            for si in range(2):
                w, wb = ws.get(wfm_s); ws.advance()
                for g in range(4):
                    ps, pb = bank()
                    mm_group(ps, pb, lambda k, w=w, g=g: w[:, k, g * 128:(g + 1) * 128], lambda k: hnT[:, k, :], 8, [wb] + b_hnT)
                    pv2 = ps.rearrange("p (t two) -> p t two", two=2)
                    sg = si * 4 + g
                    P.add("act", lambda e, pv2=pv2, sg=sg: e.activation(out=st_kc[0:64, sg, :], in_=pv2[0:64, :, 0], func=ACT.Copy),
                          reads=[pb], writes=[b_stkc])
                    P.add("dve", lambda e, pv2=pv2, sg=sg: e.tensor_copy(out=st_kc[64:128, sg, :], in_=pv2[64:128, :, 1]),
                          reads=[pb], writes=[b_stkc])
            P.add("pool", lambda e, t=t, t0=t0: e.dma_start(out=kc2_s.ap()[:, :, t * 256:(t + 1) * 256].rearrange("s p c -> p s c"), in_=st_kc),
                  reads=[b_stkc], writes=[b_kc2[t]], dma=True)
            w, wb = ws.get(wfm_s); ws.advance()
            for pi in range(4):
                ps, pb = bank()
                mm_group(ps, pb, lambda k, w=w, pi=pi: w[:, k, pi * 128:(pi + 1) * 128], lambda k: hnT[:, k, :], 8, [wb] + b_hnT)
                sl = pi % 2
                if sl == 0:
                    P.add("act", lambda e, ps=ps, sl=sl: e.activation(out=st_k[sl], in_=ps, func=ACT.Copy), reads=[pb], writes=[b_stk[sl]])
                else:
                    P.add("dve", lambda e, ps=ps, sl=sl: e.tensor_copy(out=st_k[sl], in_=ps), reads=[pb], writes=[b_stk[sl]])
                P.add("pool", lambda e, t=t, t0=t0, pi=pi, sl=sl: e.dma_start(out=kT_flat[pi * 128:(pi + 1) * 128, t0:t0 + TT], in_=st_k[sl]),
                      reads=[b_stk[sl]], writes=[b_kT[t]], dma=True)
            w, wb = ws.get(wtm_s); ws.advance()
            for j in range(4):
                ps, pb = bank()
                mm_group(ps, pb, lambda k, j=j: hnT[:, k, j * 128:(j + 1) * 128], lambda k, w=w: w[:, k, :], 8, [wb] + b_hnT)
                src = ps.rearrange("p (s d) -> p s d", d=64)
                if j % 2 == 0:
                    P.add("act", lambda e, j=j, src=src: e.activation(out=st_v[:, j, :, 0:64], in_=src, func=ACT.Copy),
                          reads=[pb], writes=[b_stv])
                else:
                    P.add("dve", lambda e, j=j, src=src: e.tensor_copy(out=st_v[:, j, :, 0:64], in_=src), reads=[pb], writes=[b_stv])
            for sg in range(8):
                P.add("pool", lambda e, t=t, t0=t0, sg=sg: e.dma_start(out=va_s.ap()[:, sg, 4 * t:4 * t + 4, :], in_=st_v[:, :, sg, :]),
                      reads=[b_stv], writes=[b_va[t]], dma=True)
            for half in range(2):
                w, wb = ws.get(wq_s); ws.advance()
                for c4 in range(4):
                    c = half * 4 + c4
                    ps, pb = bank()
                    mm_group(ps, pb, lambda k, w=w, c4=c4: w[:, k, c4 * 128:(c4 + 1) * 128], lambda k: hnT[:, k, :], 8, [wb] + b_hnT)
                    sl = c % 2
                    if sl == 0:
                        P.add("act", lambda e, ps=ps, sl=sl: e.activation(out=st_k[sl], in_=ps, func=ACT.Copy), reads=[pb], writes=[b_stk[sl]])
                    else:
                        P.add("dve", lambda e, ps=ps, sl=sl: e.tensor_copy(out=st_k[sl], in_=ps), reads=[pb], writes=[b_stk[sl]])
                    P.add("pool", lambda e, t=t, t0=t0, c=c, sl=sl: e.dma_start(out=qT_flat[c * 128:(c + 1) * 128, t0:t0 + TT], in_=st_k[sl]),
                          reads=[b_stk[sl]], writes=[b_qT[t]], dma=True)
            w, wb = ws.get(wg_s); ws.advance()
            for j in range(4):
                ps, pb = bank()
                mm_group(ps[:, 0:48], pb, lambda k, j=j: hnT[:, k, j * 128:(j + 1) * 128], lambda k, w=w: w[:, k, 0:48], 8, [wb] + b_hnT)
                P.add("act", lambda e, j=j, ps=ps: e.activation(out=st_g[:, j, :], in_=ps[:, 0:48], func=ACT.Sigmoid),
                      reads=[pb], writes=[b_stg])
            P.add("pool", lambda e, t=t, t0=t0: e.dma_start(out=gate_s.ap()[t0:t0 + TT, :].rearrange("(j p) c -> p j c", p=128), in_=st_g),
                  reads=[b_stg], writes=[b_gate[t]], dma=True)
            for j in range(4):
                P.add("pool", lambda e, t=t, t0=t0, j=j: e.dma_start(out=h1_s.ap()[t0 + j * 128:t0 + (j + 1) * 128, :], in_=xs[:, j, :]),
                      reads=[b_xs[j]], writes=[b_h1[t]], dma=True)

            if final_stub:
                for j in range(4):
                    for hf in range(2):
                        tmp = c_sb[hf]
                        P.add("dve", lambda e, j=j, hf=hf, tmp=tmp: e.scalar_tensor_tensor(
                            out=tmp, in0=xs[:, j, hf * 512:(hf + 1) * 512], scalar=rstd[:, j:j + 1],
                            in1=finalg[:, hf * 512:(hf + 1) * 512], op0=ALU.mult, op1=ALU.mult),
                            reads=[b_xs[j], b_rstd, b_finalg], writes=[b_csb[hf]])
                        out_ops.append(P.add("pool", lambda e, j=j, hf=hf, tmp=tmp, t0=t0: e.dma_start(
                            out=out_t.ap()[t0 + j * 128:t0 + (j + 1) * 128, hf * 512:(hf + 1) * 512], in_=tmp),
                            reads=[b_csb[hf]], writes=[P.buf()], dma=True))

    SLOPES = [2.0 ** (-(h + 1) / 2.0) for h in range(NH)]

    def phase2a(nqt, ntiles_avail):
        P.barrier()
        A.reset(frame0)
        nkeys = ntiles_avail * TT
        nchunks_av = nkeys // 128
        Z = A.alloc([128, S], BF); b_const = P.buf("const2a")
        VMw = A.alloc([128, 2304], BF)
        Cm = A.alloc([128, 128], BF); Wm = A.alloc([128, 128], BF)
        AB = A.alloc([128, NH, 64], F32); CB = A.alloc([128, NH, 64], F32)
        Ttab = A.alloc([128, 255], F32)
        ctmp = A.alloc([128, 1024], F32); b_ctmp = P.buf("ctmp")
        kcT = A.alloc([64, 4, 512], BF); b_kcT = P.buf("kcT")
        cvr = A.alloc([128, 4, 4, 193], BF); b_cvr = P.buf("cvr")
        w1sb = A.alloc([128, 16, 256], BF); b_w1sb = P.buf("w1sb")
        w2sb = A.alloc([128, 2, 64], BF); b_w2sb = P.buf("w2sb")
        pe2f = A.alloc([128, 16], F32); pe2b = A.alloc([128, 16], BF); b_pe2 = P.buf("pe2")
        Xb = A.alloc([128, S // 2], BF); b_X = P.buf("X")
        H1 = A.alloc([128, 2, 512], BF); b_H1 = P.buf("H1")
        ks = A.alloc([64, S], BF); b_ks = P.buf("ks")
        vs = A.alloc([128, 64, 65], BF); b_vs = P.buf("vs")
        kw = [A.alloc([64, 640], BF) for _ in range(2)]; b_kw = [P.buf() for _ in range(2)]
        vw = [A.alloc([128, 5, 65], BF) for _ in range(2)]; b_vw = [P.buf() for _ in range(2)]
        qt = [A.alloc([64, 512], BF) for _ in range(2)]; b_qt = [P.buf() for _ in range(2)]
        gt = [A.alloc([128, 48], F32) for _ in range(2)]; b_gt = [P.buf() for _ in range(2)]
        pc = A.alloc([128, 4, 512], BF); b_pc = [P.buf() for _ in range(4)]
        NPP = 4
        pP = [A.alloc([128, 512], BF) for _ in range(NPP)]; b_pP = [P.buf() for _ in range(NPP)]
        Mb = [A.alloc([128, 512], BF) for _ in range(2)]; b_Mb = [P.buf() for _ in range(2)]
        acc = [A.alloc([128, 128], F32) for _ in range(2)]; b_acc = [P.buf() for _ in range(2)]
        wk = A.alloc([128, 128], F32); b_wk = P.buf("wk")
        m8 = A.alloc([128, 16], F32); b_m8 = P.buf("m8")
        selm = A.alloc([128, 128], BF); b_selm = P.buf("selm")
        rsum = A.alloc([128, 12], F32); b_rsum = P.buf("rsum")
        rinv = A.alloc([128, 12], F32); b_rinv = P.buf("rinv")
        fac = A.alloc([128, 12], F32); b_fac = P.buf("fac")
        t1 = A.alloc([128, 256], F32); t2 = A.alloc([128, 256], F32); t3 = A.alloc([128, 256], F32)
        b_t1 = P.buf("t1"); b_t2 = P.buf("t2"); b_t3 = P.buf("t3")
        ot = [A.alloc([128, 256], BF) for _ in range(2)]; b_ot = [P.buf() for _ in range(2)]
        zt = A.alloc([128, 512], BF); b_zt = P.buf("zt")
        oTs = [A.alloc([65, 512], F32) for _ in range(2)]; b_oTs = [P.buf() for _ in range(2)]
        P.add("pool", lambda e: e.memset(zt, 0.0), writes=[b_zt])

        def zero_acc(bk, ncol):
            P.add("pe", lambda e: e.matmul(bk[0][:, 0:ncol], lhsT=zt[:, 0:128], rhs=zt[:, 0:ncol], start=True, stop=False),
                  reads=[b_zt], writes=[bk[1]])

        def zbuild(e):
            r = None
            return r
        for pz in range(8):
            x0 = pz * 1024

            def zb(e, x0=x0):
                e.memset(ctmp, 1.0)
                return e
            P.add("pool", lambda e: e.memset(ctmp, 1.0), writes=[b_ctmp])
            P.add("pool", lambda e, x0=x0: e.affine_select(out=ctmp, in_=ctmp, pattern=[[1, 1024]], compare_op=ALU.is_ge,
                                                           fill=0.0, base=x0, channel_multiplier=-64),
                  reads=[b_ctmp], writes=[b_ctmp])
            P.add("pool", lambda e, x0=x0: e.affine_select(out=ctmp, in_=ctmp, pattern=[[-1, 1024]], compare_op=ALU.is_ge,
                                                           fill=0.0, base=63 - x0, channel_multiplier=64),
                  reads=[b_ctmp], writes=[b_ctmp])
            P.add("dve", lambda e, x0=x0: e.tensor_copy(out=Z[:, x0:x0 + 1024], in_=ctmp), reads=[b_ctmp], writes=[b_const])
        for pz in range(3):
            x0 = pz * 768
            P.add("pool", lambda e: e.memset(ctmp[:, 0:768], 0.0), writes=[b_ctmp])
            P.add("pool", lambda e, x0=x0: e.affine_select(out=ctmp[:, 0:768], in_=ctmp[:, 0:768], pattern=[[1, 768]],
                                                           compare_op=ALU.is_ge, fill=NEGM, base=x0 - 31, channel_multiplier=-16),
                  reads=[b_ctmp], writes=[b_ctmp])
            P.add("dve", lambda e, x0=x0: e.tensor_copy(out=VMw[:, x0:x0 + 768], in_=ctmp[:, 0:768]), reads=[b_ctmp], writes=[b_const])
        for (M_, pat, base, cm) in ((Cm, [[1, 128]], 0, -1), (Wm, [[-1, 128]], -1, 1)):
            P.add("pool", lambda e: e.memset(ctmp[:, 0:128], 0.0), writes=[b_ctmp])
            P.add("pool", lambda e, pat=pat, base=base, cm=cm: e.affine_select(
                out=ctmp[:, 0:128], in_=ctmp[:, 0:128], pattern=pat, compare_op=ALU.is_ge, fill=NEGM, base=base, channel_multiplier=cm),
                reads=[b_ctmp], writes=[b_ctmp])
            P.add("dve", lambda e, M_=M_: e.tensor_copy(out=M_, in_=ctmp[:, 0:128]), reads=[b_ctmp], writes=[b_const])
        ov = ctmp[:, 0:512].rearrange("p (c j) -> p c j", c=4)
        P.add("pool", lambda e: e.memset(ctmp[:, 0:512], 1.0), writes=[b_ctmp])
        P.add("pool", lambda e: e.affine_select(out=ov, in_=ov, pattern=[[128, 4], [-4, 128]], compare_op=ALU.is_ge,
                                                fill=0.0, base=1, channel_multiplier=1), reads=[b_ctmp], writes=[b_ctmp])
        P.add("pool", lambda e: e.affine_select(out=ov, in_=ov, pattern=[[-128, 4], [4, 128]], compare_op=ALU.is_ge,
                                                fill=0.0, base=3, channel_multiplier=-1), reads=[b_ctmp], writes=[b_ctmp])
        P.add("pool", lambda e: e.memset(cvr, 0.0), writes=[b_cvr])
        for g in range(4):
            P.add("dve", lambda e, g=g: e.tensor_copy(out=cvr[:, g, :, 65:193], in_=ov), reads=[b_ctmp], writes=[b_cvr])
            P.add("dve", lambda e, g=g: e.memset(cvr[:, g, :, 64:65], 1.0), writes=[b_cvr])
        def tt(e):
            e.memset(Ttab[0:64, 0:126], 0.0); e.memset(Ttab[0:64, 126:128], 1e4); e.memset(Ttab[0:64, 128:255], -1e30)
            e.memset(Ttab[64:128, 0:127], 0.0); e.memset(Ttab[64:128, 127:129], 1e4)
            return e.memset(Ttab[64:128, 129:255], -1e30)
        P.add("pool", tt, writes=[b_const])
        P.add("pool", lambda e: e.iota(ctmp[:, 0:64], pattern=[[-128, 64]], base=-64, channel_multiplier=1,
                                       allow_small_or_imprecise_dtypes=True), writes=[b_ctmp])
        P.add("pool", lambda e: e.iota(ctmp[:, 64:128], pattern=[[-128, 64]], base=-48, channel_multiplier=16,
                                       allow_small_or_imprecise_dtypes=True), reads=[b_ctmp], writes=[b_ctmp])
        for h in range(NH):
            P.add("dve", lambda e, h=h: e.tensor_scalar(out=AB[:, h, :], in0=ctmp[:, 0:64], scalar1=SLOPES[h], scalar2=None, op0=ALU.mult),
                  reads=[b_ctmp], writes=[b_const])
            P.add("dve", lambda e, h=h: e.tensor_scalar(out=CB[:, h, :], in0=ctmp[:, 64:128], scalar1=-0.5, scalar2=SLOPES[h],
                                                        op0=ALU.add, op1=ALU.mult), reads=[b_ctmp], writes=[b_const])

        P.add("pool", lambda e: e.memset(kcT, 0.0), writes=[b_kcT])
        P.add("pool", lambda e: e.memset(H1, 0.0), writes=[b_H1])
        if nkeys < S:
            P.add("pool", lambda e: e.memset(Xb, 0.0), writes=[b_X])
        for si in range(2):
            pe_t = pe_k_t if si == 0 else pe_v_t
            P.add("sp", lambda e, si=si: e.dma_start(out=w1sb, in_=w1_s[si].ap()), reads=[wbuf(w1_s[si])], writes=[b_w1sb], dma=True)
            P.add("sp", lambda e, si=si: e.dma_start(out=w2sb, in_=w2_s[si].ap()), reads=[wbuf(w2_s[si])], writes=[b_w2sb], dma=True)
            pe_src = bass.AP(pe_t, 0, [[1, 128], [128, 16]])
            P.add("sp", lambda e, pe_src=pe_src: e.dma_start(out=pe2f, in_=pe_src, allow_slow_non_contiguous=True), writes=[b_pe2], dma=True)
            P.add("dve", lambda e: e.tensor_copy(out=pe2b, in_=pe2f), reads=[b_pe2], writes=[b_pe2])
            for hc in range(2):
                ps, pb = bank()

                def bm(e, ps=ps, hc=hc):
                    for lp in range(16):
                        r = e.matmul(ps[:, 0:1], lhsT=w1sb[:, lp, hc * 128:(hc + 1) * 128], rhs=pe2b[:, lp:lp + 1],
                                     start=(lp == 0), stop=(lp == 15))
                    return r
                P.add("pe", bm, reads=[b_w1sb, b_pe2], writes=[pb])
                P.add("dve", lambda e, ps=ps, hc=hc, si=si: e.tensor_copy(out=cbias[:, si, hc:hc + 1], in_=ps[:, 0:1]),
                      reads=[pb], writes=[b_cbias])
            for g in range(4):
                npair = nkeys // 2
                P.add("sp", lambda e, si=si, g=g, npair=npair: e.dma_start(out=Xb[:, 0:npair], in_=kc2_s.ap()[si * 4 + g, :, 0:npair]),
                      reads=b_kc2[:ntiles_avail], writes=[b_X], dma=True)
                for hc in range(2):
                    ps, pb = bank()

                    def cm_(e, ps=ps, hc=hc):
                        for lp in range(16):
                            r = e.matmul(ps[:, 0:511], lhsT=w1sb[:, lp, hc * 128:(hc + 1) * 128], rhs=Xb[:, lp:lp + 4081:8],
                                         start=(lp == 0), stop=(lp == 15))
                        return r
                    P.add("pe", cm_, reads=[b_w1sb, b_X], writes=[pb])
                    P.add("act", lambda e, ps=ps, hc=hc, si=si: e.activation(out=H1[:, hc, 0:511], in_=ps[:, 0:511], func=ACT.Silu,
                                                                             bias=cbias[:, si, hc:hc + 1], scale=1.0),
                          reads=[pb, b_cbias], writes=[b_H1])
                if si == 0:
                    ps, pb = bank()

                    def k2(e, ps=ps):
                        for hc in range(2):
                            r = e.matmul(ps[0:64, 0:511], lhsT=w2sb[:, hc, :], rhs=H1[:, hc, 0:511], start=(hc == 0), stop=(hc == 1))
                        return r
                    P.add("pe", k2, reads=[b_w2sb, b_H1], writes=[pb])
                    P.add("dve", lambda e, ps=ps, g=g: e.tensor_copy(out=kcT[:, g, 0:511], in_=ps[0:64, 0:511]), reads=[pb], writes=[b_kcT])
                else:
                    ps, pb = bank()

                    def v2(e, ps=ps):
                        for c in range(4):
                            for hc in range(2):
                                r = e.matmul(ps[:, c * 64:(c + 1) * 64], lhsT=H1[:, hc, c * 128:(c + 1) * 128], rhs=w2sb[:, hc, :],
                                             start=(hc == 0), stop=(hc == 1))
                        return r
                    P.add("pe", v2, reads=[b_w2sb, b_H1], writes=[pb])
                    P.add("dve", lambda e, ps=ps, g=g: e.tensor_copy(out=cvr[:, g, :, 0:64], in_=ps[:, 0:256].rearrange("p (c d) -> p c d", c=4)),
                          reads=[pb], writes=[b_cvr])

        bS = [(banks[0], bbufs[0]), (banks[1], bbufs[1])]
        bOA, bOB, bOs, bOw = (banks[2], bbufs[2]), (banks[3], bbufs[3]), (banks[4], bbufs[4]), (banks[5], bbufs[5])
        st2 = {"s": 0, "p": 0, "it": 0}

        def sbank():
            i = st2["s"]; st2["s"] = (i + 1) % 2
            return bS[i]

        def pslot():
            i = st2["p"]; st2["p"] = (i + 1) % NPP
            return pP[i], b_pP[i]

        def mask_mm(e, ps, M_):
            for h in range(4):
                r = e.matmul(ps[:, h * 128:(h + 1) * 128], lhsT=ident, rhs=M_, start=False, stop=True)
            return r

        RNG = []
        for g_ in range(4):
            smin = SLOPES[4 * g_ + 3]
            r_ = 1
            while smin * (128 * r_ - 127) < 100.0:
                r_ += 1
            RNG.append(r_)

        def qtile(g, qi):
            it = st2["it"]; st2["it"] += 1
            sl = it % 2
            tl = qi // 4
            c0 = max(0, qi - 4)
            nwc = qi - c0 + 1
            P.add("sp", lambda e, g=g, qi=qi, sl=sl: e.dma_start(
                out=qt[sl].rearrange("d (h t) -> d h t", h=4), in_=qT_s.ap()[4 * g:4 * g + 4, :, qi * 128:(qi + 1) * 128].rearrange("h d t -> d h t")),
                reads=[b_qT[tl]], writes=[b_qt[sl]], dma=True)
            P.add("sp", lambda e, qi=qi, sl=sl: e.dma_start(out=gt[sl], in_=gate_s.ap()[qi * 128:(qi + 1) * 128, :]),
                  reads=[b_gate[tl]], writes=[b_gt[sl]], dma=True)
            P.add("sp", lambda e, g=g, qi=qi, sl=sl, c0=c0, nwc=nwc: e.dma_start(
                out=kw[sl][:, 0:nwc * 128], in_=kT_s.ap()[4 + g, :, c0 * 128:(qi + 1) * 128]),
                reads=b_kT[c0 // 4:tl + 1], writes=[b_kw[sl]], dma=True)
            P.add("sp", lambda e, g=g, qi=qi, sl=sl, c0=c0, nwc=nwc: e.dma_start(
                out=vw[sl][:, 0:nwc, :], in_=va_s.ap()[:, 4 + g, c0:qi + 1, :]),
                reads=b_va[c0 // 4:tl + 1], writes=[b_vw[sl]], dma=True)
            qv = qt[sl]
            zero_acc(bOA, 386); zero_acc(bOB, 386)
            first_slc = max(0, qi + 1 - RNG[g])
            mb = Mb[sl]; bmb = b_Mb[sl]
            ncc = qi // 16 + 1
            tasks = []
            gv = gt[sl].rearrange("p (h b) -> p h b", b=3)

            def finalize(br, src_bk, dst_bk):
                if br == 0:
                    P.add("act", lambda e: e.activation(out=oTs[br], in_=src_bk[0][0:65, :], func=ACT.Copy), reads=[src_bk[1]], writes=[b_oTs[br]])
                else:
                    P.add("dve", lambda e: e.tensor_copy(out=oTs[br], in_=src_bk[0][0:65, :]), reads=[src_bk[1]], writes=[b_oTs[br]])

                def trs(e):
                    for h in range(4):
                        r = e.transpose(out=dst_bk[0][:, h * 65:(h + 1) * 65], in_=oTs[br][:, h * 128:(h + 1) * 128], identity=identf[0:65, 0:65])
                    return r
                P.add("pe", trs, reads=[b_oTs[br], b_ident], writes=[dst_bk[1]])

            def mk_cmp(c):
                dl = qi - 16 * c
                bk = {}

                def S_():
                    ps, pb = sbank(); bk["ps"] = ps; bk["pb"] = pb

                    def smm(e):
                        r = e.matmul(ps, lhsT=kcT[:, g, c * 128:(c + 1) * 128], rhs=qv, start=True, stop=(dl > 16))
                        if dl <= 16:
                            r = mask_mm(e, ps, VMw[:, 128 * dl:128 * dl + 128])
                        return r
                    P.add("pe", smm, reads=[b_kcT, b_qt[sl], b_const, b_ident], writes=[pb])

                def A_():
                    ps, pb = bk["ps"], bk["pb"]
                    for h in range(4):
                        P.add("act", lambda e, h=h: e.activation(
                            out=pc[:, c, h * 128:(h + 1) * 128], in_=ps[:, h * 128:(h + 1) * 128], func=ACT.Exp,
                            bias=CB[:, 4 * g + h, dl:dl + 1], scale=1.0), reads=[pb, b_const], writes=[b_pc[c]])

                def V_():
                    def omm(e):
                        for h in range(4):
                            ob = bOA[0] if h < 2 else bOB[0]
                            col = (h % 2) * 193
                            r = e.matmul(ob[:, col:col + 193], lhsT=pc[:, c, h * 128:(h + 1) * 128], rhs=cvr[:, g, c, :],
                                         start=False, stop=(c == ncc - 1))
                        return r
                    P.add("pe", omm, reads=[b_pc[c], b_cvr], writes=[bOA[1], bOB[1]])
                    if c == ncc - 1:
                        selection()
                return (S_, A_, V_)

            def selection():
                P.add("dve", lambda e: e.tensor_scalar_max(out=rsum[:, 0:2], in0=bOA[0][:, 64:258:193], scalar1=1e-30),
                      reads=[bOA[1]], writes=[b_rsum])
                P.add("dve", lambda e: e.tensor_scalar_max(out=rsum[:, 2:4], in0=bOB[0][:, 64:258:193], scalar1=1e-30),
                      reads=[bOB[1]], writes=[b_rsum])
                P.add("dve", lambda e: e.reciprocal(out=rinv[:, 0:4], in_=rsum[:, 0:4]), reads=[b_rsum], writes=[b_rinv])
                P.add("dve", lambda e: e.tensor_tensor(out=fac[:, 0:4], in0=gv[:, 4 * g:4 * g + 4, 0], in1=rinv[:, 0:4], op=ALU.mult),
                      reads=[b_gt[sl], b_rinv], writes=[b_fac])
                tsl = Ttab[:, 127 - 2 * qi:255 - 2 * qi]
                for h in range(4):
                    ob, obb = (bOA if h < 2 else bOB)
                    col = (h % 2) * 193 + 65
                    src1 = tsl if h == 0 else acc[(h + 1) % 2]
                    rb = [obb, b_rinv] + ([b_const] if h == 0 else [b_acc[(h + 1) % 2]])
                    P.add("dve", lambda e, ob=ob, col=col, h=h, src1=src1: e.scalar_tensor_tensor(
                        out=acc[h % 2], in0=ob[:, col:col + 128], scalar=rinv[:, h:h + 1], in1=src1, op0=ALU.mult, op1=ALU.add),
                        reads=rb, writes=[b_acc[h % 2]])
                af = acc[1]; baf = b_acc[1]
                P.add("dve", lambda e: e.tensor_scalar_add(out=af[:, 0:1], in0=af[:, 0:1], scalar1=1e4), reads=[baf], writes=[baf])
                P.add("dve", lambda e: e.max(out=m8[:, 0:8], in_=af), reads=[baf], writes=[b_m8])
                P.add("dve", lambda e: e.match_replace(out=wk, in_to_replace=m8[:, 0:8], in_values=af, imm_value=-3e38),
                      reads=[baf, b_m8], writes=[b_wk])
                P.add("dve", lambda e: e.max(out=m8[:, 8:16], in_=wk), reads=[b_wk], writes=[b_m8])
                P.add("dve", lambda e: e.tensor_scalar(out=selm, in0=af, scalar1=m8[:, 15:16], scalar2=-1.0, op0=ALU.is_ge, op1=ALU.add),
                      reads=[baf, b_m8], writes=[b_selm])
                tb, tbb = tbank()
                P.add("pe", lambda e: e.transpose(out=tb[:, 0:128], in_=selm, identity=ident), reads=[b_selm, b_ident], writes=[tbb])
                P.add("act", lambda e: e.activation(out=mb.rearrange("p (h t) -> p h t", h=4),
                                                    in_=tb[:, 0:128].unsqueeze(1).to_broadcast([128, 4, 128]),
                                                    func=ACT.Copy, scale=30000.0), reads=[tbb], writes=[bmb])
                for hb, (ob, obb) in enumerate((bOA, bOB)):
                    P.add("dve", lambda e, hb=hb, ob=ob: e.tensor_tensor(
                        out=t1[:, hb * 128:(hb + 1) * 128].rearrange("p (h d) -> p h d", h=2),
                        in0=ob[:, 0:386].rearrange("p (h x) -> p h x", h=2)[:, :, 0:64],
                        in1=fac[:, 2 * hb:2 * hb + 2].unsqueeze(2).to_broadcast([128, 2, 64]), op=ALU.mult),
                        reads=[obb, b_fac], writes=[b_t1])

            def mk_kv(c, kind):
                bk = {}
                rc = qi - c

                def S_():
                    ps, pb = sbank(); bk["ps"] = ps; bk["pb"] = pb
                    if kind == "win":
                        masked = (c == qi) or (c == qi - 4)

                        def wmm(e):
                            r = e.matmul(ps, lhsT=kw[sl][:, (c - c0) * 128:(c - c0 + 1) * 128], rhs=qv, start=True, stop=not masked)
                            if c == qi:
                                r = mask_mm(e, ps, Cm)
                            elif c == qi - 4:
                                r = mask_mm(e, ps, Wm)
                            return r
                        P.add("pe", wmm, reads=[b_kw[sl], b_qt[sl], b_const, b_ident], writes=[pb])
                    else:
                        def s2(e):
                            e.matmul(ps, lhsT=ks[:, c * 128:(c + 1) * 128], rhs=qv, start=True, stop=False)
                            r = e.matmul(ps, lhsT=Z[:, c * 128:(c + 1) * 128], rhs=mb, start=False, stop=(c != qi))
                            if c == qi:
                                r = mask_mm(e, ps, Cm)
                            return r
                        P.add("pe", s2, reads=[b_ks, b_qt[sl], bmb, b_const, b_ident], writes=[pb])

                def A_():
                    ps, pb = bk["ps"], bk["pb"]
                    pt_, ptb = pslot(); bk["pt"] = pt_; bk["ptb"] = ptb
                    for h in range(4):
                        P.add("act", lambda e, h=h: e.activation(
                            out=pt_[:, h * 128:(h + 1) * 128], in_=ps[:, h * 128:(h + 1) * 128], func=ACT.Exp,
                            bias=AB[:, 4 * g + h, rc:rc + 1], scale=1.0), reads=[pb, b_const], writes=[ptb])

                def V_():
                    pt_, ptb = bk["pt"], bk["ptb"]
                    if kind == "win":
                        P.add("pe", lambda e: e.matmul(bOw[0][0:65, :], lhsT=vw[sl][:, c - c0, :], rhs=pt_, start=(c == c0), stop=(c == qi)),
                              reads=[ptb, b_vw[sl]], writes=[bOw[1]])
                        if c == qi:
                            finalize(0, bOw, bOA)
                    else:
                        P.add("pe", lambda e: e.matmul(bOs[0][0:65, :], lhsT=vs[:, c, :], rhs=pt_, start=(c == first_slc), stop=(c == qi)),
                              reads=[ptb, b_vs], writes=[bOs[1]])
                        if c == qi:
                            finalize(1, bOs, bOB)
                return (S_, A_, V_)

            for c in range(ncc):
                tasks.append(mk_cmp(c))
            for c in range(c0, qi + 1):
                tasks.append(mk_kv(c, "win"))
            for c in range(max(0, qi + 1 - RNG[g]), qi + 1):
                tasks.append(mk_kv(c, "slc"))
            tasks[0][0]()
            for ti in range(len(tasks)):
                if ti + 1 < len(tasks):
                    tasks[ti + 1][0]()
                tasks[ti][1]()
                tasks[ti][2]()
            P.add("dve", lambda e: e.reciprocal(out=rinv[:, 4:8], in_=bOB[0][:, 64:260:65]), reads=[bOB[1]], writes=[b_rinv])
            P.add("dve", lambda e: e.reciprocal(out=rinv[:, 8:12], in_=bOA[0][:, 64:260:65]), reads=[bOA[1]], writes=[b_rinv])
            for b_ in (1, 2):
                P.add("dve", lambda e, b_=b_: e.tensor_tensor(out=fac[:, 4 * b_:4 * b_ + 4], in0=gv[:, 4 * g:4 * g + 4, b_],
                                                              in1=rinv[:, 4 * b_:4 * b_ + 4], op=ALU.mult),
                      reads=[b_gt[sl], b_rinv], writes=[b_fac])
            P.add("dve", lambda e: e.tensor_tensor(out=t2.rearrange("p (h d) -> p h d", h=4),
                                                   in0=bOB[0][:, 0:260].rearrange("p (h x) -> p h x", h=4)[:, :, 0:64],
                                                   in1=fac[:, 4:8].unsqueeze(2).to_broadcast([128, 4, 64]), op=ALU.mult),
                  reads=[bOB[1], b_fac], writes=[b_t2])
            P.add("dve", lambda e: e.tensor_tensor(out=t3.rearrange("p (h d) -> p h d", h=4),
                                                   in0=bOA[0][:, 0:260].rearrange("p (h x) -> p h x", h=4)[:, :, 0:64],
                                                   in1=fac[:, 8:12].unsqueeze(2).to_broadcast([128, 4, 64]), op=ALU.mult),
                  reads=[bOA[1], b_fac], writes=[b_t3])
            P.add("pool", lambda e: e.tensor_tensor(out=t2, in0=t2, in1=t3, op=ALU.add), reads=[b_t2, b_t3], writes=[b_t2])
            o_ = ot[sl]
            P.add("pool", lambda e, o_=o_: e.tensor_tensor(out=o_, in0=t1, in1=t2, op=ALU.add), reads=[b_t1, b_t2], writes=[b_ot[sl]])
            P.add("pool", lambda e, o_=o_, qi=qi, g=g: e.dma_start(out=o_s.ap()[qi * 128:(qi + 1) * 128, g * 256:(g + 1) * 256], in_=o_),
                  reads=[b_ot[sl]], writes=[b_o[tl]], dma=True)

        for g in range(4):
            P.add("sp", lambda e, g=g: e.dma_start(out=ks[:, 0:nkeys], in_=kT_s.ap()[g, :, 0:nkeys]), reads=b_kT[:ntiles_avail], writes=[b_ks], dma=True)
            P.add("sp", lambda e, g=g: e.dma_start(out=vs[:, 0:nchunks_av, :], in_=va_s.ap()[:, g, 0:nchunks_av, :]),
                  reads=b_va[:ntiles_avail], writes=[b_vs], dma=True)
            for qi in range(nqt):
                qtile(g, qi)

    def phase2b(ntiles):
        P.barrier()
        pl = []
        for _ in range(ntiles):
            pl += [(wo_s, 0, 8, i * 512, 512) for i in range(2)]
            pl += ffn_plan(1)
        ws = WStream(pl)
        for t in range(ntiles):
            t0 = t * TT
            for j in range(4):
                P.add("sp", lambda e, j=j, t0=t0: e.dma_start(out=xs[:, j, :], in_=h1_s.ap()[t0 + j * 128:t0 + (j + 1) * 128, :]),
                      reads=[b_h1[t]], writes=[b_xs[j]], dma=True)
                P.add("sp", lambda e, j=j, t0=t0: e.dma_start(out=hn[:, j, :], in_=o_s.ap()[t0 + j * 128:t0 + (j + 1) * 128, :]),
                      reads=[b_o[t]], writes=[b_hn[j]], dma=True)
            transpose_hn()
            down_proj_tm(ws, wo_s, hnT, b_hnT)
            norm_T()
            ffn(1, ws)
            rstd_only()
            for j in range(4):
                for hf in range(2):
                    tmp = c_sb[hf]
                    P.add("dve", lambda e, j=j, hf=hf, tmp=tmp: e.scalar_tensor_tensor(
                        out=tmp, in0=xs[:, j, hf * 512:(hf + 1) * 512], scalar=rstd[:, j:j + 1],
                        in1=finalg[:, hf * 512:(hf + 1) * 512], op0=ALU.mult, op1=ALU.mult),
                        reads=[b_xs[j], b_rstd, b_finalg], writes=[b_csb[hf]])
                    out_ops.append(P.add("pool", lambda e, j=j, hf=hf, tmp=tmp, t0=t0: e.dma_start(
                        out=out_t.ap()[t0 + j * 128:t0 + (j + 1) * 128, hf * 512:(hf + 1) * 512], in_=tmp),
                        reads=[b_csb[hf]], writes=[P.buf()], dma=True))

    return dict(nc=nc, P=P, A=A, phase1=phase1, phase2a=phase2a, phase2b=phase2b, out_ops=out_ops, locals=locals())


_CACHE = {}


def kernel(**inputs):
    if "nc" not in _CACHE:
        ctx = build()
        ctx["phase1"](NT)
        ctx["phase2a"](NQ, NT)
        ctx["phase2b"](NT)
        ctx["P"].emit(final_dma_ops=ctx["out_ops"])
        _CACHE["nc"] = ctx["nc"]
    nc = _CACHE["nc"]
    f = {k: np.ascontiguousarray(np.asarray(v, dtype=np.float32)) for k, v in inputs.items()}
    in_maps = []
    for c in range(8):
        b = c // 2
        d = dict(f)
        d["x"] = np.ascontiguousarray(f["x"][b])
        for k in ("a_norm", "a_w_in", "a_conv", "a_w_out", "b_norm", "b_w_qg", "b_w_o"):
            d[k] = np.ascontiguousarray(f[k][0])
        in_maps.append(d)
    res = run_bass_kernel_spmd(nc, in_maps, core_ids=list(range(8)))
    out = np.stack([np.asarray(res.results[2 * b]["out"]) for b in range(4)], axis=0)
    return out.astype(np.float32)
```

```python
import contextlib
import numpy as np
import concourse.bass as bass
import concourse.mybir as mybir
from concourse.bass_utils import run_bass_kernel_spmd

F32 = mybir.dt.float32
BF = mybir.dt.bfloat16
ACT = mybir.ActivationFunctionType
ALU = mybir.AluOpType

D = 1024
S = 8192
FF = 2816
NH = 16
NG = 4
DH = 64
TT = 512
NT = S // TT
NQ = S // 128
NEGM = -30000.0
EPS = 1e-5

ENGS = ("pe", "act", "dve", "pool", "sp")


class Buf:
    __slots__ = ("name", "last_w", "readers")

    def __init__(self, name):
        self.name = name
        self.last_w = None
        self.readers = []


class Op:
    __slots__ = ("eng", "fn", "deps", "dma", "sig", "sem", "has_dep", "prewait")

    def __init__(self, eng, fn, dma):
        self.eng = eng
        self.fn = fn
        self.dma = dma
        self.deps = []
        self.sig = None
        self.sem = None
        self.has_dep = False
        self.prewait = None


class Prog:
    def __init__(self, nc, n_dma_sems=24):
        self.nc = nc
        self.ops = {e: [] for e in ENGS}
        self.n_dma_sems = n_dma_sems
        self.nbuf = 0
        self.dmas_since_barrier = []

    def buf(self, name=None):
        self.nbuf += 1
        return Buf(name or f"b{self.nbuf}")

    def add(self, eng, fn, reads=(), writes=(), dma=False):
        op = Op(eng, fn, dma)
        deps = {}

        def dep(p, kind):
            if p is None or p is op:
                return
            if p.eng == eng and not p.dma:
                if eng == "pe":
                    return
                if kind in ("war", "waw"):
                    return
            deps[id(p)] = p

        for r in reads:
            dep(r.last_w, "raw")
        for w in writes:
            dep(w.last_w, "waw")
            for rd in w.readers:
                dep(rd, "war")
        for r in reads:
            if dma:
                r.readers.append(op)
            else:
                r.readers = [x for x in r.readers if x.dma or x.eng != eng]
                r.readers.append(op)
        for w in writes:
            w.last_w = op
            w.readers = []
        op.deps = list(deps.values())
        for p in op.deps:
            p.has_dep = True
        self.ops[eng].append(op)
        if dma:
            self.dmas_since_barrier.append(op)
        return op

    def barrier(self):
        lasts = []
        for e in ENGS:
            for op in reversed(self.ops[e]):
                if not op.dma:
                    lasts.append(op)
                    break
        dm = list(self.dmas_since_barrier)
        self.dmas_since_barrier = []
        for e in ENGS:
            op = Op(e, lambda eng: eng.nop(), False)
            op.deps = [p for p in lasts if p.eng != e] + dm
            for p in op.deps:
                p.has_dep = True
            self.ops[e].append(op)

    def emit(self, final_dma_ops=()):
        nc = self.nc
        if final_dma_ops:
            fin = Op("sp", lambda e: e.nop(), False)
            fin.deps = list(final_dma_ops)
            for p in fin.deps:
                p.has_dep = True
            self.ops["sp"].append(fin)
        with contextlib.ExitStack() as st:
            esem = {e: st.enter_context(nc.semaphore(f"s_{e}")) for e in ENGS}
            dsem = {e: [st.enter_context(nc.semaphore(f"d_{e}{i}")) for i in range(self.n_dma_sems)]
                    for e in ("sp", "pool", "act")}
            for e in ENGS:
                cnt = 0
                dcnt = 0
                for op in self.ops[e]:
                    if op.dma:
                        R = self.n_dma_sems
                        op.sem = dsem[e][dcnt % R]
                        op.sig = 16 * (dcnt // R + 1)
                        if op.sig > 16:
                            op.prewait = (op.sem, op.sig - 16)
                        dcnt += 1
                    elif op.has_dep:
                        cnt += 1
                        op.sem = esem[e]
                        op.sig = cnt
            block = st.enter_context(nc.Block())

            def run(e, eng_obj):
                waited = {}
                for op in self.ops[e]:
                    ws = {}
                    if op.prewait:
                        ws[id(op.prewait[0])] = op.prewait
                    for p in op.deps:
                        k = id(p.sem)
                        if k not in ws or ws[k][1] < p.sig:
                            ws[k] = (p.sem, p.sig)
                    for k, (s, v) in ws.items():
                        if waited.get(k, 0) >= v:
                            continue
                        waited[k] = v
                        eng_obj.wait_ge(s, v)
                    inst = op.fn(eng_obj)
                    if op.dma:
                        inst.then_inc(op.sem, 16)
                    elif op.has_dep:
                        inst.then_inc(op.sem, 1)

            @block.sync
            def _(sync):
                run("sp", sync)

            @block.tensor
            def _(t):
                run("pe", t)

            @block.vector
            def _(v):
                run("dve", v)

            @block.scalar
            def _(a):
                run("act", a)

            @block.gpsimd
            def _(g):
                run("pool", g)


class Arena:
    def __init__(self, nc, base, top):
        self.nc = nc
        self.base = (base + 31) // 32 * 32
        self.top = top
        self.cur = self.base
        self.n = 0

    def mark(self):
        return self.cur

    def reset(self, m):
        self.cur = m

    def alloc(self, shape, dtype):
        nb = 4 if dtype == F32 else 2
        sz = nb
        for s in shape[1:]:
            sz *= s
        sz = (sz + 31) // 32 * 32
        off = self.cur
        assert off + sz <= self.top, ("SBUF overflow", off + sz - self.base, self.top - self.base)
        self.cur += sz
        self.n += 1
        return self.nc.alloc_sbuf_tensor_at(f"t{self.n}", list(shape), dtype, offset=off).ap()


def dram_ap(t, offset, pat):
    return bass.AP(t, offset, [list(p) for p in pat])


def build(debug=False):
    nc = bass.Bass("TRN2", target_bir_lowering=False)
    P = Prog(nc)
    A = Arena(nc, nc.sbuf_base, nc.sbuf_top)

    def din(name, shape):
        return nc.dram_tensor(name, list(shape), F32, kind="ExternalInput")

    x_t = din("x", [S, D])
    a_norm_t = din("a_norm", [D]); a_w_in_t = din("a_w_in", [D, 3 * D]); a_conv_t = din("a_conv", [3, D])
    a_w_out_t = din("a_w_out", [D, D]); kv_norm_t = din("kv_norm", [D]); w_kv_t = din("w_kv", [D, 1536])
    pe_k_t = din("cmp_pe_k", [32, 64]); w1_k_t = din("cmp_w1_k", [2048, 256]); w2_k_t = din("cmp_w2_k", [256, 64])
    pe_v_t = din("cmp_pe_v", [32, 64]); w1_v_t = din("cmp_w1_v", [2048, 256]); w2_v_t = din("cmp_w2_v", [256, 64])
    b_norm_t = din("b_norm", [D]); b_w_qg_t = din("b_w_qg", [D, 1072]); b_w_o_t = din("b_w_o", [D, D])
    f_norm_t = din("f_norm", [2, D]); f_w_gu_t = din("f_w_gu", [2, D, 2 * FF]); f_w_down_t = din("f_w_down", [2, FF, D])
    final_norm_t = din("final_norm", [D])
    out_t = nc.dram_tensor("out", [S, D], F32, kind="ExternalOutput")

    def scr(name, shape, dt=BF):
        return nc.dram_tensor(name, list(shape), dt)

    win_s = scr("win_s", [128, 8, 3072]); wout_s = scr("wout_s", [128, 8, 1024])
    wgu_s = [scr(f"wgu_s{l}", [128, 8, 2 * FF]) for l in range(2)]
    wdn_s = [scr(f"wdn_s{l}", [128, 22, 1024]) for l in range(2)]
    wfm_s = scr("wfm_s", [128, 8, 1536]); wtm_s = scr("wtm_s", [128, 8, 512])
    wq_s = scr("wq_s", [128, 8, 1024]); wg_s = scr("wg_s", [128, 8, 48]); wo_s = scr("wo_s", [128, 8, 1024])
    w1_s = [scr(f"w1_s{i}", [128, 16, 256]) for i in range(2)]
    w2_s = [scr(f"w2_s{i}", [128, 2, 64]) for i in range(2)]
    kT_s = scr("kT_s", [8, 64, S])
    kc2_s = scr("kc2_s", [8, 128, S // 2])
    va_s = scr("va_s", [128, 8, 64, 65])
    qT_s = scr("qT_s", [NH, 64, S])
    gate_s = scr("gate_s", [S, 48], F32)
    h1_s = scr("h1_s", [S, D], F32)
    o_s = scr("o_s", [S, D])

    NBANK = 6
    banks = [nc.alloc_psum_tensor(f"pb{i}", [128, 512], F32).ap() for i in range(NBANK)]
    bbufs = [P.buf(f"pb{i}") for i in range(NBANK)]
    tbanks = [nc.alloc_psum_tensor(f"tb{i}", [128, 1024], BF).ap() for i in range(2)]
    tbufs = [P.buf(f"tb{i}") for i in range(2)]
    bstate = {"b": 0, "t": 0}

    def bank():
        i = bstate["b"]
        bstate["b"] = (i + 1) % NBANK
        return banks[i], bbufs[i]

    def tbank():
        i = bstate["t"]
        bstate["t"] = (i + 1) % 2
        return tbanks[i], tbufs[i]

    ident = A.alloc([128, 128], BF); b_ident = P.buf("ident")
    identf = A.alloc([128, 128], F32)
    gP = A.alloc([128, 6, 8], F32); b_gP = P.buf("gP")
    convP = A.alloc([128, 8, 3], F32); b_convP = P.buf("convP")
    finalg = A.alloc([128, D], F32); b_finalg = P.buf("finalg")
    ss = A.alloc([128, 4], F32); b_ss = P.buf("ss")
    rstd = A.alloc([128, 4], F32); b_rstd = P.buf("rstd")
    cvh = A.alloc([128, 8, 2], F32); b_cvh = [P.buf(f"cvh{f}") for f in range(8)]
    cbias = A.alloc([128, 2, 2], F32); b_cbias = P.buf("cbias")
    frame0 = A.mark()

    xs = A.alloc([128, 4, D], F32); b_xs = [P.buf(f"xs{j}") for j in range(4)]
    junk = A.alloc([128, D], BF); b_junk = P.buf("junk")
    hn = A.alloc([128, 4, D], BF); b_hn = [P.buf(f"hn{j}") for j in range(4)]
    hnT = A.alloc([128, 8, TT], BF); b_hnT = [P.buf(f"hnT{j}") for j in range(4)]
    actT = A.alloc([128, 22, TT], BF); b_actT = [P.buf(f"actT{k}") for k in range(22)]
    silu_t = [A.alloc([128, TT], F32) for _ in range(2)]; b_silu = [P.buf() for _ in range(2)]
    NSLOT = 5
    ring = [A.alloc([128, 8, 512], BF) for _ in range(NSLOT)]; b_ring = [P.buf(f"ring{i}") for i in range(NSLOT)]
    c_sb = [A.alloc([128, TT], F32) for _ in range(2)]; b_csb = [P.buf() for _ in range(2)]
    cv = [A.alloc([128, TT + 2], F32) for _ in range(2)]; b_cv = [P.buf() for _ in range(2)]
    u_t = [A.alloc([128, TT], F32) for _ in range(2)]; b_u = [P.buf() for _ in range(2)]
    buT = A.alloc([128, 8, TT], BF); b_buT = [P.buf(f"buT{k}") for k in range(8)]
    st_k = [A.alloc([128, TT], BF) for _ in range(2)]; b_stk = [P.buf() for _ in range(2)]
    st_kc = A.alloc([128, 8, 256], BF); b_stkc = P.buf("stkc")
    st_v = A.alloc([128, 4, 8, 65], BF); b_stv = P.buf("stv")
    st_g = A.alloc([128, 4, 48], F32); b_stg = P.buf("stg")
    dense_end = A.mark()

    def mk_ident(e):
        e.memset(identf, 0.0)
        return e.affine_select(out=identf, in_=identf, pattern=[[-1, 128]], compare_op=ALU.not_equal,
                               fill=1.0, base=0, channel_multiplier=1)
    P.add("pool", mk_ident, writes=[b_ident])
    P.add("dve", lambda e: e.tensor_copy(out=ident, in_=identf), reads=[b_ident], writes=[b_ident])
    gsrc = [a_norm_t.ap(), f_norm_t.ap()[0, :], kv_norm_t.ap(), b_norm_t.ap(), f_norm_t.ap()[1, :]]
    for i, g in enumerate(gsrc):
        P.add("sp", lambda e, i=i, g=g: e.dma_start(out=gP[:, i, :], in_=g.rearrange("(c p) -> p c", p=128),
                                                    allow_slow_non_contiguous=True), writes=[b_gP], dma=True)
    P.add("dve", lambda e: e.tensor_scalar_mul(out=gP[:, 5, :], in0=gP[:, 3, :], scalar1=0.125),
          reads=[b_gP], writes=[b_gP])
    for c in range(8):
        P.add("sp", lambda e, c=c: e.dma_start(
            out=convP[:, c, :], in_=a_conv_t.ap()[:, c * 128:(c + 1) * 128].rearrange("k p -> p k"),
            allow_slow_non_contiguous=True), writes=[b_convP], dma=True)
    P.add("sp", lambda e: e.dma_start(out=finalg, in_=final_norm_t.ap().unsqueeze(0).to_broadcast([128, D])),
          writes=[b_finalg], dma=True)
    P.add("pool", lambda e: e.memset(cvh, 0.0), writes=b_cvh)
    P.add("pool", lambda e: e.memset(st_v, 1.0), writes=[b_stv])

    stg_f = [xs[:, 0:2, :].rearrange("p a (b c) -> p (a b) c", c=512), xs[:, 2:4, :].rearrange("p a (b c) -> p (a b) c", c=512)]
    stg_b = [actT[:, 0:4, :], actT[:, 4:8, :]]
    b_sf = [P.buf("sf0"), P.buf("sf1")]
    b_sb = [P.buf("sb0"), P.buf("sb1")]
    prep_state = {"i": 0}
    wbufs = {}

    def wbuf(t):
        k = t.name
        if k not in wbufs:
            wbufs[k] = P.buf(k)
        return wbufs[k]

    def prep(src, k0, nk, c0, ncol, dsts, gain=None):
        i = prep_state["i"]
        prep_state["i"] += 1
        sl = i % 2
        sf = stg_f[sl][:, 0:nk, 0:ncol]
        sb = stg_b[sl][:, 0:nk, 0:ncol]
        srcv = src[k0 * 128:(k0 + nk) * 128, c0:c0 + ncol].rearrange("(k p) n -> p k n", p=128)
        P.add("sp", lambda e: e.dma_start(out=sf, in_=srcv), writes=[b_sf[sl]], dma=True)
        eng = ("dve", "act")[i % 2]
        if gain is None:
            if eng == "act":
                P.add("act", lambda e: e.activation(out=sb, in_=sf, func=ACT.Copy), reads=[b_sf[sl]], writes=[b_sb[sl]])
            else:
                P.add(eng, lambda e: e.tensor_copy(out=sb, in_=sf), reads=[b_sf[sl]], writes=[b_sb[sl]])
        else:
            def cast(e):
                for k in range(nk):
                    gcol = gP[:, gain, k0 + k:k0 + k + 1]
                    if eng == "act":
                        r = e.activation(out=sb[:, k, :], in_=sf[:, k, :], func=ACT.Copy, scale=gcol)
                    else:
                        r = e.tensor_scalar(out=sb[:, k, :], in0=sf[:, k, :], scalar1=gcol, scalar2=None, op0=ALU.mult)
                return r
            P.add(eng, cast, reads=[b_sf[sl], b_gP], writes=[b_sb[sl]])
        for (dt_, dk0, dc0) in dsts:
            dv = dt_.ap()[:, dk0:dk0 + nk, dc0:dc0 + ncol]
            P.add("pool", lambda e, dv=dv: e.dma_start(out=dv, in_=sb), reads=[b_sb[sl]], writes=[wbuf(dt_)], dma=True)

    def prep_mat(src, K, c0, ncols, dst, dc0, gain=None):
        nkc = K // 128
        for cc in range(0, ncols, 512):
            w = min(512, ncols - cc)
            for k0 in range(0, nkc, 4):
                nk = min(4, nkc - k0)
                prep(src, k0, nk, c0 + cc, w, [(dst, k0, dc0 + cc)], gain)

    for f in range(8):
        for k0 in (0, 4):
            prep(a_w_in_t.ap(), k0, 4, 1024 + f * 128, 128, [(win_s, k0, f * 384)], gain=0)
            prep(a_w_in_t.ap(), k0, 4, 2048 + f * 128, 128, [(win_s, k0, f * 384 + 128)], gain=0)
            prep(a_w_in_t.ap(), k0, 4, f * 128, 128, [(win_s, k0, f * 384 + 256)], gain=0)
    prep_mat(a_w_out_t.ap(), D, 0, 1024, wout_s, 0)
    for l in range(2):
        prep_mat(f_w_gu_t.ap()[l], D, 0, 2 * FF, wgu_s[l], 0, gain=(1 if l == 0 else 4))
        prep_mat(f_w_down_t.ap()[l], FF, 0, 1024, wdn_s[l], 0)
    wkv = w_kv_t.ap()
    for st_i, src_set in enumerate((0, 1)):
        for g in range(4):
            for k0 in (0, 4):
                prep(wkv, k0, 4, src_set * 256 + g * 64, 64,
                     [(wfm_s, k0, (st_i * 4 + g) * 128), (wfm_s, k0, (st_i * 4 + g) * 128 + 64)], gain=2)
    prep_mat(wkv, D, 2 * 256, 256, wfm_s, 1024, gain=2)
    prep_mat(wkv, D, 4 * 256, 256, wfm_s, 1280, gain=2)
    prep_mat(wkv, D, 3 * 256, 256, wtm_s, 0, gain=2)
    prep_mat(wkv, D, 5 * 256, 256, wtm_s, 256, gain=2)
    prep_mat(b_w_qg_t.ap(), D, 0, 1024, wq_s, 0, gain=5)
    prep_mat(b_w_qg_t.ap(), D, 1024, 48, wg_s, 0, gain=3)
    prep_mat(b_w_o_t.ap(), D, 0, 1024, wo_s, 0)
    for i, (w1, w2) in enumerate(((w1_k_t, w2_k_t), (w1_v_t, w2_v_t))):
        prep_mat(w1.ap(), 2048, 0, 256, w1_s[i], 0)
        prep_mat(w2.ap(), 256, 0, 64, w2_s[i], 0)

    ring_state = {"i": 0}

    def wload(dt_, k0, nk, c0, ncol):
        i = ring_state["i"]
        ring_state["i"] += 1
        sl = i % NSLOT
        dst = ring[sl][:, 0:nk, 0:ncol]
        srcv = dt_.ap()[:, k0:k0 + nk, c0:c0 + ncol]
        P.add("sp", lambda e: e.dma_start(out=dst, in_=srcv), reads=[wbuf(dt_)], writes=[b_ring[sl]], dma=True)
        return ring[sl], b_ring[sl]

    class WStream:
        def __init__(self, plan, depth=NSLOT - 1):
            self.plan = plan
            self.depth = depth
            self.loaded = []
            self.pos = 0
            for _ in range(min(depth, len(plan))):
                self._issue()

        def _issue(self):
            p = self.plan[len(self.loaded)]
            self.loaded.append(wload(*p))

        def get(self, expect=None):
            r = self.loaded[self.pos]
            if expect is not None:
                assert self.plan[self.pos][0] is expect, (self.plan[self.pos][0].name, expect.name)
            self.pos += 1
            return r

        def advance(self):
            if len(self.loaded) < len(self.plan):
                self._issue()

    def ffn_plan(l):
        pl = []
        for i in range(6):
            w = 512 if i < 5 else 256
            pl.append((wgu_s[l], 0, 8, i * 512, w))
            pl.append((wgu_s[l], 0, 8, FF + i * 512, w))
        for nh in range(2):
            for (k0, nk) in ((0, 8), (8, 8), (16, 6)):
                pl.append((wdn_s[l], k0, nk, nh * 512, 512))
        return pl

    def tile_plan1():
        pl = [(win_s, 0, 8, f * 384, 384) for f in range(8)]
        pl += [(wout_s, 0, 8, i * 512, 512) for i in range(2)]
        pl += ffn_plan(0)
        pl += [(wfm_s, 0, 8, i * 512, 512) for i in range(3)]
        pl += [(wtm_s, 0, 8, 0, 512)]
        pl += [(wq_s, 0, 8, i * 512, 512) for i in range(2)]
        pl += [(wg_s, 0, 8, 0, 48)]
        return pl

    def mm_group(ps, pb, lhs_fn, rhs_fn, nk, reads, n=None):
        def f(e):
            for k in range(nk):
                r = e.matmul(ps, lhsT=lhs_fn(k), rhs=rhs_fn(k), start=(k == 0), stop=(k == nk - 1))
            return r
        P.add("pe", f, reads=reads, writes=[pb])

    evict_rr = {"i": 0}

    def rstd_only():
        P.add("pool", lambda e: e.memset(ss, 0.0), writes=[b_ss])
        for j in range(4):
            P.add("act", lambda e, j=j: e.activation(out=junk, in_=xs[:, j, :], func=ACT.Square,
                                                      accum_out=ss[:, j:j + 1]),
                  reads=[b_xs[j]], writes=[b_ss, b_junk])
        P.add("dve", lambda e: e.tensor_scalar(out=rstd, in0=ss, scalar1=1.0 / D, scalar2=EPS, op0=ALU.mult, op1=ALU.add),
              reads=[b_ss], writes=[b_rstd])
        P.add("act", lambda e: e.activation(out=rstd, in_=rstd, func=ACT.Sqrt), reads=[b_rstd], writes=[b_rstd])
        P.add("dve", lambda e: e.reciprocal(out=rstd, in_=rstd), reads=[b_rstd], writes=[b_rstd])

    def transpose_hn():
        for j in range(4):
            tb, tbb = tbank()

            def tr(e, j=j, tb=tb):
                for k in range(8):
                    r = e.transpose(out=tb[:, k * 128:(k + 1) * 128], in_=hn[:, j, k * 128:(k + 1) * 128], identity=ident)
                return r
            P.add("pe", tr, reads=[b_hn[j], b_ident], writes=[tbb])
            eng = "act" if j % 2 == 0 else "dve"
            src = tb.rearrange("p (k t) -> p k t", k=8)
            dst = hnT[:, :, j * 128:(j + 1) * 128]
            if eng == "act":
                P.add("act", lambda e, src=src, dst=dst: e.activation(out=dst, in_=src, func=ACT.Copy),
                      reads=[tbb], writes=[b_hnT[j]])
            else:
                P.add("dve", lambda e, src=src, dst=dst: e.tensor_copy(out=dst, in_=src), reads=[tbb], writes=[b_hnT[j]])

    def norm_T():
        rstd_only()
        for j in range(4):
            if j % 2 == 0:
                P.add("act", lambda e, j=j: e.activation(out=hn[:, j, :], in_=xs[:, j, :], func=ACT.Copy, scale=rstd[:, j:j + 1]),
                      reads=[b_xs[j], b_rstd], writes=[b_hn[j]])
            else:
                P.add("dve", lambda e, j=j: e.tensor_scalar(out=hn[:, j, :], in0=xs[:, j, :], scalar1=rstd[:, j:j + 1],
                                                            scalar2=None, op0=ALU.mult),
                      reads=[b_xs[j], b_rstd], writes=[b_hn[j]])
        transpose_hn()

    def ffn(l, ws):
        for i in range(6):
            nch = 4 if i < 5 else 2
            gw, gb = ws.get(wgu_s[l])
            uw, ub = ws.get(wgu_s[l])
            for c in range(nch):
                fc = i * 4 + c
                pg, pgb = bank()
                mm_group(pg, pgb, lambda k, c=c, gw=gw: gw[:, k, c * 128:(c + 1) * 128], lambda k: hnT[:, k, :], 8,
                         [gb] + b_hnT)
                pu, pub = bank()
                mm_group(pu, pub, lambda k, c=c, uw=uw: uw[:, k, c * 128:(c + 1) * 128], lambda k: hnT[:, k, :], 8,
                         [ub] + b_hnT)
                sl = fc % 2
                P.add("act", lambda e, pg=pg, sl=sl: e.activation(out=silu_t[sl], in_=pg, func=ACT.Silu),
                      reads=[pgb], writes=[b_silu[sl]])
                P.add("dve", lambda e, pu=pu, sl=sl, fc=fc: e.tensor_tensor(out=actT[:, fc, :], in0=pu, in1=silu_t[sl], op=ALU.mult),
                      reads=[pub, b_silu[sl]], writes=[b_actT[fc]])
            ws.advance()
            ws.advance()
        for nh in range(2):
            pss = [bank() for _ in range(4)]
            for (k0, nk) in ((0, 8), (8, 8), (16, 6)):
                dw, db = ws.get(wdn_s[l]); ws.advance()
                for j in range(4):
                    ps, pb = pss[j]

                    def f(e, j=j, ps=ps, dw=dw, k0=k0, nk=nk):
                        for k in range(nk):
                            r = e.matmul(ps, lhsT=actT[:, k0 + k, j * 128:(j + 1) * 128], rhs=dw[:, k, :],
                                         start=(k0 + k == 0), stop=(k0 + k == 21))
                        return r
                    P.add("pe", f, reads=[db] + b_actT[k0:k0 + nk], writes=[pb])
            for j in range(4):
                ps, pb = pss[j]
                P.add("dve", lambda e, j=j, ps=ps, nh=nh: e.tensor_tensor(
                    out=xs[:, j, nh * 512:(nh + 1) * 512], in0=ps, in1=xs[:, j, nh * 512:(nh + 1) * 512], op=ALU.add),
                    reads=[pb, b_xs[j]], writes=[b_xs[j]])

    def down_proj_tm(ws, wt, srcT, b_src):
        for nh in range(2):
            w, wb = ws.get(wt); ws.advance()
            for j in range(4):
                ps, pb = bank()
                mm_group(ps, pb, lambda k, j=j: srcT[:, k, j * 128:(j + 1) * 128], lambda k, w=w: w[:, k, :], 8,
                         [wb] + b_src)
                P.add("dve", lambda e, j=j, ps=ps, nh=nh: e.tensor_tensor(
                    out=xs[:, j, nh * 512:(nh + 1) * 512], in0=ps, in1=xs[:, j, nh * 512:(nh + 1) * 512], op=ALU.add),
                    reads=[pb, b_xs[j]], writes=[b_xs[j]])

    b_kT = [P.buf(f"kT{t}") for t in range(NT)]
    b_kc2 = [P.buf(f"kc2{t}") for t in range(NT)]
    b_va = [P.buf(f"va{t}") for t in range(NT)]
    b_qT = [P.buf(f"qT{t}") for t in range(NT)]
    b_gate = [P.buf(f"gate{t}") for t in range(NT)]
    b_h1 = [P.buf(f"h1{t}") for t in range(NT)]
    b_o = [P.buf(f"o{t}") for t in range(NT)]
    out_ops = []

    x_ap = x_t.ap()

    P.barrier()

    kT_flat = kT_s.ap().rearrange("s d t -> (s d) t")
    qT_flat = qT_s.ap().rearrange("h d t -> (h d) t")

    def phase1(ntiles, final_stub=False):
        ws = WStream([p for _ in range(ntiles) for p in tile_plan1()])
        for t in range(ntiles):
            t0 = t * TT
            for j in range(4):
                P.add("sp", lambda e, t=t, t0=t0, j=j: e.dma_start(out=xs[:, j, :], in_=x_ap[t0 + j * 128:t0 + (j + 1) * 128, :]),
                      writes=[b_xs[j]], dma=True)
            norm_T()
            if debug and t == 0:
                d1 = nc.dram_tensor("dbg_hnT", [128, 8, TT], BF, kind="ExternalOutput")
                P.add("pool", lambda e: e.dma_start(out=d1.ap(), in_=hnT), reads=b_hnT, writes=[P.buf()], dma=True)
                d0 = nc.dram_tensor("dbg_rstd", [128, 4], F32, kind="ExternalOutput")
                P.add("pool", lambda e: e.dma_start(out=d0.ap(), in_=rstd), reads=[b_rstd], writes=[P.buf()], dma=True)
            for f in range(8):
                w, wb = ws.get(win_s); ws.advance()
                pc, pcb = bank()
                mm_group(pc, pcb, lambda k, w=w: w[:, k, 0:128], lambda k: hnT[:, k, :], 8, [wb] + b_hnT)
                pv, pvb = bank()
                mm_group(pv, pvb, lambda k, w=w: w[:, k, 128:256], lambda k: hnT[:, k, :], 8, [wb] + b_hnT)
                pq, pqb = bank()
                mm_group(pq, pqb, lambda k, w=w: w[:, k, 256:384], lambda k: hnT[:, k, :], 8, [wb] + b_hnT)
                sl = f % 2
                P.add("act", lambda e, pc=pc, sl=sl: e.activation(out=c_sb[sl], in_=pc, func=ACT.Copy),
                      reads=[pcb], writes=[b_csb[sl]])
                P.add("dve", lambda e, sl=sl, f=f: e.tensor_copy(out=cv[sl][:, 0:2], in_=cvh[:, f, :]),
                      reads=[b_cvh[f]], writes=[b_cv[sl]])
                P.add("dve", lambda e, pv=pv, sl=sl: e.tensor_tensor(out=cv[sl][:, 2:TT + 2], in0=pv, in1=c_sb[sl], op=ALU.mult),
                      reads=[pvb, b_csb[sl], b_cv[sl]], writes=[b_cv[sl]])
                P.add("dve", lambda e, sl=sl, f=f: e.tensor_copy(out=cvh[:, f, :], in_=cv[sl][:, TT:TT + 2]),
                      reads=[b_cv[sl]], writes=[b_cvh[f]])
                P.add("act", lambda e, sl=sl, f=f: e.activation(out=u_t[sl], in_=cv[sl][:, 2:TT + 2], func=ACT.Copy, scale=convP[:, f, 2:3]),
                      reads=[b_cv[sl], b_convP], writes=[b_u[sl]])
                P.add("dve", lambda e, sl=sl, f=f: e.scalar_tensor_tensor(out=u_t[sl], in0=cv[sl][:, 1:TT + 1], scalar=convP[:, f, 1:2],
                                                                           in1=u_t[sl], op0=ALU.mult, op1=ALU.add),
                      reads=[b_cv[sl], b_convP, b_u[sl]], writes=[b_u[sl]])
                P.add("dve", lambda e, sl=sl, f=f: e.scalar_tensor_tensor(out=u_t[sl], in0=cv[sl][:, 0:TT], scalar=convP[:, f, 0:1],
                                                                           in1=u_t[sl], op0=ALU.mult, op1=ALU.add),
                      reads=[b_cv[sl], b_convP, b_u[sl]], writes=[b_u[sl]])
                P.add("dve", lambda e, pq=pq, sl=sl, f=f: e.tensor_tensor(out=buT[:, f, :], in0=pq, in1=u_t[sl], op=ALU.mult),
                      reads=[pqb, b_u[sl]], writes=[b_buT[f]])
            if debug and t == 0:
                d6 = nc.dram_tensor("dbg_cvh", [128, 8, 2], F32, kind="ExternalOutput")
                P.add("pool", lambda e: e.dma_start(out=d6.ap(), in_=cvh), reads=b_cvh, writes=[P.buf()], dma=True)
            if debug and t == 1:
                d7 = nc.dram_tensor("dbg_buT1", [128, 8, TT], BF, kind="ExternalOutput")
                P.add("pool", lambda e: e.dma_start(out=d7.ap(), in_=buT), reads=b_buT, writes=[P.buf()], dma=True)
            if debug and t == 0:
                d2 = nc.dram_tensor("dbg_buT", [128, 8, TT], BF, kind="ExternalOutput")
                P.add("pool", lambda e: e.dma_start(out=d2.ap(), in_=buT), reads=b_buT, writes=[P.buf()], dma=True)
            down_proj_tm(ws, wout_s, buT, b_buT)
            if debug and t == 0:
                d3 = nc.dram_tensor("dbg_ha", [128, 4, D], F32, kind="ExternalOutput")
                P.add("pool", lambda e: e.dma_start(out=d3.ap(), in_=xs), reads=b_xs, writes=[P.buf()], dma=True)
            norm_T()
            if debug and t == 0:
                d4 = nc.dram_tensor("dbg_hnT2", [128, 8, TT], BF, kind="ExternalOutput")
                P.add("pool", lambda e: e.dma_start(out=d4.ap(), in_=hnT), reads=b_hnT, writes=[P.buf()], dma=True)
            ffn(0, ws)
            if debug and t == 0:
                d5 = nc.dram_tensor("dbg_actT", [128, 22, TT], BF, kind="ExternalOutput")
                P.add("pool", lambda e: e.dma_start(out=d5.ap(), in_=actT), reads=b_actT, writes=[P.buf()], dma=True)
            norm_T()
            for si in range(2):
                w, wb = ws.get(wfm_s); ws.advance()
                for g in range(4):
                    ps, pb = bank()
                    mm_group(ps, pb, lambda k, w=w, g=g: w[:, k, g * 128:(g + 1) * 128], lambda k: hnT[:, k, :], 8, [wb] + b_hnT)
                    pv2 = ps.rearrange("p (t two) -> p t two", two=2)
                    sg = si * 4 + g
                    P.add("act", lambda e, pv2=pv2, sg=sg: e.activation(out=st_kc[0:64, sg, :], in_=pv2[0:64, :, 0], func=ACT.Copy),
                          reads=[pb], writes=[b_stkc])
                    P.add("dve", lambda e, pv2=pv2, sg=sg: e.tensor_copy(out=st_kc[64:128, sg, :], in_=pv2[64:128, :, 1]),
                          reads=[pb], writes=[b_stkc])
            P.add("pool", lambda e, t=t, t0=t0: e.dma_start(out=kc2_s.ap()[:, :, t * 256:(t + 1) * 256].rearrange("s p c -> p s c"), in_=st_kc),
                  reads=[b_stkc], writes=[b_kc2[t]], dma=True)
            w, wb = ws.get(wfm_s); ws.advance()
            for pi in range(4):
                ps, pb = bank()
                mm_group(ps, pb, lambda k, w=w, pi=pi: w[:, k, pi * 128:(pi + 1) * 128], lambda k: hnT[:, k, :], 8, [wb] + b_hnT)
                sl = pi % 2
                if sl == 0:
                    P.add("act", lambda e, ps=ps, sl=sl: e.activation(out=st_k[sl], in_=ps, func=ACT.Copy), reads=[pb], writes=[b_stk[sl]])
                else:
                    P.add("dve", lambda e, ps=ps, sl=sl: e.tensor_copy(out=st_k[sl], in_=ps), reads=[pb], writes=[b_stk[sl]])
                P.add("pool", lambda e, t=t, t0=t0, pi=pi, sl=sl: e.dma_start(out=kT_flat[pi * 128:(pi + 1) * 128, t0:t0 + TT], in_=st_k[sl]),
                      reads=[b_stk[sl]], writes=[b_kT[t]], dma=True)
            w, wb = ws.get(wtm_s); ws.advance()
            for j in range(4):
                ps, pb = bank()
                mm_group(ps, pb, lambda k, j=j: hnT[:, k, j * 128:(j + 1) * 128], lambda k, w=w: w[:, k, :], 8, [wb] + b_hnT)
                src = ps.rearrange("p (s d) -> p s d", d=64)
                if j % 2 == 0:
                    P.add("act", lambda e, j=j, src=src: e.activation(out=st_v[:, j, :, 0:64], in_=src, func=ACT.Copy),
                          reads=[pb], writes=[b_stv])
                else:
                    P.add("dve", lambda e, j=j, src=src: e.tensor_copy(out=st_v[:, j, :, 0:64], in_=src), reads=[pb], writes=[b_stv])
            for sg in range(8):
                P.add("pool", lambda e, t=t, t0=t0, sg=sg: e.dma_start(out=va_s.ap()[:, sg, 4 * t:4 * t + 4, :], in_=st_v[:, :, sg, :]),
                      reads=[b_stv], writes=[b_va[t]], dma=True)
            for half in range(2):
                w, wb = ws.get(wq_s); ws.advance()
                for c4 in range(4):
                    c = half * 4 + c4
                    ps, pb = bank()
                    mm_group(ps, pb, lambda k, w=w, c4=c4: w[:, k, c4 * 128:(c4 + 1) * 128], lambda k: hnT[:, k, :], 8, [wb] + b_hnT)
                    sl = c % 2
                    if sl == 0:
                        P.add("act", lambda e, ps=ps, sl=sl: e.activation(out=st_k[sl], in_=ps, func=ACT.Copy), reads=[pb], writes=[b_stk[sl]])
                    else:
                        P.add("dve", lambda e, ps=ps, sl=sl: e.tensor_copy(out=st_k[sl], in_=ps), reads=[pb], writes=[b_stk[sl]])
                    P.add("pool", lambda e, t=t, t0=t0, c=c, sl=sl: e.dma_start(out=qT_flat[c * 128:(c + 1) * 128, t0:t0 + TT], in_=st_k[sl]),
                          reads=[b_stk[sl]], writes=[b_qT[t]], dma=True)
            w, wb = ws.get(wg_s); ws.advance()
            for j in range(4):
                ps, pb = bank()
                mm_group(ps[:, 0:48], pb, lambda k, j=j: hnT[:, k, j * 128:(j + 1) * 128], lambda k, w=w: w[:, k, 0:48], 8, [wb] + b_hnT)
                P.add("act", lambda e, j=j, ps=ps: e.activation(out=st_g[:, j, :], in_=ps[:, 0:48], func=ACT.Sigmoid),
                      reads=[pb], writes=[b_stg])
            P.add("pool", lambda e, t=t, t0=t0: e.dma_start(out=gate_s.ap()[t0:t0 + TT, :].rearrange("(j p) c -> p j c", p=128), in_=st_g),
                  reads=[b_stg], writes=[b_gate[t]], dma=True)
            for j in range(4):
                P.add("pool", lambda e, t=t, t0=t0, j=j: e.dma_start(out=h1_s.ap()[t0 + j * 128:t0 + (j + 1) * 128, :], in_=xs[:, j, :]),
                      reads=[b_xs[j]], writes=[b_h1[t]], dma=True)

            if final_stub:
                for j in range(4):
                    for hf in range(2):
                        tmp = c_sb[hf]
                        P.add("dve", lambda e, j=j, hf=hf, tmp=tmp: e.scalar_tensor_tensor(
                            out=tmp, in0=xs[:, j, hf * 512:(hf + 1) * 512], scalar=rstd[:, j:j + 1],
                            in1=finalg[:, hf * 512:(hf + 1) * 512], op0=ALU.mult, op1=ALU.mult),
                            reads=[b_xs[j], b_rstd, b_finalg], writes=[b_csb[hf]])
                        out_ops.append(P.add("pool", lambda e, j=j, hf=hf, tmp=tmp, t0=t0: e.dma_start(
                            out=out_t.ap()[t0 + j * 128:t0 + (j + 1) * 128, hf * 512:(hf + 1) * 512], in_=tmp),
                            reads=[b_csb[hf]], writes=[P.buf()], dma=True))

    SLOPES = [2.0 ** (-(h + 1) / 2.0) for h in range(NH)]

    def phase2a(nqt, ntiles_avail):
        P.barrier()
        A.reset(frame0)
        nkeys = ntiles_avail * TT
        nchunks_av = nkeys // 128
        b_const = P.buf("const2a")
        VMw = A.alloc([128, 2304], BF)
        Cm = A.alloc([128, 128], BF); Wm = A.alloc([128, 128], BF)
        AB = A.alloc([128, NH, 64], F32); CB = A.alloc([128, NH, 64], F32)
        Ttab = A.alloc([128, 255], F32)
        ctmp = A.alloc([128, 1024], F32); b_ctmp = P.buf("ctmp")
        kcT = A.alloc([64, 4, 512], BF); b_kcT = P.buf("kcT")
        cvr = A.alloc([128, 4, 4, 193], BF); b_cvr = P.buf("cvr")
        w1sb = A.alloc([128, 16, 256], BF); b_w1sb = P.buf("w1sb")
        w2sb = A.alloc([128, 2, 64], BF); b_w2sb = P.buf("w2sb")
        pe2f = A.alloc([128, 16], F32); pe2b = A.alloc([128, 16], BF); b_pe2 = P.buf("pe2")
        Xb = A.alloc([128, S // 2], BF); b_X = P.buf("X")
        H1 = A.alloc([128, 2, 512], BF); b_H1 = P.buf("H1")
        ks = A.alloc([128, S], BF); b_ks = P.buf("ks"); b_oh = P.buf("onehot")
        vs = A.alloc([128, 64, 65], BF); b_vs = P.buf("vs")
        kw = [A.alloc([64, 640], BF) for _ in range(2)]; b_kw = [P.buf() for _ in range(2)]
        vw = [A.alloc([128, 5, 65], BF) for _ in range(2)]; b_vw = [P.buf() for _ in range(2)]
        qt = [A.alloc([128, 512], BF) for _ in range(2)]; b_qt = [P.buf() for _ in range(2)]
        qtB = [A.alloc([128, 512], BF) for _ in range(2)]; b_qtB = [P.buf() for _ in range(2)]
        b_mA = [P.buf() for _ in range(2)]; b_mB = [P.buf() for _ in range(2)]
        selmW = A.alloc([128, 192], BF)
        gt = [A.alloc([128, 48], F32) for _ in range(2)]; b_gt = [P.buf() for _ in range(2)]
        pc = A.alloc([128, 4, 512], BF); b_pc = [P.buf() for _ in range(4)]
        NPP = 4
        pP = [A.alloc([128, 512], BF) for _ in range(NPP)]; b_pP = [P.buf() for _ in range(NPP)]
        Mb = [A.alloc([128, 512], BF) for _ in range(2)]; b_Mb = [P.buf() for _ in range(2)]
        acc = [A.alloc([128, 128], F32) for _ in range(2)]; b_acc = [P.buf() for _ in range(2)]
        wk = A.alloc([128, 128], F32); b_wk = P.buf("wk")
        m8 = A.alloc([128, 16], F32); b_m8 = P.buf("m8")
        selm = A.alloc([128, 128], BF); b_selm = P.buf("selm")
        rsum = A.alloc([128, 12], F32); b_rsum = P.buf("rsum")
        rinv = A.alloc([128, 12], F32); b_rinv = P.buf("rinv")
        fac = A.alloc([128, 12], F32); b_fac = P.buf("fac")
        t1 = A.alloc([128, 256], F32); t2 = A.alloc([128, 256], F32); t3 = A.alloc([128, 256], F32)
        b_t1 = P.buf("t1"); b_t2 = P.buf("t2"); b_t3 = P.buf("t3")
        ot = [A.alloc([128, 256], BF) for _ in range(2)]; b_ot = [P.buf() for _ in range(2)]
        zt = A.alloc([128, 512], BF); b_zt = P.buf("zt")
        P.add("pool", lambda e: e.memset(zt, 0.0), writes=[b_zt])

        def zero_acc(bk, ncol):
            P.add("pe", lambda e: e.matmul(bk[0][:, 0:ncol], lhsT=zt[:, 0:128], rhs=zt[:, 0:ncol], start=True, stop=False),
                  reads=[b_zt], writes=[bk[1]])

        def zbuild(e):
            r = None
            return r
        ohv = ctmp[64:128, 0:128]
        for half in range(2):
            P.add("pool", lambda e: e.memset(ctmp[64:128, 0:256], 1.0), writes=[b_ctmp])
            P.add("pool", lambda e, half=half: e.affine_select(out=ohv, in_=ohv, pattern=[[1, 128]], compare_op=ALU.is_equal,
                                                               fill=0.0, base=-64 * half, channel_multiplier=-1),
                  reads=[b_ctmp], writes=[b_ctmp])
            lo, hi = half * 64, half * 64 + 64
            P.add("dve", lambda e, lo=lo, hi=hi: e.tensor_copy(
                out=ks[64:128, lo * 64:hi * 64].rearrange("p (b k) -> p b k", k=64),
                in_=ctmp[64:128, lo:hi].unsqueeze(2).to_broadcast([64, 64, 64])), reads=[b_ctmp], writes=[b_oh])
        P.add("pool", lambda e: e.memset(selmW, 0.0), writes=[b_selm])
        for pz in range(3):
            x0 = pz * 768
            P.add("pool", lambda e: e.memset(ctmp[:, 0:768], 0.0), writes=[b_ctmp])
            P.add("pool", lambda e, x0=x0: e.affine_select(out=ctmp[:, 0:768], in_=ctmp[:, 0:768], pattern=[[1, 768]],
                                                           compare_op=ALU.is_ge, fill=NEGM, base=x0 - 31, channel_multiplier=-16),
                  reads=[b_ctmp], writes=[b_ctmp])
            P.add("dve", lambda e, x0=x0: e.tensor_copy(out=VMw[:, x0:x0 + 768], in_=ctmp[:, 0:768]), reads=[b_ctmp], writes=[b_const])
        for (M_, pat, base, cm) in ((Cm, [[1, 128]], 0, -1), (Wm, [[-1, 128]], -1, 1)):
            P.add("pool", lambda e: e.memset(ctmp[:, 0:128], 0.0), writes=[b_ctmp])
            P.add("pool", lambda e, pat=pat, base=base, cm=cm: e.affine_select(
                out=ctmp[:, 0:128], in_=ctmp[:, 0:128], pattern=pat, compare_op=ALU.is_ge, fill=NEGM, base=base, channel_multiplier=cm),
                reads=[b_ctmp], writes=[b_ctmp])
            P.add("dve", lambda e, M_=M_: e.tensor_copy(out=M_, in_=ctmp[:, 0:128]), reads=[b_ctmp], writes=[b_const])
        ov = ctmp[:, 0:512].rearrange("p (c j) -> p c j", c=4)
        P.add("pool", lambda e: e.memset(ctmp[:, 0:512], 1.0), writes=[b_ctmp])
        P.add("pool", lambda e: e.affine_select(out=ov, in_=ov, pattern=[[128, 4], [-4, 128]], compare_op=ALU.is_ge,
                                                fill=0.0, base=1, channel_multiplier=1), reads=[b_ctmp], writes=[b_ctmp])
        P.add("pool", lambda e: e.affine_select(out=ov, in_=ov, pattern=[[-128, 4], [4, 128]], compare_op=ALU.is_ge,
                                                fill=0.0, base=3, channel_multiplier=-1), reads=[b_ctmp], writes=[b_ctmp])
        P.add("pool", lambda e: e.memset(cvr, 0.0), writes=[b_cvr])
        for g in range(4):
            P.add("dve", lambda e, g=g: e.tensor_copy(out=cvr[:, g, :, 65:193], in_=ov), reads=[b_ctmp], writes=[b_cvr])
            P.add("dve", lambda e, g=g: e.memset(cvr[:, g, :, 64:65], 1.0), writes=[b_cvr])
        def tt(e):
            e.memset(Ttab[0:64, 0:126], 0.0); e.memset(Ttab[0:64, 126:128], 1e4); e.memset(Ttab[0:64, 128:255], -1e30)
            e.memset(Ttab[64:128, 0:127], 0.0); e.memset(Ttab[64:128, 127:129], 1e4)
            return e.memset(Ttab[64:128, 129:255], -1e30)
        P.add("pool", tt, writes=[b_const])
        P.add("pool", lambda e: e.iota(ctmp[:, 0:64], pattern=[[-128, 64]], base=-64, channel_multiplier=1,
                                       allow_small_or_imprecise_dtypes=True), writes=[b_ctmp])
        P.add("pool", lambda e: e.iota(ctmp[:, 64:128], pattern=[[-128, 64]], base=-48, channel_multiplier=16,
                                       allow_small_or_imprecise_dtypes=True), reads=[b_ctmp], writes=[b_ctmp])
        for h in range(NH):
            P.add("dve", lambda e, h=h: e.tensor_scalar(out=AB[:, h, :], in0=ctmp[:, 0:64], scalar1=SLOPES[h], scalar2=None, op0=ALU.mult),
                  reads=[b_ctmp], writes=[b_const])
            P.add("dve", lambda e, h=h: e.tensor_scalar(out=CB[:, h, :], in0=ctmp[:, 64:128], scalar1=-0.5, scalar2=SLOPES[h],
                                                        op0=ALU.add, op1=ALU.mult), reads=[b_ctmp], writes=[b_const])

        P.add("pool", lambda e: e.memset(kcT, 0.0), writes=[b_kcT])
        P.add("pool", lambda e: e.memset(H1, 0.0), writes=[b_H1])
        if nkeys < S:
            P.add("pool", lambda e: e.memset(Xb, 0.0), writes=[b_X])
        for si in range(2):
            pe_t = pe_k_t if si == 0 else pe_v_t
            P.add("sp", lambda e, si=si: e.dma_start(out=w1sb, in_=w1_s[si].ap()), reads=[wbuf(w1_s[si])], writes=[b_w1sb], dma=True)
            P.add("sp", lambda e, si=si: e.dma_start(out=w2sb, in_=w2_s[si].ap()), reads=[wbuf(w2_s[si])], writes=[b_w2sb], dma=True)
            pe_src = bass.AP(pe_t, 0, [[1, 128], [128, 16]])
            P.add("sp", lambda e, pe_src=pe_src: e.dma_start(out=pe2f, in_=pe_src, allow_slow_non_contiguous=True), writes=[b_pe2], dma=True)
            P.add("dve", lambda e: e.tensor_copy(out=pe2b, in_=pe2f), reads=[b_pe2], writes=[b_pe2])
            for hc in range(2):
                ps, pb = bank()

                def bm(e, ps=ps, hc=hc):
                    for lp in range(16):
                        r = e.matmul(ps[:, 0:1], lhsT=w1sb[:, lp, hc * 128:(hc + 1) * 128], rhs=pe2b[:, lp:lp + 1],
                                     start=(lp == 0), stop=(lp == 15))
                    return r
                P.add("pe", bm, reads=[b_w1sb, b_pe2], writes=[pb])
                P.add("dve", lambda e, ps=ps, hc=hc, si=si: e.tensor_copy(out=cbias[:, si, hc:hc + 1], in_=ps[:, 0:1]),
                      reads=[pb], writes=[b_cbias])
            for g in range(4):
                npair = nkeys // 2
                P.add("sp", lambda e, si=si, g=g, npair=npair: e.dma_start(out=Xb[:, 0:npair], in_=kc2_s.ap()[si * 4 + g, :, 0:npair]),
                      reads=b_kc2[:ntiles_avail], writes=[b_X], dma=True)
                for hc in range(2):
                    ps, pb = bank()

                    def cm_(e, ps=ps, hc=hc):
                        for lp in range(16):
                            r = e.matmul(ps[:, 0:511], lhsT=w1sb[:, lp, hc * 128:(hc + 1) * 128], rhs=Xb[:, lp:lp + 4081:8],
                                         start=(lp == 0), stop=(lp == 15))
                        return r
                    P.add("pe", cm_, reads=[b_w1sb, b_X], writes=[pb])
                    P.add("act", lambda e, ps=ps, hc=hc, si=si: e.activation(out=H1[:, hc, 0:511], in_=ps[:, 0:511], func=ACT.Silu,
                                                                             bias=cbias[:, si, hc:hc + 1], scale=1.0),
                          reads=[pb, b_cbias], writes=[b_H1])
                if si == 0:
                    ps, pb = bank()

                    def k2(e, ps=ps):
                        for hc in range(2):
                            r = e.matmul(ps[0:64, 0:511], lhsT=w2sb[:, hc, :], rhs=H1[:, hc, 0:511], start=(hc == 0), stop=(hc == 1))
                        return r
                    P.add("pe", k2, reads=[b_w2sb, b_H1], writes=[pb])
                    P.add("dve", lambda e, ps=ps, g=g: e.tensor_copy(out=kcT[:, g, 0:511], in_=ps[0:64, 0:511]), reads=[pb], writes=[b_kcT])
                else:
                    ps, pb = bank()

                    def v2(e, ps=ps):
                        for c in range(4):
                            for hc in range(2):
                                r = e.matmul(ps[:, c * 64:(c + 1) * 64], lhsT=H1[:, hc, c * 128:(c + 1) * 128], rhs=w2sb[:, hc, :],
                                             start=(hc == 0), stop=(hc == 1))
                        return r
                    P.add("pe", v2, reads=[b_w2sb, b_H1], writes=[pb])
                    P.add("dve", lambda e, ps=ps, g=g: e.tensor_copy(out=cvr[:, g, :, 0:64], in_=ps[:, 0:256].rearrange("p (c d) -> p c d", c=4)),
                          reads=[pb], writes=[b_cvr])

        bS = [(banks[0], bbufs[0]), (banks[1], bbufs[1])]
        bOA, bOB, bOs, bOw = (banks[2], bbufs[2]), (banks[3], bbufs[3]), (banks[4], bbufs[4]), (banks[5], bbufs[5])
        st2 = {"s": 0, "p": 0, "it": 0}

        def sbank():
            i = st2["s"]; st2["s"] = (i + 1) % 2
            return bS[i]

        def pslot():
            i = st2["p"]; st2["p"] = (i + 1) % NPP
            return pP[i], b_pP[i]

        def mask_mm(e, ps, M_):
            for h in range(4):
                r = e.matmul(ps[:, h * 128:(h + 1) * 128], lhsT=ident, rhs=M_, start=False, stop=True)
            return r

        RNG = []
        for g_ in range(4):
            smin = SLOPES[4 * g_ + 3]
            r_ = 1
            while smin * (128 * r_ - 127) < 100.0:
                r_ += 1
            RNG.append(r_)

        def qtile(g, qi):
            it = st2["it"]; st2["it"] += 1
            sl = it % 2
            tl = qi // 4
            c0 = max(0, qi - 4)
            nwc = qi - c0 + 1
            P.add("sp", lambda e, g=g, qi=qi, sl=sl: e.dma_start(
                out=qt[sl][0:64, :].rearrange("d (h t) -> d h t", h=4), in_=qT_s.ap()[4 * g:4 * g + 4, :, qi * 128:(qi + 1) * 128].rearrange("h d t -> d h t")),
                reads=[b_qT[tl]], writes=[b_qt[sl]], dma=True)
            if qi >= 32:
                P.add("sp", lambda e, g=g, qi=qi, sl=sl: e.dma_start(
                    out=qtB[sl][0:64, :].rearrange("d (h t) -> d h t", h=4), in_=qT_s.ap()[4 * g:4 * g + 4, :, qi * 128:(qi + 1) * 128].rearrange("h d t -> d h t")),
                    reads=[b_qT[tl]], writes=[b_qtB[sl]], dma=True)
            P.add("sp", lambda e, qi=qi, sl=sl: e.dma_start(out=gt[sl], in_=gate_s.ap()[qi * 128:(qi + 1) * 128, :]),
                  reads=[b_gate[tl]], writes=[b_gt[sl]], dma=True)
            P.add("sp", lambda e, g=g, qi=qi, sl=sl, c0=c0, nwc=nwc: e.dma_start(
                out=kw[sl][:, 0:nwc * 128], in_=kT_s.ap()[4 + g, :, c0 * 128:(qi + 1) * 128]),
                reads=b_kT[c0 // 4:tl + 1], writes=[b_kw[sl]], dma=True)
            P.add("sp", lambda e, g=g, qi=qi, sl=sl, c0=c0, nwc=nwc: e.dma_start(
                out=vw[sl][:, 0:nwc, :], in_=va_s.ap()[:, 4 + g, c0:qi + 1, :]),
                reads=b_va[c0 // 4:tl + 1], writes=[b_vw[sl]], dma=True)
            qv = qt[sl][0:64, :]
            zero_acc(bOA, 386); zero_acc(bOB, 386); zero_acc(bOw, 260); zero_acc(bOs, 260)
            mb = Mb[sl]; bmb = b_Mb[sl]
            ncc = qi // 16 + 1
            tasks = []

            def mk_cmp(c):
                dl = qi - 16 * c
                bk = {}

                def S_():
                    ps, pb = sbank(); bk["ps"] = ps; bk["pb"] = pb

                    def smm(e):
                        r = e.matmul(ps, lhsT=kcT[:, g, c * 128:(c + 1) * 128], rhs=qv, start=True, stop=(dl > 16))
                        if dl <= 16:
                            r = mask_mm(e, ps, VMw[:, 128 * dl:128 * dl + 128])
                        return r
                    P.add("pe", smm, reads=[b_kcT, b_qt[sl], b_const, b_ident], writes=[pb])

                def A_():
                    ps, pb = bk["ps"], bk["pb"]
                    for h in range(4):
                        P.add("act", lambda e, h=h: e.activation(
                            out=pc[:, c, h * 128:(h + 1) * 128], in_=ps[:, h * 128:(h + 1) * 128], func=ACT.Exp,
                            bias=CB[:, 4 * g + h, dl:dl + 1], scale=1.0), reads=[pb, b_const], writes=[b_pc[c]])

                def V_():
                    def omm(e):
                        for h in range(4):
                            ob = bOA[0] if h < 2 else bOB[0]
                            col = (h % 2) * 193
                            r = e.matmul(ob[:, col:col + 193], lhsT=pc[:, c, h * 128:(h + 1) * 128], rhs=cvr[:, g, c, :],
                                         start=False, stop=(c == ncc - 1))
                        return r
                    P.add("pe", omm, reads=[b_pc[c], b_cvr], writes=[bOA[1], bOB[1]])
                    if c == ncc - 1:
                        selection()
                return (S_, A_, V_)

            def selection():
                P.add("dve", lambda e: e.tensor_scalar_max(out=rsum[:, 0:2], in0=bOA[0][:, 64:258:193], scalar1=1e-30),
                      reads=[bOA[1]], writes=[b_rsum])
                P.add("dve", lambda e: e.tensor_scalar_max(out=rsum[:, 2:4], in0=bOB[0][:, 64:258:193], scalar1=1e-30),
                      reads=[bOB[1]], writes=[b_rsum])
                P.add("dve", lambda e: e.reciprocal(out=rinv[:, 0:4], in_=rsum[:, 0:4]), reads=[b_rsum], writes=[b_rinv])
                tsl = Ttab[:, 127 - 2 * qi:255 - 2 * qi]
                for h in range(4):
                    ob, obb = (bOA if h < 2 else bOB)
                    col = (h % 2) * 193 + 65
                    src1 = tsl if h == 0 else acc[(h + 1) % 2]
                    rb = [obb, b_rinv] + ([b_const] if h == 0 else [b_acc[(h + 1) % 2]])
                    P.add("dve", lambda e, ob=ob, col=col, h=h, src1=src1: e.scalar_tensor_tensor(
                        out=acc[h % 2], in0=ob[:, col:col + 128], scalar=rinv[:, h:h + 1], in1=src1, op0=ALU.mult, op1=ALU.add),
                        reads=rb, writes=[b_acc[h % 2]])
                af = acc[1]; baf = b_acc[1]
                P.add("dve", lambda e: e.tensor_scalar_add(out=af[:, 0:1], in0=af[:, 0:1], scalar1=1e4), reads=[baf], writes=[baf])
                P.add("dve", lambda e: e.max(out=m8[:, 0:8], in_=af), reads=[baf], writes=[b_m8])
                P.add("dve", lambda e: e.match_replace(out=wk, in_to_replace=m8[:, 0:8], in_values=af, imm_value=-3e38),
                      reads=[baf, b_m8], writes=[b_wk])
                P.add("dve", lambda e: e.max(out=m8[:, 8:16], in_=wk), reads=[b_wk], writes=[b_m8])
                P.add("dve", lambda e: e.tensor_scalar(out=selmW[:, 64:192], in0=af, scalar1=m8[:, 15:16], scalar2=-1.0, op0=ALU.is_ge, op1=ALU.add),
                      reads=[baf, b_m8], writes=[b_selm])
                tb, tbb = tbank()

                def trm(e):
                    r = e.transpose(out=tb[:, 0:128], in_=selmW[:, 0:128], identity=ident)
                    if qi >= 32:
                        r = e.transpose(out=tb[:, 128:256], in_=selmW[:, 64:192], identity=ident)
                    return r
                P.add("pe", trm, reads=[b_selm, b_ident], writes=[tbb])
                P.add("act", lambda e: e.activation(out=qt[sl][64:128, :].rearrange("p (h t) -> p h t", h=4),
                                                    in_=tb[64:128, 0:128].unsqueeze(1).to_broadcast([64, 4, 128]),
                                                    func=ACT.Copy, scale=30000.0), reads=[tbb], writes=[b_mA[sl]])
                if qi >= 32:
                    P.add("act", lambda e: e.activation(out=qtB[sl][64:128, :].rearrange("p (h t) -> p h t", h=4),
                                                        in_=tb[64:128, 128:256].unsqueeze(1).to_broadcast([64, 4, 128]),
                                                        func=ACT.Copy, scale=30000.0), reads=[tbb], writes=[b_mB[sl]])

            def mk_kv(c, kind):
                bk = {}
                rc = qi - c

                def S_():
                    ps, pb = sbank(); bk["ps"] = ps; bk["pb"] = pb
                    if kind == "win":
                        masked = (c == qi) or (c == qi - 4)

                        def wmm(e):
                            r = e.matmul(ps, lhsT=kw[sl][:, (c - c0) * 128:(c - c0 + 1) * 128], rhs=qv, start=True, stop=not masked)
                            if c == qi:
                                r = mask_mm(e, ps, Cm)
                            elif c == qi - 4:
                                r = mask_mm(e, ps, Wm)
                            return r
                        P.add("pe", wmm, reads=[b_kw[sl], b_qt[sl], b_const, b_ident], writes=[pb])
                    else:
                        rhs_t = qt[sl] if c < 32 else qtB[sl]
                        rb_ = [b_qt[sl], b_mA[sl]] if c < 32 else [b_qtB[sl], b_mB[sl]]

                        def s2(e):
                            r = e.matmul(ps, lhsT=ks[:, c * 128:(c + 1) * 128], rhs=rhs_t, start=True, stop=(c != qi))
                            if c == qi:
                                r = mask_mm(e, ps, Cm)
                            return r
                        P.add("pe", s2, reads=[b_ks, b_oh, b_const, b_ident] + rb_, writes=[pb])

                def A_():
                    ps, pb = bk["ps"], bk["pb"]
                    pt_, ptb = pslot(); bk["pt"] = pt_; bk["ptb"] = ptb
                    for h in range(4):
                        P.add("act", lambda e, h=h: e.activation(
                            out=pt_[:, h * 128:(h + 1) * 128], in_=ps[:, h * 128:(h + 1) * 128], func=ACT.Exp,
                            bias=AB[:, 4 * g + h, rc:rc + 1], scale=1.0), reads=[pb, b_const], writes=[ptb])

                def V_():
                    pt_, ptb = bk["pt"], bk["ptb"]
                    if kind == "win":
                        def wpv(e):
                            for h in range(4):
                                r = e.matmul(bOw[0][:, h * 65:(h + 1) * 65], lhsT=pt_[:, h * 128:(h + 1) * 128], rhs=vw[sl][:, c - c0, :],
                                             start=False, stop=(c == qi))
                            return r
                        P.add("pe", wpv, reads=[ptb, b_vw[sl]], writes=[bOw[1]])
                    else:
                        def spv(e):
                            for h in range(4):
                                r = e.matmul(bOs[0][:, h * 65:(h + 1) * 65], lhsT=pt_[:, h * 128:(h + 1) * 128], rhs=vs[:, c, :],
                                             start=False, stop=(c == qi))
                            return r
                        P.add("pe", spv, reads=[ptb, b_vs], writes=[bOs[1]])
                return (S_, A_, V_)

            for c in range(ncc):
                tasks.append(mk_cmp(c))
            for c in range(c0, qi + 1):
                tasks.append(mk_kv(c, "win"))
            for c in range(max(0, qi + 1 - RNG[g]), qi + 1):
                tasks.append(mk_kv(c, "slc"))
            tasks[0][0]()
            for ti in range(len(tasks)):
                if ti + 1 < len(tasks):
                    tasks[ti + 1][0]()
                tasks[ti][1]()
                tasks[ti][2]()
            P.add("dve", lambda e: e.reciprocal(out=rinv[:, 4:8], in_=bOs[0][:, 64:260:65]), reads=[bOs[1]], writes=[b_rinv])
            P.add("dve", lambda e: e.reciprocal(out=rinv[:, 8:12], in_=bOw[0][:, 64:260:65]), reads=[bOw[1]], writes=[b_rinv])
            gv = gt[sl].rearrange("p (h b) -> p h b", b=3)
            for b_ in range(3):
                P.add("dve", lambda e, b_=b_, gv=gv, g=g: e.tensor_tensor(out=fac[:, 4 * b_:4 * b_ + 4], in0=gv[:, 4 * g:4 * g + 4, b_],
                                                                          in1=rinv[:, 4 * b_:4 * b_ + 4], op=ALU.mult),
                      reads=[b_gt[sl], b_rinv], writes=[b_fac])
            for hb, (ob, obb) in enumerate((bOA, bOB)):
                P.add("dve", lambda e, hb=hb, ob=ob: e.tensor_tensor(
                    out=t1[:, hb * 128:(hb + 1) * 128].rearrange("p (h d) -> p h d", h=2),
                    in0=ob[:, 0:386].rearrange("p (h x) -> p h x", h=2)[:, :, 0:64],
                    in1=fac[:, 2 * hb:2 * hb + 2].unsqueeze(2).to_broadcast([128, 2, 64]), op=ALU.mult),
                    reads=[obb, b_fac], writes=[b_t1])
            P.add("dve", lambda e: e.tensor_tensor(out=t2.rearrange("p (h d) -> p h d", h=4),
                                                   in0=bOs[0][:, 0:260].rearrange("p (h x) -> p h x", h=4)[:, :, 0:64],
                                                   in1=fac[:, 4:8].unsqueeze(2).to_broadcast([128, 4, 64]), op=ALU.mult),
                  reads=[bOs[1], b_fac], writes=[b_t2])
            P.add("dve", lambda e: e.tensor_tensor(out=t3.rearrange("p (h d) -> p h d", h=4),
                                                   in0=bOw[0][:, 0:260].rearrange("p (h x) -> p h x", h=4)[:, :, 0:64],
                                                   in1=fac[:, 8:12].unsqueeze(2).to_broadcast([128, 4, 64]), op=ALU.mult),
                  reads=[bOw[1], b_fac], writes=[b_t3])
            P.add("pool", lambda e: e.tensor_tensor(out=t2, in0=t2, in1=t3, op=ALU.add), reads=[b_t2, b_t3], writes=[b_t2])
            o_ = ot[sl]
            P.add("pool", lambda e, o_=o_: e.tensor_tensor(out=o_, in0=t1, in1=t2, op=ALU.add), reads=[b_t1, b_t2], writes=[b_ot[sl]])
            P.add("pool", lambda e, o_=o_, qi=qi, g=g: e.dma_start(out=o_s.ap()[qi * 128:(qi + 1) * 128, g * 256:(g + 1) * 256], in_=o_),
                  reads=[b_ot[sl]], writes=[b_o[tl]], dma=True)

        for g in range(4):
            P.add("sp", lambda e, g=g: e.dma_start(out=ks[0:64, 0:nkeys], in_=kT_s.ap()[g, :, 0:nkeys]), reads=b_kT[:ntiles_avail], writes=[b_ks], dma=True)
            P.add("sp", lambda e, g=g: e.dma_start(out=vs[:, 0:nchunks_av, :], in_=va_s.ap()[:, g, 0:nchunks_av, :]),
                  reads=b_va[:ntiles_avail], writes=[b_vs], dma=True)
            for qi in range(nqt):
                qtile(g, qi)

    def phase2b(ntiles):
        P.barrier()
        pl = []
        for _ in range(ntiles):
            pl += [(wo_s, 0, 8, i * 512, 512) for i in range(2)]
            pl += ffn_plan(1)
        ws = WStream(pl)
        for t in range(ntiles):
            t0 = t * TT
            for j in range(4):
                P.add("sp", lambda e, j=j, t0=t0: e.dma_start(out=xs[:, j, :], in_=h1_s.ap()[t0 + j * 128:t0 + (j + 1) * 128, :]),
                      reads=[b_h1[t]], writes=[b_xs[j]], dma=True)
                P.add("sp", lambda e, j=j, t0=t0: e.dma_start(out=hn[:, j, :], in_=o_s.ap()[t0 + j * 128:t0 + (j + 1) * 128, :]),
                      reads=[b_o[t]], writes=[b_hn[j]], dma=True)
            transpose_hn()
            down_proj_tm(ws, wo_s, hnT, b_hnT)
            norm_T()
            ffn(1, ws)
            rstd_only()
            for j in range(4):
                for hf in range(2):
                    tmp = c_sb[hf]
                    P.add("dve", lambda e, j=j, hf=hf, tmp=tmp: e.scalar_tensor_tensor(
                        out=tmp, in0=xs[:, j, hf * 512:(hf + 1) * 512], scalar=rstd[:, j:j + 1],
                        in1=finalg[:, hf * 512:(hf + 1) * 512], op0=ALU.mult, op1=ALU.mult),
                        reads=[b_xs[j], b_rstd, b_finalg], writes=[b_csb[hf]])
                    out_ops.append(P.add("pool", lambda e, j=j, hf=hf, tmp=tmp, t0=t0: e.dma_start(
                        out=out_t.ap()[t0 + j * 128:t0 + (j + 1) * 128, hf * 512:(hf + 1) * 512], in_=tmp),
                        reads=[b_csb[hf]], writes=[P.buf()], dma=True))

    return dict(nc=nc, P=P, A=A, phase1=phase1, phase2a=phase2a, phase2b=phase2b, out_ops=out_ops, locals=locals())


_CACHE = {}


def kernel(**inputs):
    if "nc" not in _CACHE:
        ctx = build()
        ctx["phase1"](NT)
        ctx["phase2a"](NQ, NT)
        ctx["phase2b"](NT)
        ctx["P"].emit(final_dma_ops=ctx["out_ops"])
        _CACHE["nc"] = ctx["nc"]
    nc = _CACHE["nc"]
    f = {k: np.ascontiguousarray(np.asarray(v, dtype=np.float32)) for k, v in inputs.items()}
    in_maps = []
    for c in range(8):
        b = c // 2
        d = dict(f)
        d["x"] = np.ascontiguousarray(f["x"][b])
        for k in ("a_norm", "a_w_in", "a_conv", "a_w_out", "b_norm", "b_w_qg", "b_w_o"):
            d[k] = np.ascontiguousarray(f[k][0])
        in_maps.append(d)
    res = run_bass_kernel_spmd(nc, in_maps, core_ids=list(range(8)))
    out = np.stack([np.asarray(res.results[2 * b]["out"]) for b in range(4)], axis=0)
    return out.astype(np.float32)
```

```python
import contextlib
import numpy as np
import concourse.bass as bass
import concourse.mybir as mybir
from concourse.bass_utils import run_bass_kernel_spmd

F32 = mybir.dt.float32
BF = mybir.dt.bfloat16
ACT = mybir.ActivationFunctionType
ALU = mybir.AluOpType

D = 1024
S = 8192
FF = 2816
NH = 16
NG = 4
DH = 64
TT = 512
NT = S // TT
NQ = S // 128
NEGM = -30000.0
EPS = 1e-5

ENGS = ("pe", "act", "dve", "pool", "sp")


class Buf:
    __slots__ = ("name", "last_w", "readers")

    def __init__(self, name):
        self.name = name
        self.last_w = None
        self.readers = []


class Op:
    __slots__ = ("eng", "fn", "deps", "dma", "sig", "sem", "has_dep", "prewait")

    def __init__(self, eng, fn, dma):
        self.eng = eng
        self.fn = fn
        self.dma = dma
        self.deps = []
        self.sig = None
        self.sem = None
        self.has_dep = False
        self.prewait = None


class Prog:
    def __init__(self, nc, n_dma_sems=24):
        self.nc = nc
        self.ops = {e: [] for e in ENGS}
        self.n_dma_sems = n_dma_sems
        self.nbuf = 0
        self.dmas_since_barrier = []

    def buf(self, name=None):
        self.nbuf += 1
        return Buf(name or f"b{self.nbuf}")

    def add(self, eng, fn, reads=(), writes=(), dma=False):
        op = Op(eng, fn, dma)
        deps = {}

        def dep(p, kind):
            if p is None or p is op:
                return
            if p.eng == eng and not p.dma:
                if eng == "pe":
                    return
                if kind in ("war", "waw"):
                    return
            deps[id(p)] = p

        for r in reads:
            dep(r.last_w, "raw")
        for w in writes:
            dep(w.last_w, "waw")
            for rd in w.readers:
                dep(rd, "war")
        for r in reads:
            if dma:
                r.readers.append(op)
            else:
                r.readers = [x for x in r.readers if x.dma or x.eng != eng]
                r.readers.append(op)
        for w in writes:
            w.last_w = op
            w.readers = []
        op.deps = list(deps.values())
        for p in op.deps:
            p.has_dep = True
        self.ops[eng].append(op)
        if dma:
            self.dmas_since_barrier.append(op)
        return op

    def barrier(self):
        lasts = []
        for e in ENGS:
            for op in reversed(self.ops[e]):
                if not op.dma:
                    lasts.append(op)
                    break
        dm = list(self.dmas_since_barrier)
        self.dmas_since_barrier = []
        for e in ENGS:
            op = Op(e, lambda eng: eng.nop(), False)
            op.deps = [p for p in lasts if p.eng != e] + dm
            for p in op.deps:
                p.has_dep = True
            self.ops[e].append(op)

    def emit(self, final_dma_ops=()):
        nc = self.nc
        if final_dma_ops:
            fin = Op("sp", lambda e: e.nop(), False)
            fin.deps = list(final_dma_ops)
            for p in fin.deps:
                p.has_dep = True
            self.ops["sp"].append(fin)
        with contextlib.ExitStack() as st:
            esem = {e: st.enter_context(nc.semaphore(f"s_{e}")) for e in ENGS}
            dsem = {e: [st.enter_context(nc.semaphore(f"d_{e}{i}")) for i in range(self.n_dma_sems)]
                    for e in ("sp", "pool", "act")}
            for e in ENGS:
                cnt = 0
                dcnt = 0
                for op in self.ops[e]:
                    if op.dma:
                        R = self.n_dma_sems
                        op.sem = dsem[e][dcnt % R]
                        op.sig = 16 * (dcnt // R + 1)
                        if op.sig > 16:
                            op.prewait = (op.sem, op.sig - 16)
                        dcnt += 1
                    elif op.has_dep:
                        cnt += 1
                        op.sem = esem[e]
                        op.sig = cnt
            block = st.enter_context(nc.Block())

            def run(e, eng_obj):
                waited = {}
                for op in self.ops[e]:
                    ws = {}
                    if op.prewait:
                        ws[id(op.prewait[0])] = op.prewait
                    for p in op.deps:
                        k = id(p.sem)
                        if k not in ws or ws[k][1] < p.sig:
                            ws[k] = (p.sem, p.sig)
                    for k, (s, v) in ws.items():
                        if waited.get(k, 0) >= v:
                            continue
                        waited[k] = v
                        eng_obj.wait_ge(s, v)
                    inst = op.fn(eng_obj)
                    if op.dma:
                        inst.then_inc(op.sem, 16)
                    elif op.has_dep:
                        inst.then_inc(op.sem, 1)

            @block.sync
            def _(sync):
                run("sp", sync)

            @block.tensor
            def _(t):
                run("pe", t)

            @block.vector
            def _(v):
                run("dve", v)

            @block.scalar
            def _(a):
                run("act", a)

            @block.gpsimd
            def _(g):
                run("pool", g)


class Arena:
    def __init__(self, nc, base, top):
        self.nc = nc
        self.base = (base + 31) // 32 * 32
        self.top = top
        self.cur = self.base
        self.n = 0

    def mark(self):
        return self.cur

    def reset(self, m):
        self.cur = m

    def alloc(self, shape, dtype):
        nb = 4 if dtype == F32 else 2
        sz = nb
        for s in shape[1:]:
            sz *= s
        sz = (sz + 31) // 32 * 32
        off = self.cur
        assert off + sz <= self.top, ("SBUF overflow", off + sz - self.base, self.top - self.base)
        self.cur += sz
        self.n += 1
        return self.nc.alloc_sbuf_tensor_at(f"t{self.n}", list(shape), dtype, offset=off).ap()


def dram_ap(t, offset, pat):
    return bass.AP(t, offset, [list(p) for p in pat])


def build(debug=False):
    nc = bass.Bass("TRN2", target_bir_lowering=False)
    P = Prog(nc)
    A = Arena(nc, nc.sbuf_base, nc.sbuf_top)

    def din(name, shape):
        return nc.dram_tensor(name, list(shape), F32, kind="ExternalInput")

    x_t = din("x", [S, D])
    a_norm_t = din("a_norm", [D]); a_w_in_t = din("a_w_in", [D, 3 * D]); a_conv_t = din("a_conv", [3, D])
    a_w_out_t = din("a_w_out", [D, D]); kv_norm_t = din("kv_norm", [D]); w_kv_t = din("w_kv", [D, 1536])
    pe_k_t = din("cmp_pe_k", [32, 64]); w1_k_t = din("cmp_w1_k", [2048, 256]); w2_k_t = din("cmp_w2_k", [256, 64])
    pe_v_t = din("cmp_pe_v", [32, 64]); w1_v_t = din("cmp_w1_v", [2048, 256]); w2_v_t = din("cmp_w2_v", [256, 64])
    b_norm_t = din("b_norm", [D]); b_w_qg_t = din("b_w_qg", [D, 1072]); b_w_o_t = din("b_w_o", [D, D])
    f_norm_t = din("f_norm", [2, D]); f_w_gu_t = din("f_w_gu", [2, D, 2 * FF]); f_w_down_t = din("f_w_down", [2, FF, D])
    final_norm_t = din("final_norm", [D])
    out_t = nc.dram_tensor("out", [S, D], F32, kind="ExternalOutput")

    def scr(name, shape, dt=BF):
        return nc.dram_tensor(name, list(shape), dt)

    win_s = scr("win_s", [128, 8, 3072]); wout_s = scr("wout_s", [128, 8, 1024])
    wgu_s = [scr(f"wgu_s{l}", [128, 8, 2 * FF]) for l in range(2)]
    wdn_s = [scr(f"wdn_s{l}", [128, 22, 1024]) for l in range(2)]
    wfm_s = scr("wfm_s", [128, 8, 1536]); wtm_s = scr("wtm_s", [128, 8, 512])
    wq_s = scr("wq_s", [128, 8, 1024]); wg_s = scr("wg_s", [128, 8, 48]); wo_s = scr("wo_s", [128, 8, 1024])
    w1_s = [scr(f"w1_s{i}", [128, 16, 256]) for i in range(2)]
    w2_s = [scr(f"w2_s{i}", [128, 2, 64]) for i in range(2)]
    kT_s = scr("kT_s", [8, 64, S])
    kc2_s = scr("kc2_s", [8, 128, S // 2])
    va_s = scr("va_s", [128, 8, 64, 65])
    qT_s = scr("qT_s", [NH, 64, S])
    gate_s = scr("gate_s", [S, 48], F32)
    h1_s = scr("h1_s", [S, D], F32)
    o_s = scr("o_s", [S, D])

    NBANK = 6
    banks = [nc.alloc_psum_tensor(f"pb{i}", [128, 512], F32).ap() for i in range(NBANK)]
    bbufs = [P.buf(f"pb{i}") for i in range(NBANK)]
    tbanks = [nc.alloc_psum_tensor(f"tb{i}", [128, 1024], BF).ap() for i in range(2)]
    tbufs = [P.buf(f"tb{i}") for i in range(2)]
    bstate = {"b": 0, "t": 0}

    def bank():
        i = bstate["b"]
        bstate["b"] = (i + 1) % NBANK
        return banks[i], bbufs[i]

    def tbank():
        i = bstate["t"]
        bstate["t"] = (i + 1) % 2
        return tbanks[i], tbufs[i]

    ident = A.alloc([128, 128], BF); b_ident = P.buf("ident")
    identf = A.alloc([128, 128], F32)
    gP = A.alloc([128, 6, 8], F32); b_gP = P.buf("gP")
    convP = A.alloc([128, 8, 3], F32); b_convP = P.buf("convP")
    finalg = A.alloc([128, D], F32); b_finalg = P.buf("finalg")
    ss = A.alloc([128, 4], F32); b_ss = P.buf("ss")
    rstd = A.alloc([128, 4], F32); b_rstd = P.buf("rstd")
    cvh = A.alloc([128, 8, 2], F32); b_cvh = [P.buf(f"cvh{f}") for f in range(8)]
    cbias = A.alloc([128, 2, 2], F32); b_cbias = P.buf("cbias")
    frame0 = A.mark()

    xs = A.alloc([128, 4, D], F32); b_xs = [P.buf(f"xs{j}") for j in range(4)]
    junk = A.alloc([128, D], BF); b_junk = P.buf("junk")
    hn = A.alloc([128, 4, D], BF); b_hn = [P.buf(f"hn{j}") for j in range(4)]
    hnT = A.alloc([128, 8, TT], BF); b_hnT = [P.buf(f"hnT{j}") for j in range(4)]
    actT = A.alloc([128, 22, TT], BF); b_actT = [P.buf(f"actT{k}") for k in range(22)]
    silu_t = [A.alloc([128, TT], F32) for _ in range(2)]; b_silu = [P.buf() for _ in range(2)]
    NSLOT = 5
    ring = [A.alloc([128, 8, 512], BF) for _ in range(NSLOT)]; b_ring = [P.buf(f"ring{i}") for i in range(NSLOT)]
    c_sb = [A.alloc([128, TT], F32) for _ in range(2)]; b_csb = [P.buf() for _ in range(2)]
    cv = [A.alloc([128, TT + 2], F32) for _ in range(2)]; b_cv = [P.buf() for _ in range(2)]
    u_t = [A.alloc([128, TT], F32) for _ in range(2)]; b_u = [P.buf() for _ in range(2)]
    buT = A.alloc([128, 8, TT], BF); b_buT = [P.buf(f"buT{k}") for k in range(8)]
    st_k = [A.alloc([128, TT], BF) for _ in range(2)]; b_stk = [P.buf() for _ in range(2)]
    st_kc = A.alloc([128, 8, 256], BF); b_stkc = P.buf("stkc")
    st_v = A.alloc([128, 4, 8, 65], BF); b_stv = P.buf("stv")
    st_g = A.alloc([128, 4, 48], F32); b_stg = P.buf("stg")
    dense_end = A.mark()

    P.add("pool", lambda e: e.memset(identf, 0.0), writes=[b_ident])
    P.add("pool", lambda e: e.affine_select(out=identf, in_=identf, pattern=[[-1, 128]], compare_op=ALU.not_equal,
                                            fill=1.0, base=0, channel_multiplier=1), reads=[b_ident], writes=[b_ident])
    P.add("dve", lambda e: e.tensor_copy(out=ident, in_=identf), reads=[b_ident], writes=[b_ident])
    gsrc = [a_norm_t.ap(), f_norm_t.ap()[0, :], kv_norm_t.ap(), b_norm_t.ap(), f_norm_t.ap()[1, :]]
    for i, g in enumerate(gsrc):
        P.add("sp", lambda e, i=i, g=g: e.dma_start(out=gP[:, i, :], in_=g.rearrange("(c p) -> p c", p=128),
                                                    allow_slow_non_contiguous=True), writes=[b_gP], dma=True)
    P.add("dve", lambda e: e.tensor_scalar_mul(out=gP[:, 5, :], in0=gP[:, 3, :], scalar1=0.125),
          reads=[b_gP], writes=[b_gP])
    for c in range(8):
        P.add("sp", lambda e, c=c: e.dma_start(
            out=convP[:, c, :], in_=a_conv_t.ap()[:, c * 128:(c + 1) * 128].rearrange("k p -> p k"),
            allow_slow_non_contiguous=True), writes=[b_convP], dma=True)
    P.add("sp", lambda e: e.dma_start(out=finalg, in_=final_norm_t.ap().unsqueeze(0).to_broadcast([128, D])),
          writes=[b_finalg], dma=True)
    P.add("pool", lambda e: e.memset(cvh, 0.0), writes=b_cvh)
    P.add("pool", lambda e: e.memset(st_v, 1.0), writes=[b_stv])

    stg_f = [xs[:, 0:2, :].rearrange("p a (b c) -> p (a b) c", c=512), xs[:, 2:4, :].rearrange("p a (b c) -> p (a b) c", c=512)]
    stg_b = [actT[:, 0:4, :], actT[:, 4:8, :]]
    b_sf = [P.buf("sf0"), P.buf("sf1")]
    b_sb = [P.buf("sb0"), P.buf("sb1")]
    prep_state = {"i": 0}
    wbufs = {}

    def wbuf(t):
        k = t.name
        if k not in wbufs:
            wbufs[k] = P.buf(k)
        return wbufs[k]

    def prep(src, k0, nk, c0, ncol, dsts, gain=None):
        i = prep_state["i"]
        prep_state["i"] += 1
        sl = i % 2
        sf = stg_f[sl][:, 0:nk, 0:ncol]
        sb = stg_b[sl][:, 0:nk, 0:ncol]
        srcv = src[k0 * 128:(k0 + nk) * 128, c0:c0 + ncol].rearrange("(k p) n -> p k n", p=128)
        P.add("sp", lambda e: e.dma_start(out=sf, in_=srcv), writes=[b_sf[sl]], dma=True)
        eng = ("dve", "act")[i % 2]
        if gain is None:
            if eng == "act":
                P.add("act", lambda e: e.activation(out=sb, in_=sf, func=ACT.Copy), reads=[b_sf[sl]], writes=[b_sb[sl]])
            else:
                P.add(eng, lambda e: e.tensor_copy(out=sb, in_=sf), reads=[b_sf[sl]], writes=[b_sb[sl]])
        else:
            def cast(e):
                for k in range(nk):
                    gcol = gP[:, gain, k0 + k:k0 + k + 1]
                    if eng == "act":
                        r = e.activation(out=sb[:, k, :], in_=sf[:, k, :], func=ACT.Copy, scale=gcol)
                    else:
                        r = e.tensor_scalar(out=sb[:, k, :], in0=sf[:, k, :], scalar1=gcol, scalar2=None, op0=ALU.mult)
                return r
            P.add(eng, cast, reads=[b_sf[sl], b_gP], writes=[b_sb[sl]])
        for (dt_, dk0, dc0) in dsts:
            dv = dt_.ap()[:, dk0:dk0 + nk, dc0:dc0 + ncol]
            P.add("pool", lambda e, dv=dv: e.dma_start(out=dv, in_=sb), reads=[b_sb[sl]], writes=[wbuf(dt_)], dma=True)

    def prep_mat(src, K, c0, ncols, dst, dc0, gain=None):
        nkc = K // 128
        for cc in range(0, ncols, 512):
            w = min(512, ncols - cc)
            for k0 in range(0, nkc, 4):
                nk = min(4, nkc - k0)
                prep(src, k0, nk, c0 + cc, w, [(dst, k0, dc0 + cc)], gain)

    for f in range(8):
        for k0 in (0, 4):
            prep(a_w_in_t.ap(), k0, 4, 1024 + f * 128, 128, [(win_s, k0, f * 384)], gain=0)
            prep(a_w_in_t.ap(), k0, 4, 2048 + f * 128, 128, [(win_s, k0, f * 384 + 128)], gain=0)
            prep(a_w_in_t.ap(), k0, 4, f * 128, 128, [(win_s, k0, f * 384 + 256)], gain=0)
    prep_mat(a_w_out_t.ap(), D, 0, 1024, wout_s, 0)
    for l in range(2):
        prep_mat(f_w_gu_t.ap()[l], D, 0, 2 * FF, wgu_s[l], 0, gain=(1 if l == 0 else 4))
        prep_mat(f_w_down_t.ap()[l], FF, 0, 1024, wdn_s[l], 0)
    wkv = w_kv_t.ap()
    for st_i, src_set in enumerate((0, 1)):
        for g in range(4):
            for k0 in (0, 4):
                prep(wkv, k0, 4, src_set * 256 + g * 64, 64,
                     [(wfm_s, k0, (st_i * 4 + g) * 128), (wfm_s, k0, (st_i * 4 + g) * 128 + 64)], gain=2)
    prep_mat(wkv, D, 2 * 256, 256, wfm_s, 1024, gain=2)
    prep_mat(wkv, D, 4 * 256, 256, wfm_s, 1280, gain=2)
    prep_mat(wkv, D, 3 * 256, 256, wtm_s, 0, gain=2)
    prep_mat(wkv, D, 5 * 256, 256, wtm_s, 256, gain=2)
    prep_mat(b_w_qg_t.ap(), D, 0, 1024, wq_s, 0, gain=5)
    prep_mat(b_w_qg_t.ap(), D, 1024, 48, wg_s, 0, gain=3)
    prep_mat(b_w_o_t.ap(), D, 0, 1024, wo_s, 0)
    for i, (w1, w2) in enumerate(((w1_k_t, w2_k_t), (w1_v_t, w2_v_t))):
        prep_mat(w1.ap(), 2048, 0, 256, w1_s[i], 0)
        prep_mat(w2.ap(), 256, 0, 64, w2_s[i], 0)

    ring_state = {"i": 0}

    def wload(dt_, k0, nk, c0, ncol):
        i = ring_state["i"]
        ring_state["i"] += 1
        sl = i % NSLOT
        dst = ring[sl][:, 0:nk, 0:ncol]
        srcv = dt_.ap()[:, k0:k0 + nk, c0:c0 + ncol]
        P.add("sp", lambda e: e.dma_start(out=dst, in_=srcv), reads=[wbuf(dt_)], writes=[b_ring[sl]], dma=True)
        return ring[sl], b_ring[sl]

    class WStream:
        def __init__(self, plan, depth=NSLOT - 1):
            self.plan = plan
            self.depth = depth
            self.loaded = []
            self.pos = 0
            for _ in range(min(depth, len(plan))):
                self._issue()

        def _issue(self):
            p = self.plan[len(self.loaded)]
            self.loaded.append(wload(*p))

        def get(self, expect=None):
            r = self.loaded[self.pos]
            if expect is not None:
                assert self.plan[self.pos][0] is expect, (self.plan[self.pos][0].name, expect.name)
            self.pos += 1
            return r

        def advance(self):
            if len(self.loaded) < len(self.plan):
                self._issue()

    def ffn_plan(l):
        pl = []
        for i in range(6):
            w = 512 if i < 5 else 256
            pl.append((wgu_s[l], 0, 8, i * 512, w))
            pl.append((wgu_s[l], 0, 8, FF + i * 512, w))
        for nh in range(2):
            for (k0, nk) in ((0, 8), (8, 8), (16, 6)):
                pl.append((wdn_s[l], k0, nk, nh * 512, 512))
        return pl

    def tile_plan1():
        pl = [(win_s, 0, 8, f * 384, 384) for f in range(8)]
        pl += [(wout_s, 0, 8, i * 512, 512) for i in range(2)]
        pl += ffn_plan(0)
        pl += [(wfm_s, 0, 8, i * 512, 512) for i in range(3)]
        pl += [(wtm_s, 0, 8, 0, 512)]
        pl += [(wq_s, 0, 8, i * 512, 512) for i in range(2)]
        pl += [(wg_s, 0, 8, 0, 48)]
        return pl

    def mm_group(ps, pb, lhs_fn, rhs_fn, nk, reads, n=None):
        def f(e):
            for k in range(nk):
                r = e.matmul(ps, lhsT=lhs_fn(k), rhs=rhs_fn(k), start=(k == 0), stop=(k == nk - 1))
            return r
        P.add("pe", f, reads=reads, writes=[pb])

    evict_rr = {"i": 0}

    def rstd_only():
        P.add("pool", lambda e: e.memset(ss, 0.0), writes=[b_ss])
        for j in range(4):
            P.add("act", lambda e, j=j: e.activation(out=junk, in_=xs[:, j, :], func=ACT.Square,
                                                      accum_out=ss[:, j:j + 1]),
                  reads=[b_xs[j]], writes=[b_ss, b_junk])
        P.add("dve", lambda e: e.tensor_scalar(out=rstd, in0=ss, scalar1=1.0 / D, scalar2=EPS, op0=ALU.mult, op1=ALU.add),
              reads=[b_ss], writes=[b_rstd])
        P.add("act", lambda e: e.activation(out=rstd, in_=rstd, func=ACT.Sqrt), reads=[b_rstd], writes=[b_rstd])
        P.add("dve", lambda e: e.reciprocal(out=rstd, in_=rstd), reads=[b_rstd], writes=[b_rstd])

    def transpose_hn():
        for j in range(4):
            tb, tbb = tbank()

            def tr(e, j=j, tb=tb):
                for k in range(8):
                    r = e.transpose(out=tb[:, k * 128:(k + 1) * 128], in_=hn[:, j, k * 128:(k + 1) * 128], identity=ident)
                return r
            P.add("pe", tr, reads=[b_hn[j], b_ident], writes=[tbb])
            eng = "act" if j % 2 == 0 else "dve"
            src = tb.rearrange("p (k t) -> p k t", k=8)
            dst = hnT[:, :, j * 128:(j + 1) * 128]
            if eng == "act":
                P.add("act", lambda e, src=src, dst=dst: e.activation(out=dst, in_=src, func=ACT.Copy),
                      reads=[tbb], writes=[b_hnT[j]])
            else:
                P.add("dve", lambda e, src=src, dst=dst: e.tensor_copy(out=dst, in_=src), reads=[tbb], writes=[b_hnT[j]])

    def norm_T():
        rstd_only()
        for j in range(4):
            if j % 2 == 0:
                P.add("act", lambda e, j=j: e.activation(out=hn[:, j, :], in_=xs[:, j, :], func=ACT.Copy, scale=rstd[:, j:j + 1]),
                      reads=[b_xs[j], b_rstd], writes=[b_hn[j]])
            else:
                P.add("dve", lambda e, j=j: e.tensor_scalar(out=hn[:, j, :], in0=xs[:, j, :], scalar1=rstd[:, j:j + 1],
                                                            scalar2=None, op0=ALU.mult),
                      reads=[b_xs[j], b_rstd], writes=[b_hn[j]])
        transpose_hn()

    def ffn(l, ws):
        for i in range(6):
            nch = 4 if i < 5 else 2
            gw, gb = ws.get(wgu_s[l])
            uw, ub = ws.get(wgu_s[l])
            for c in range(nch):
                fc = i * 4 + c
                pg, pgb = bank()
                mm_group(pg, pgb, lambda k, c=c, gw=gw: gw[:, k, c * 128:(c + 1) * 128], lambda k: hnT[:, k, :], 8,
                         [gb] + b_hnT)
                pu, pub = bank()
                mm_group(pu, pub, lambda k, c=c, uw=uw: uw[:, k, c * 128:(c + 1) * 128], lambda k: hnT[:, k, :], 8,
                         [ub] + b_hnT)
                sl = fc % 2
                P.add("act", lambda e, pg=pg, sl=sl: e.activation(out=silu_t[sl], in_=pg, func=ACT.Silu),
                      reads=[pgb], writes=[b_silu[sl]])
                P.add("dve", lambda e, pu=pu, sl=sl, fc=fc: e.tensor_tensor(out=actT[:, fc, :], in0=pu, in1=silu_t[sl], op=ALU.mult),
                      reads=[pub, b_silu[sl]], writes=[b_actT[fc]])
            ws.advance()
            ws.advance()
        for nh in range(2):
            pss = [bank() for _ in range(4)]
            for (k0, nk) in ((0, 8), (8, 8), (16, 6)):
                dw, db = ws.get(wdn_s[l]); ws.advance()
                for j in range(4):
                    ps, pb = pss[j]

                    def f(e, j=j, ps=ps, dw=dw, k0=k0, nk=nk):
                        for k in range(nk):
                            r = e.matmul(ps, lhsT=actT[:, k0 + k, j * 128:(j + 1) * 128], rhs=dw[:, k, :],
                                         start=(k0 + k == 0), stop=(k0 + k == 21))
                        return r
                    P.add("pe", f, reads=[db] + b_actT[k0:k0 + nk], writes=[pb])
            for j in range(4):
                ps, pb = pss[j]
                P.add("dve", lambda e, j=j, ps=ps, nh=nh: e.tensor_tensor(
                    out=xs[:, j, nh * 512:(nh + 1) * 512], in0=ps, in1=xs[:, j, nh * 512:(nh + 1) * 512], op=ALU.add),
                    reads=[pb, b_xs[j]], writes=[b_xs[j]])

    def down_proj_tm(ws, wt, srcT, b_src):
        for nh in range(2):
            w, wb = ws.get(wt); ws.advance()
            for j in range(4):
                ps, pb = bank()
                mm_group(ps, pb, lambda k, j=j: srcT[:, k, j * 128:(j + 1) * 128], lambda k, w=w: w[:, k, :], 8,
                         [wb] + b_src)
                P.add("dve", lambda e, j=j, ps=ps, nh=nh: e.tensor_tensor(
                    out=xs[:, j, nh * 512:(nh + 1) * 512], in0=ps, in1=xs[:, j, nh * 512:(nh + 1) * 512], op=ALU.add),
                    reads=[pb, b_xs[j]], writes=[b_xs[j]])

    b_kT = [P.buf(f"kT{t}") for t in range(NT)]
    b_kc2 = [P.buf(f"kc2{t}") for t in range(NT)]
    b_va = [P.buf(f"va{t}") for t in range(NT)]
    b_qT = [P.buf(f"qT{t}") for t in range(NT)]
    b_gate = [P.buf(f"gate{t}") for t in range(NT)]
    b_h1 = [P.buf(f"h1{t}") for t in range(NT)]
    b_o = [P.buf(f"o{t}") for t in range(NT)]
    out_ops = []

    x_ap = x_t.ap()

    P.barrier()

    kT_flat = kT_s.ap().rearrange("s d t -> (s d) t")
    qT_flat = qT_s.ap().rearrange("h d t -> (h d) t")

    def phase1(ntiles, final_stub=False):
        ws = WStream([p for _ in range(ntiles) for p in tile_plan1()])
        for t in range(ntiles):
            t0 = t * TT
            for j in range(4):
                P.add("sp", lambda e, t=t, t0=t0, j=j: e.dma_start(out=xs[:, j, :], in_=x_ap[t0 + j * 128:t0 + (j + 1) * 128, :]),
                      writes=[b_xs[j]], dma=True)
            norm_T()
            if debug and t == 0:
                d1 = nc.dram_tensor("dbg_hnT", [128, 8, TT], BF, kind="ExternalOutput")
                P.add("pool", lambda e: e.dma_start(out=d1.ap(), in_=hnT), reads=b_hnT, writes=[P.buf()], dma=True)
                d0 = nc.dram_tensor("dbg_rstd", [128, 4], F32, kind="ExternalOutput")
                P.add("pool", lambda e: e.dma_start(out=d0.ap(), in_=rstd), reads=[b_rstd], writes=[P.buf()], dma=True)
            for f in range(8):
                w, wb = ws.get(win_s); ws.advance()
                pc, pcb = bank()
                mm_group(pc, pcb, lambda k, w=w: w[:, k, 0:128], lambda k: hnT[:, k, :], 8, [wb] + b_hnT)
                pv, pvb = bank()
                mm_group(pv, pvb, lambda k, w=w: w[:, k, 128:256], lambda k: hnT[:, k, :], 8, [wb] + b_hnT)
                pq, pqb = bank()
                mm_group(pq, pqb, lambda k, w=w: w[:, k, 256:384], lambda k: hnT[:, k, :], 8, [wb] + b_hnT)
                sl = f % 2
                P.add("act", lambda e, pc=pc, sl=sl: e.activation(out=c_sb[sl], in_=pc, func=ACT.Copy),
                      reads=[pcb], writes=[b_csb[sl]])
                P.add("dve", lambda e, sl=sl, f=f: e.tensor_copy(out=cv[sl][:, 0:2], in_=cvh[:, f, :]),
                      reads=[b_cvh[f]], writes=[b_cv[sl]])
                P.add("dve", lambda e, pv=pv, sl=sl: e.tensor_tensor(out=cv[sl][:, 2:TT + 2], in0=pv, in1=c_sb[sl], op=ALU.mult),
                      reads=[pvb, b_csb[sl], b_cv[sl]], writes=[b_cv[sl]])
                P.add("dve", lambda e, sl=sl, f=f: e.tensor_copy(out=cvh[:, f, :], in_=cv[sl][:, TT:TT + 2]),
                      reads=[b_cv[sl]], writes=[b_cvh[f]])
                P.add("act", lambda e, sl=sl, f=f: e.activation(out=u_t[sl], in_=cv[sl][:, 2:TT + 2], func=ACT.Copy, scale=convP[:, f, 2:3]),
                      reads=[b_cv[sl], b_convP], writes=[b_u[sl]])
                P.add("dve", lambda e, sl=sl, f=f: e.scalar_tensor_tensor(out=u_t[sl], in0=cv[sl][:, 1:TT + 1], scalar=convP[:, f, 1:2],
                                                                           in1=u_t[sl], op0=ALU.mult, op1=ALU.add),
                      reads=[b_cv[sl], b_convP, b_u[sl]], writes=[b_u[sl]])
                P.add("dve", lambda e, sl=sl, f=f: e.scalar_tensor_tensor(out=u_t[sl], in0=cv[sl][:, 0:TT], scalar=convP[:, f, 0:1],
                                                                           in1=u_t[sl], op0=ALU.mult, op1=ALU.add),
                      reads=[b_cv[sl], b_convP, b_u[sl]], writes=[b_u[sl]])
                P.add("dve", lambda e, pq=pq, sl=sl, f=f: e.tensor_tensor(out=buT[:, f, :], in0=pq, in1=u_t[sl], op=ALU.mult),
                      reads=[pqb, b_u[sl]], writes=[b_buT[f]])
            if debug and t == 0:
                d6 = nc.dram_tensor("dbg_cvh", [128, 8, 2], F32, kind="ExternalOutput")
                P.add("pool", lambda e: e.dma_start(out=d6.ap(), in_=cvh), reads=b_cvh, writes=[P.buf()], dma=True)
            if debug and t == 1:
                d7 = nc.dram_tensor("dbg_buT1", [128, 8, TT], BF, kind="ExternalOutput")
                P.add("pool", lambda e: e.dma_start(out=d7.ap(), in_=buT), reads=b_buT, writes=[P.buf()], dma=True)
            if debug and t == 0:
                d2 = nc.dram_tensor("dbg_buT", [128, 8, TT], BF, kind="ExternalOutput")
                P.add("pool", lambda e: e.dma_start(out=d2.ap(), in_=buT), reads=b_buT, writes=[P.buf()], dma=True)
            down_proj_tm(ws, wout_s, buT, b_buT)
            if debug and t == 0:
                d3 = nc.dram_tensor("dbg_ha", [128, 4, D], F32, kind="ExternalOutput")
                P.add("pool", lambda e: e.dma_start(out=d3.ap(), in_=xs), reads=b_xs, writes=[P.buf()], dma=True)
            norm_T()
            if debug and t == 0:
                d4 = nc.dram_tensor("dbg_hnT2", [128, 8, TT], BF, kind="ExternalOutput")
                P.add("pool", lambda e: e.dma_start(out=d4.ap(), in_=hnT), reads=b_hnT, writes=[P.buf()], dma=True)
            ffn(0, ws)
            if debug and t == 0:
                d5 = nc.dram_tensor("dbg_actT", [128, 22, TT], BF, kind="ExternalOutput")
                P.add("pool", lambda e: e.dma_start(out=d5.ap(), in_=actT), reads=b_actT, writes=[P.buf()], dma=True)
            norm_T()
            for si in range(2):
                w, wb = ws.get(wfm_s); ws.advance()
                for g in range(4):
                    ps, pb = bank()
                    mm_group(ps, pb, lambda k, w=w, g=g: w[:, k, g * 128:(g + 1) * 128], lambda k: hnT[:, k, :], 8, [wb] + b_hnT)
                    pv2 = ps.rearrange("p (t two) -> p t two", two=2)
                    sg = si * 4 + g
                    P.add("act", lambda e, pv2=pv2, sg=sg: e.activation(out=st_kc[0:64, sg, :], in_=pv2[0:64, :, 0], func=ACT.Copy),
                          reads=[pb], writes=[b_stkc])
                    P.add("dve", lambda e, pv2=pv2, sg=sg: e.tensor_copy(out=st_kc[64:128, sg, :], in_=pv2[64:128, :, 1]),
                          reads=[pb], writes=[b_stkc])
            P.add("pool", lambda e, t=t, t0=t0: e.dma_start(out=kc2_s.ap()[:, :, t * 256:(t + 1) * 256].rearrange("s p c -> p s c"), in_=st_kc),
                  reads=[b_stkc], writes=[b_kc2[t]], dma=True)
            w, wb = ws.get(wfm_s); ws.advance()
            for pi in range(4):
                ps, pb = bank()
                mm_group(ps, pb, lambda k, w=w, pi=pi: w[:, k, pi * 128:(pi + 1) * 128], lambda k: hnT[:, k, :], 8, [wb] + b_hnT)
                sl = pi % 2
                if sl == 0:
                    P.add("act", lambda e, ps=ps, sl=sl: e.activation(out=st_k[sl], in_=ps, func=ACT.Copy), reads=[pb], writes=[b_stk[sl]])
                else:
                    P.add("dve", lambda e, ps=ps, sl=sl: e.tensor_copy(out=st_k[sl], in_=ps), reads=[pb], writes=[b_stk[sl]])
                P.add("pool", lambda e, t=t, t0=t0, pi=pi, sl=sl: e.dma_start(out=kT_flat[pi * 128:(pi + 1) * 128, t0:t0 + TT], in_=st_k[sl]),
                      reads=[b_stk[sl]], writes=[b_kT[t]], dma=True)
            w, wb = ws.get(wtm_s); ws.advance()
            for j in range(4):
                ps, pb = bank()
                mm_group(ps, pb, lambda k, j=j: hnT[:, k, j * 128:(j + 1) * 128], lambda k, w=w: w[:, k, :], 8, [wb] + b_hnT)
                src = ps.rearrange("p (s d) -> p s d", d=64)
                if j % 2 == 0:
                    P.add("act", lambda e, j=j, src=src: e.activation(out=st_v[:, j, :, 0:64], in_=src, func=ACT.Copy),
                          reads=[pb], writes=[b_stv])
                else:
                    P.add("dve", lambda e, j=j, src=src: e.tensor_copy(out=st_v[:, j, :, 0:64], in_=src), reads=[pb], writes=[b_stv])
            for sg in range(8):
                P.add("pool", lambda e, t=t, t0=t0, sg=sg: e.dma_start(out=va_s.ap()[:, sg, 4 * t:4 * t + 4, :], in_=st_v[:, :, sg, :]),
                      reads=[b_stv], writes=[b_va[t]], dma=True)
            for half in range(2):
                w, wb = ws.get(wq_s); ws.advance()
                for c4 in range(4):
                    c = half * 4 + c4
                    ps, pb = bank()
                    mm_group(ps, pb, lambda k, w=w, c4=c4: w[:, k, c4 * 128:(c4 + 1) * 128], lambda k: hnT[:, k, :], 8, [wb] + b_hnT)
                    sl = c % 2
                    if sl == 0:
                        P.add("act", lambda e, ps=ps, sl=sl: e.activation(out=st_k[sl], in_=ps, func=ACT.Copy), reads=[pb], writes=[b_stk[sl]])
                    else:
                        P.add("dve", lambda e, ps=ps, sl=sl: e.tensor_copy(out=st_k[sl], in_=ps), reads=[pb], writes=[b_stk[sl]])
                    P.add("pool", lambda e, t=t, t0=t0, c=c, sl=sl: e.dma_start(out=qT_flat[c * 128:(c + 1) * 128, t0:t0 + TT], in_=st_k[sl]),
                          reads=[b_stk[sl]], writes=[b_qT[t]], dma=True)
            w, wb = ws.get(wg_s); ws.advance()
            for j in range(4):
                ps, pb = bank()
                mm_group(ps[:, 0:48], pb, lambda k, j=j: hnT[:, k, j * 128:(j + 1) * 128], lambda k, w=w: w[:, k, 0:48], 8, [wb] + b_hnT)
                P.add("act", lambda e, j=j, ps=ps: e.activation(out=st_g[:, j, :], in_=ps[:, 0:48], func=ACT.Sigmoid),
                      reads=[pb], writes=[b_stg])
            P.add("pool", lambda e, t=t, t0=t0: e.dma_start(out=gate_s.ap()[t0:t0 + TT, :].rearrange("(j p) c -> p j c", p=128), in_=st_g),
                  reads=[b_stg], writes=[b_gate[t]], dma=True)
            for j in range(4):
                P.add("pool", lambda e, t=t, t0=t0, j=j: e.dma_start(out=h1_s.ap()[t0 + j * 128:t0 + (j + 1) * 128, :], in_=xs[:, j, :]),
                      reads=[b_xs[j]], writes=[b_h1[t]], dma=True)

            if final_stub:
                for j in range(4):
                    for hf in range(2):
                        tmp = c_sb[hf]
                        P.add("dve", lambda e, j=j, hf=hf, tmp=tmp: e.scalar_tensor_tensor(
                            out=tmp, in0=xs[:, j, hf * 512:(hf + 1) * 512], scalar=rstd[:, j:j + 1],
                            in1=finalg[:, hf * 512:(hf + 1) * 512], op0=ALU.mult, op1=ALU.mult),
                            reads=[b_xs[j], b_rstd, b_finalg], writes=[b_csb[hf]])
                        out_ops.append(P.add("pool", lambda e, j=j, hf=hf, tmp=tmp, t0=t0: e.dma_start(
                            out=out_t.ap()[t0 + j * 128:t0 + (j + 1) * 128, hf * 512:(hf + 1) * 512], in_=tmp),
                            reads=[b_csb[hf]], writes=[P.buf()], dma=True))

    SLOPES = [2.0 ** (-(h + 1) / 2.0) for h in range(NH)]

    def phase2a(nqt, ntiles_avail):
        P.barrier()
        A.reset(frame0)
        nkeys = ntiles_avail * TT
        nchunks_av = nkeys // 128
        b_const = P.buf("const2a")
        VMw = A.alloc([128, 2304], BF)
        Cm = A.alloc([128, 128], BF); Wm = A.alloc([128, 128], BF)
        AB = A.alloc([128, NH, 64], F32); CB = A.alloc([128, NH, 64], F32)
        Ttab = A.alloc([128, 255], F32)
        ctmp = A.alloc([128, 1024], F32); b_ctmp = P.buf("ctmp")
        kcT = A.alloc([64, 4, 512], BF); b_kcT = P.buf("kcT")
        cvr = A.alloc([128, 4, 4, 193], BF); b_cvr = P.buf("cvr")
        w1sb = A.alloc([128, 16, 256], BF); b_w1sb = P.buf("w1sb")
        w2sb = A.alloc([128, 2, 64], BF); b_w2sb = P.buf("w2sb")
        pe2f = A.alloc([128, 16], F32); pe2b = A.alloc([128, 16], BF); b_pe2 = P.buf("pe2")
        Xb = A.alloc([128, S // 2], BF); b_X = P.buf("X")
        H1 = A.alloc([128, 2, 512], BF); b_H1 = P.buf("H1")
        ks = A.alloc([128, S], BF); b_ks = P.buf("ks"); b_oh = P.buf("onehot")
        vs = A.alloc([128, 64, 65], BF); b_vs = P.buf("vs")
        kw = [A.alloc([64, 640], BF) for _ in range(2)]; b_kw = [P.buf() for _ in range(2)]
        vw = [A.alloc([128, 5, 65], BF) for _ in range(2)]; b_vw = [P.buf() for _ in range(2)]
        qt = [A.alloc([128, 512], BF) for _ in range(2)]; b_qt = [P.buf() for _ in range(2)]
        qtB = [A.alloc([128, 512], BF) for _ in range(2)]; b_qtB = [P.buf() for _ in range(2)]
        b_mA = [P.buf() for _ in range(2)]; b_mB = [P.buf() for _ in range(2)]
        selmW = A.alloc([128, 192], BF)
        gt = [A.alloc([128, 48], F32) for _ in range(2)]; b_gt = [P.buf() for _ in range(2)]
        pc = A.alloc([128, 4, 512], BF); b_pc = [P.buf() for _ in range(4)]
        NPP = 4
        pP = [A.alloc([128, 512], BF) for _ in range(NPP)]; b_pP = [P.buf() for _ in range(NPP)]
        Mb = [A.alloc([128, 512], BF) for _ in range(2)]; b_Mb = [P.buf() for _ in range(2)]
        acc = [A.alloc([128, 128], F32) for _ in range(2)]; b_acc = [P.buf() for _ in range(2)]
        wk = A.alloc([128, 128], F32); b_wk = P.buf("wk")
        m8 = A.alloc([128, 16], F32); b_m8 = P.buf("m8")
        selm = A.alloc([128, 128], BF); b_selm = P.buf("selm")
        rsum = A.alloc([128, 12], F32); b_rsum = P.buf("rsum")
        rinv = A.alloc([128, 12], F32); b_rinv = P.buf("rinv")
        fac = A.alloc([128, 12], F32); b_fac = P.buf("fac")
        t1 = A.alloc([128, 256], F32); t2 = A.alloc([128, 256], F32); t3 = A.alloc([128, 256], F32)
        b_t1 = P.buf("t1"); b_t2 = P.buf("t2"); b_t3 = P.buf("t3")
        ot = [A.alloc([128, 256], BF) for _ in range(2)]; b_ot = [P.buf() for _ in range(2)]
        zt = A.alloc([128, 512], BF); b_zt = P.buf("zt")
        P.add("pool", lambda e: e.memset(zt, 0.0), writes=[b_zt])

        def zero_acc(bk, ncol):
            P.add("pe", lambda e: e.matmul(bk[0][:, 0:ncol], lhsT=zt[:, 0:128], rhs=zt[:, 0:ncol], start=True, stop=False),
                  reads=[b_zt], writes=[bk[1]])

        def zbuild(e):
            r = None
            return r
        ohv = ctmp[64:128, 0:128]
        for half in range(2):
            P.add("pool", lambda e: e.memset(ctmp[64:128, 0:256], 1.0), writes=[b_ctmp])
            P.add("pool", lambda e, half=half: e.affine_select(out=ohv, in_=ohv, pattern=[[1, 128]], compare_op=ALU.is_equal,
                                                               fill=0.0, base=-64 * half, channel_multiplier=-1),
                  reads=[b_ctmp], writes=[b_ctmp])
            lo, hi = half * 64, half * 64 + 64
            P.add("dve", lambda e, lo=lo, hi=hi: e.tensor_copy(
                out=ks[64:128, lo * 64:hi * 64].rearrange("p (b k) -> p b k", k=64),
                in_=ctmp[64:128, lo:hi].unsqueeze(2).to_broadcast([64, 64, 64])), reads=[b_ctmp], writes=[b_oh])
        P.add("pool", lambda e: e.memset(selmW, 0.0), writes=[b_selm])
        for pz in range(3):
            x0 = pz * 768
            P.add("pool", lambda e: e.memset(ctmp[:, 0:768], 0.0), writes=[b_ctmp])
            P.add("pool", lambda e, x0=x0: e.affine_select(out=ctmp[:, 0:768], in_=ctmp[:, 0:768], pattern=[[1, 768]],
                                                           compare_op=ALU.is_ge, fill=NEGM, base=x0 - 31, channel_multiplier=-16),
                  reads=[b_ctmp], writes=[b_ctmp])
            P.add("dve", lambda e, x0=x0: e.tensor_copy(out=VMw[:, x0:x0 + 768], in_=ctmp[:, 0:768]), reads=[b_ctmp], writes=[b_const])
        for (M_, pat, base, cm) in ((Cm, [[1, 128]], 0, -1), (Wm, [[-1, 128]], -1, 1)):
            P.add("pool", lambda e: e.memset(ctmp[:, 0:128], 0.0), writes=[b_ctmp])
            P.add("pool", lambda e, pat=pat, base=base, cm=cm: e.affine_select(
                out=ctmp[:, 0:128], in_=ctmp[:, 0:128], pattern=pat, compare_op=ALU.is_ge, fill=NEGM, base=base, channel_multiplier=cm),
                reads=[b_ctmp], writes=[b_ctmp])
            P.add("dve", lambda e, M_=M_: e.tensor_copy(out=M_, in_=ctmp[:, 0:128]), reads=[b_ctmp], writes=[b_const])
        ov = ctmp[:, 0:512].rearrange("p (c j) -> p c j", c=4)
        P.add("pool", lambda e: e.memset(ctmp[:, 0:512], 1.0), writes=[b_ctmp])
        P.add("pool", lambda e: e.affine_select(out=ov, in_=ov, pattern=[[128, 4], [-4, 128]], compare_op=ALU.is_ge,
                                                fill=0.0, base=1, channel_multiplier=1), reads=[b_ctmp], writes=[b_ctmp])
        P.add("pool", lambda e: e.affine_select(out=ov, in_=ov, pattern=[[-128, 4], [4, 128]], compare_op=ALU.is_ge,
                                                fill=0.0, base=3, channel_multiplier=-1), reads=[b_ctmp], writes=[b_ctmp])
        P.add("pool", lambda e: e.memset(cvr, 0.0), writes=[b_cvr])
        for g in range(4):
            P.add("dve", lambda e, g=g: e.tensor_copy(out=cvr[:, g, :, 65:193], in_=ov), reads=[b_ctmp], writes=[b_cvr])
            P.add("dve", lambda e, g=g: e.memset(cvr[:, g, :, 64:65], 1.0), writes=[b_cvr])
        def tt(e):
            e.memset(Ttab[0:64, 0:126], 0.0); e.memset(Ttab[0:64, 126:128], 1e4); e.memset(Ttab[0:64, 128:255], -1e30)
            e.memset(Ttab[64:128, 0:127], 0.0); e.memset(Ttab[64:128, 127:129], 1e4)
            return e.memset(Ttab[64:128, 129:255], -1e30)
        P.add("pool", tt, writes=[b_const])
        P.add("pool", lambda e: e.iota(ctmp[:, 0:64], pattern=[[-128, 64]], base=-64, channel_multiplier=1,
                                       allow_small_or_imprecise_dtypes=True), writes=[b_ctmp])
        P.add("pool", lambda e: e.iota(ctmp[:, 64:128], pattern=[[-128, 64]], base=-48, channel_multiplier=16,
                                       allow_small_or_imprecise_dtypes=True), reads=[b_ctmp], writes=[b_ctmp])
        for h in range(NH):
            P.add("dve", lambda e, h=h: e.tensor_scalar(out=AB[:, h, :], in0=ctmp[:, 0:64], scalar1=SLOPES[h], scalar2=None, op0=ALU.mult),
                  reads=[b_ctmp], writes=[b_const])
            P.add("dve", lambda e, h=h: e.tensor_scalar(out=CB[:, h, :], in0=ctmp[:, 64:128], scalar1=-0.5, scalar2=SLOPES[h],
                                                        op0=ALU.add, op1=ALU.mult), reads=[b_ctmp], writes=[b_const])

        P.add("pool", lambda e: e.memset(kcT, 0.0), writes=[b_kcT])
        P.add("pool", lambda e: e.memset(H1, 0.0), writes=[b_H1])
        if nkeys < S:
            P.add("pool", lambda e: e.memset(Xb, 0.0), writes=[b_X])
        for si in range(2):
            pe_t = pe_k_t if si == 0 else pe_v_t
            P.add("sp", lambda e, si=si: e.dma_start(out=w1sb, in_=w1_s[si].ap()), reads=[wbuf(w1_s[si])], writes=[b_w1sb], dma=True)
            P.add("sp", lambda e, si=si: e.dma_start(out=w2sb, in_=w2_s[si].ap()), reads=[wbuf(w2_s[si])], writes=[b_w2sb], dma=True)
            pe_src = bass.AP(pe_t, 0, [[1, 128], [128, 16]])
            P.add("sp", lambda e, pe_src=pe_src: e.dma_start(out=pe2f, in_=pe_src, allow_slow_non_contiguous=True), writes=[b_pe2], dma=True)
            P.add("dve", lambda e: e.tensor_copy(out=pe2b, in_=pe2f), reads=[b_pe2], writes=[b_pe2])
            for hc in range(2):
                ps, pb = bank()

                def bm(e, ps=ps, hc=hc):
                    for lp in range(16):
                        r = e.matmul(ps[:, 0:1], lhsT=w1sb[:, lp, hc * 128:(hc + 1) * 128], rhs=pe2b[:, lp:lp + 1],
                                     start=(lp == 0), stop=(lp == 15))
                    return r
                P.add("pe", bm, reads=[b_w1sb, b_pe2], writes=[pb])
                P.add("dve", lambda e, ps=ps, hc=hc, si=si: e.tensor_copy(out=cbias[:, si, hc:hc + 1], in_=ps[:, 0:1]),
                      reads=[pb], writes=[b_cbias])
            for g in range(4):
                npair = nkeys // 2
                P.add("sp", lambda e, si=si, g=g, npair=npair: e.dma_start(out=Xb[:, 0:npair], in_=kc2_s.ap()[si * 4 + g, :, 0:npair]),
                      reads=b_kc2[:ntiles_avail], writes=[b_X], dma=True)
                for hc in range(2):
                    ps, pb = bank()

                    def cm_(e, ps=ps, hc=hc):
                        for lp in range(16):
                            r = e.matmul(ps[:, 0:511], lhsT=w1sb[:, lp, hc * 128:(hc + 1) * 128], rhs=Xb[:, lp:lp + 4081:8],
                                         start=(lp == 0), stop=(lp == 15))
                        return r
                    P.add("pe", cm_, reads=[b_w1sb, b_X], writes=[pb])
                    P.add("act", lambda e, ps=ps, hc=hc, si=si: e.activation(out=H1[:, hc, 0:511], in_=ps[:, 0:511], func=ACT.Silu,
                                                                             bias=cbias[:, si, hc:hc + 1], scale=1.0),
                          reads=[pb, b_cbias], writes=[b_H1])
                if si == 0:
                    ps, pb = bank()

                    def k2(e, ps=ps):
                        for hc in range(2):
                            r = e.matmul(ps[0:64, 0:511], lhsT=w2sb[:, hc, :], rhs=H1[:, hc, 0:511], start=(hc == 0), stop=(hc == 1))
                        return r
                    P.add("pe", k2, reads=[b_w2sb, b_H1], writes=[pb])
                    P.add("dve", lambda e, ps=ps, g=g: e.tensor_copy(out=kcT[:, g, 0:511], in_=ps[0:64, 0:511]), reads=[pb], writes=[b_kcT])
                else:
                    ps, pb = bank()

                    def v2(e, ps=ps):
                        for c in range(4):
                            for hc in range(2):
                                r = e.matmul(ps[:, c * 64:(c + 1) * 64], lhsT=H1[:, hc, c * 128:(c + 1) * 128], rhs=w2sb[:, hc, :],
                                             start=(hc == 0), stop=(hc == 1))
                        return r
                    P.add("pe", v2, reads=[b_w2sb, b_H1], writes=[pb])
                    P.add("dve", lambda e, ps=ps, g=g: e.tensor_copy(out=cvr[:, g, :, 0:64], in_=ps[:, 0:256].rearrange("p (c d) -> p c d", c=4)),
                          reads=[pb], writes=[b_cvr])

        bS = [(banks[0], bbufs[0]), (banks[1], bbufs[1])]
        bOA, bOB, bOs, bOw = (banks[2], bbufs[2]), (banks[3], bbufs[3]), (banks[4], bbufs[4]), (banks[5], bbufs[5])
        st2 = {"s": 0, "p": 0, "it": 0}

        def sbank():
            i = st2["s"]; st2["s"] = (i + 1) % 2
            return bS[i]

        def pslot():
            i = st2["p"]; st2["p"] = (i + 1) % NPP
            return pP[i], b_pP[i]

        def mask_mm(e, ps, M_):
            for h in range(4):
                r = e.matmul(ps[:, h * 128:(h + 1) * 128], lhsT=ident, rhs=M_, start=False, stop=True)
            return r

        RNG = []
        for g_ in range(4):
            smin = SLOPES[4 * g_ + 3]
            r_ = 1
            while smin * (128 * r_ - 127) < 100.0:
                r_ += 1
            RNG.append(r_)
        RNGH = []
        for h_ in range(NH):
            r_ = 1
            while SLOPES[h_] * (128 * r_ - 127) < 100.0:
                r_ += 1
            RNGH.append(r_)

        def qtile(g, qi):
            it = st2["it"]; st2["it"] += 1
            sl = it % 2
            tl = qi // 4
            c0 = max(0, qi - 4)
            nwc = qi - c0 + 1
            P.add("sp", lambda e, g=g, qi=qi, sl=sl: e.dma_start(
                out=qt[sl][0:64, :].rearrange("d (h t) -> d h t", h=4), in_=qT_s.ap()[4 * g:4 * g + 4, :, qi * 128:(qi + 1) * 128].rearrange("h d t -> d h t")),
                reads=[b_qT[tl]], writes=[b_qt[sl]], dma=True)
            if qi >= 32:
                P.add("sp", lambda e, g=g, qi=qi, sl=sl: e.dma_start(
                    out=qtB[sl][0:64, :].rearrange("d (h t) -> d h t", h=4), in_=qT_s.ap()[4 * g:4 * g + 4, :, qi * 128:(qi + 1) * 128].rearrange("h d t -> d h t")),
                    reads=[b_qT[tl]], writes=[b_qtB[sl]], dma=True)
            P.add("sp", lambda e, qi=qi, sl=sl: e.dma_start(out=gt[sl], in_=gate_s.ap()[qi * 128:(qi + 1) * 128, :]),
                  reads=[b_gate[tl]], writes=[b_gt[sl]], dma=True)
            P.add("sp", lambda e, g=g, qi=qi, sl=sl, c0=c0, nwc=nwc: e.dma_start(
                out=kw[sl][:, 0:nwc * 128], in_=kT_s.ap()[4 + g, :, c0 * 128:(qi + 1) * 128]),
                reads=b_kT[c0 // 4:tl + 1], writes=[b_kw[sl]], dma=True)
            P.add("sp", lambda e, g=g, qi=qi, sl=sl, c0=c0, nwc=nwc: e.dma_start(
                out=vw[sl][:, 0:nwc, :], in_=va_s.ap()[:, 4 + g, c0:qi + 1, :]),
                reads=b_va[c0 // 4:tl + 1], writes=[b_vw[sl]], dma=True)
            qv = qt[sl][0:64, :]
            zero_acc(bOA, 386); zero_acc(bOB, 386); zero_acc(bOw, 260); zero_acc(bOs, 260)
            mb = Mb[sl]; bmb = b_Mb[sl]
            ncc = qi // 16 + 1
            tasks = []

            def mk_cmp(c):
                dl = qi - 16 * c
                bk = {}

                def S_():
                    ps, pb = sbank(); bk["ps"] = ps; bk["pb"] = pb

                    def smm(e):
                        r = e.matmul(ps, lhsT=kcT[:, g, c * 128:(c + 1) * 128], rhs=qv, start=True, stop=(dl > 16))
                        if dl <= 16:
                            r = mask_mm(e, ps, VMw[:, 128 * dl:128 * dl + 128])
                        return r
                    P.add("pe", smm, reads=[b_kcT, b_qt[sl], b_const, b_ident], writes=[pb])

                def A_():
                    ps, pb = bk["ps"], bk["pb"]
                    for h in range(4):
                        P.add("act", lambda e, h=h: e.activation(
                            out=pc[:, c, h * 128:(h + 1) * 128], in_=ps[:, h * 128:(h + 1) * 128], func=ACT.Exp,
                            bias=CB[:, 4 * g + h, dl:dl + 1], scale=1.0), reads=[pb, b_const], writes=[b_pc[c]])

                def V_():
                    def omm(e):
                        for h in range(4):
                            ob = bOA[0] if h < 2 else bOB[0]
                            col = (h % 2) * 193
                            r = e.matmul(ob[:, col:col + 193], lhsT=pc[:, c, h * 128:(h + 1) * 128], rhs=cvr[:, g, c, :],
                                         start=False, stop=(c == ncc - 1))
                        return r
                    P.add("pe", omm, reads=[b_pc[c], b_cvr], writes=[bOA[1], bOB[1]])
                    if c == ncc - 1:
                        selection()
                return (S_, A_, V_)

            def selection():
                P.add("dve", lambda e: e.tensor_scalar_max(out=rsum[:, 0:2], in0=bOA[0][:, 64:258:193], scalar1=1e-30),
                      reads=[bOA[1]], writes=[b_rsum])
                P.add("dve", lambda e: e.tensor_scalar_max(out=rsum[:, 2:4], in0=bOB[0][:, 64:258:193], scalar1=1e-30),
                      reads=[bOB[1]], writes=[b_rsum])
                P.add("dve", lambda e: e.reciprocal(out=rinv[:, 0:4], in_=rsum[:, 0:4]), reads=[b_rsum], writes=[b_rinv])
                tsl = Ttab[:, 127 - 2 * qi:255 - 2 * qi]
                for h in range(4):
                    ob, obb = (bOA if h < 2 else bOB)
                    col = (h % 2) * 193 + 65
                    src1 = tsl if h == 0 else acc[(h + 1) % 2]
                    rb = [obb, b_rinv] + ([b_const] if h == 0 else [b_acc[(h + 1) % 2]])
                    P.add("dve", lambda e, ob=ob, col=col, h=h, src1=src1: e.scalar_tensor_tensor(
                        out=acc[h % 2], in0=ob[:, col:col + 128], scalar=rinv[:, h:h + 1], in1=src1, op0=ALU.mult, op1=ALU.add),
                        reads=rb, writes=[b_acc[h % 2]])
                af = acc[1]; baf = b_acc[1]
                P.add("dve", lambda e: e.tensor_scalar_add(out=af[:, 0:1], in0=af[:, 0:1], scalar1=1e4), reads=[baf], writes=[baf])
                P.add("dve", lambda e: e.max(out=m8[:, 0:8], in_=af), reads=[baf], writes=[b_m8])
                P.add("dve", lambda e: e.match_replace(out=wk, in_to_replace=m8[:, 0:8], in_values=af, imm_value=-3e38),
                      reads=[baf, b_m8], writes=[b_wk])
                P.add("dve", lambda e: e.max(out=m8[:, 8:16], in_=wk), reads=[b_wk], writes=[b_m8])
                P.add("dve", lambda e: e.tensor_scalar(out=selmW[:, 64:192], in0=af, scalar1=m8[:, 15:16], scalar2=-1.0, op0=ALU.is_ge, op1=ALU.add),
                      reads=[baf, b_m8], writes=[b_selm])
                tb, tbb = tbank()

                def trm(e):
                    r = e.transpose(out=tb[:, 0:128], in_=selmW[:, 0:128], identity=ident)
                    if qi >= 32:
                        r = e.transpose(out=tb[:, 128:256], in_=selmW[:, 64:192], identity=ident)
                    return r
                P.add("pe", trm, reads=[b_selm, b_ident], writes=[tbb])
                P.add("act", lambda e: e.activation(out=qt[sl][64:128, :].rearrange("p (h t) -> p h t", h=4),
                                                    in_=tb[64:128, 0:128].unsqueeze(1).to_broadcast([64, 4, 128]),
                                                    func=ACT.Copy, scale=30000.0), reads=[tbb], writes=[b_mA[sl]])
                if qi >= 32:
                    P.add("act", lambda e: e.activation(out=qtB[sl][64:128, :].rearrange("p (h t) -> p h t", h=4),
                                                        in_=tb[64:128, 128:256].unsqueeze(1).to_broadcast([64, 4, 128]),
                                                        func=ACT.Copy, scale=30000.0), reads=[tbb], writes=[b_mB[sl]])

            def mk_kv(c, kind):
                bk = {}
                rc = qi - c

                def S_():
                    ps, pb = sbank(); bk["ps"] = ps; bk["pb"] = pb
                    if kind == "win":
                        masked = (c == qi) or (c == qi - 4)

                        def wmm(e):
                            r = e.matmul(ps, lhsT=kw[sl][:, (c - c0) * 128:(c - c0 + 1) * 128], rhs=qv, start=True, stop=not masked)
                            if c == qi:
                                r = mask_mm(e, ps, Cm)
                            elif c == qi - 4:
                                r = mask_mm(e, ps, Wm)
                            return r
                        P.add("pe", wmm, reads=[b_kw[sl], b_qt[sl], b_const, b_ident], writes=[pb])
                    else:
                        rhs_t = qt[sl] if c < 32 else qtB[sl]
                        rb_ = [b_qt[sl], b_mA[sl]] if c < 32 else [b_qtB[sl], b_mB[sl]]

                        def s2(e):
                            r = e.matmul(ps, lhsT=ks[:, c * 128:(c + 1) * 128], rhs=rhs_t, start=True, stop=(c != qi))
                            if c == qi:
                                r = mask_mm(e, ps, Cm)
                            return r
                        P.add("pe", s2, reads=[b_ks, b_oh, b_const, b_ident] + rb_, writes=[pb])

                def A_():
                    ps, pb = bk["ps"], bk["pb"]
                    pt_, ptb = pslot(); bk["pt"] = pt_; bk["ptb"] = ptb
                    for h in range(4):
                        if rc >= RNGH[4 * g + h]:
                            continue
                        P.add("act", lambda e, h=h: e.activation(
                            out=pt_[:, h * 128:(h + 1) * 128], in_=ps[:, h * 128:(h + 1) * 128], func=ACT.Exp,
                            bias=AB[:, 4 * g + h, rc:rc + 1], scale=1.0), reads=[pb, b_const], writes=[ptb])

                def V_():
                    pt_, ptb = bk["pt"], bk["ptb"]
                    if kind == "win":
                        def wpv(e):
                            for h in range(4):
                                if rc >= RNGH[4 * g + h]:
                                    continue
                                r = e.matmul(bOw[0][:, h * 65:(h + 1) * 65], lhsT=pt_[:, h * 128:(h + 1) * 128], rhs=vw[sl][:, c - c0, :],
                                             start=False, stop=(c == qi))
                            return r
                        P.add("pe", wpv, reads=[ptb, b_vw[sl]], writes=[bOw[1]])
                    else:
                        def spv(e):
                            for h in range(4):
                                if rc >= RNGH[4 * g + h]:
                                    continue
                                r = e.matmul(bOs[0][:, h * 65:(h + 1) * 65], lhsT=pt_[:, h * 128:(h + 1) * 128], rhs=vs[:, c, :],
                                             start=False, stop=(c == qi))
                            return r
                        P.add("pe", spv, reads=[ptb, b_vs], writes=[bOs[1]])
                return (S_, A_, V_)

            for c in range(ncc):
                tasks.append(mk_cmp(c))
            for c in range(c0, qi + 1):
                tasks.append(mk_kv(c, "win"))
            for c in range(max(0, qi + 1 - RNG[g]), qi + 1):
                tasks.append(mk_kv(c, "slc"))
            tasks[0][0]()
            for ti in range(len(tasks)):
                if ti + 1 < len(tasks):
                    tasks[ti + 1][0]()
                tasks[ti][1]()
                tasks[ti][2]()
            P.add("dve", lambda e: e.reciprocal(out=rinv[:, 4:8], in_=bOs[0][:, 64:260:65]), reads=[bOs[1]], writes=[b_rinv])
            P.add("dve", lambda e: e.reciprocal(out=rinv[:, 8:12], in_=bOw[0][:, 64:260:65]), reads=[bOw[1]], writes=[b_rinv])
            gv = gt[sl].rearrange("p (h b) -> p h b", b=3)
            for b_ in range(3):
                P.add("dve", lambda e, b_=b_, gv=gv, g=g: e.tensor_tensor(out=fac[:, 4 * b_:4 * b_ + 4], in0=gv[:, 4 * g:4 * g + 4, b_],
                                                                          in1=rinv[:, 4 * b_:4 * b_ + 4], op=ALU.mult),
                      reads=[b_gt[sl], b_rinv], writes=[b_fac])
            for hb, (ob, obb) in enumerate((bOA, bOB)):
                P.add("dve", lambda e, hb=hb, ob=ob: e.tensor_tensor(
                    out=t1[:, hb * 128:(hb + 1) * 128].rearrange("p (h d) -> p h d", h=2),
                    in0=ob[:, 0:386].rearrange("p (h x) -> p h x", h=2)[:, :, 0:64],
                    in1=fac[:, 2 * hb:2 * hb + 2].unsqueeze(2).to_broadcast([128, 2, 64]), op=ALU.mult),
                    reads=[obb, b_fac], writes=[b_t1])
            P.add("dve", lambda e: e.tensor_tensor(out=t2.rearrange("p (h d) -> p h d", h=4),
                                                   in0=bOs[0][:, 0:260].rearrange("p (h x) -> p h x", h=4)[:, :, 0:64],
                                                   in1=fac[:, 4:8].unsqueeze(2).to_broadcast([128, 4, 64]), op=ALU.mult),
                  reads=[bOs[1], b_fac], writes=[b_t2])
            P.add("dve", lambda e: e.tensor_tensor(out=t3.rearrange("p (h d) -> p h d", h=4),
                                                   in0=bOw[0][:, 0:260].rearrange("p (h x) -> p h x", h=4)[:, :, 0:64],
                                                   in1=fac[:, 8:12].unsqueeze(2).to_broadcast([128, 4, 64]), op=ALU.mult),
                  reads=[bOw[1], b_fac], writes=[b_t3])
            P.add("pool", lambda e: e.tensor_tensor(out=t2, in0=t2, in1=t3, op=ALU.add), reads=[b_t2, b_t3], writes=[b_t2])
            o_ = ot[sl]
            P.add("pool", lambda e, o_=o_: e.tensor_tensor(out=o_, in0=t1, in1=t2, op=ALU.add), reads=[b_t1, b_t2], writes=[b_ot[sl]])
            P.add("pool", lambda e, o_=o_, qi=qi, g=g: e.dma_start(out=o_s.ap()[qi * 128:(qi + 1) * 128, g * 256:(g + 1) * 256], in_=o_),
                  reads=[b_ot[sl]], writes=[b_o[tl]], dma=True)

        for g in range(4):
            P.add("sp", lambda e, g=g: e.dma_start(out=ks[0:64, 0:nkeys], in_=kT_s.ap()[g, :, 0:nkeys]), reads=b_kT[:ntiles_avail], writes=[b_ks], dma=True)
            P.add("sp", lambda e, g=g: e.dma_start(out=vs[:, 0:nchunks_av, :], in_=va_s.ap()[:, g, 0:nchunks_av, :]),
                  reads=b_va[:ntiles_avail], writes=[b_vs], dma=True)
            for qi in range(nqt):
                qtile(g, qi)

    def phase2b(ntiles):
        P.barrier()
        pl = []
        for _ in range(ntiles):
            pl += [(wo_s, 0, 8, i * 512, 512) for i in range(2)]
            pl += ffn_plan(1)
        ws = WStream(pl)
        for t in range(ntiles):
            t0 = t * TT
            for j in range(4):
                P.add("sp", lambda e, j=j, t0=t0: e.dma_start(out=xs[:, j, :], in_=h1_s.ap()[t0 + j * 128:t0 + (j + 1) * 128, :]),
                      reads=[b_h1[t]], writes=[b_xs[j]], dma=True)
                P.add("sp", lambda e, j=j, t0=t0: e.dma_start(out=hn[:, j, :], in_=o_s.ap()[t0 + j * 128:t0 + (j + 1) * 128, :]),
                      reads=[b_o[t]], writes=[b_hn[j]], dma=True)
            transpose_hn()
            down_proj_tm(ws, wo_s, hnT, b_hnT)
            norm_T()
            ffn(1, ws)
            rstd_only()
            for j in range(4):
                for hf in range(2):
                    tmp = c_sb[hf]
                    P.add("dve", lambda e, j=j, hf=hf, tmp=tmp: e.scalar_tensor_tensor(
                        out=tmp, in0=xs[:, j, hf * 512:(hf + 1) * 512], scalar=rstd[:, j:j + 1],
                        in1=finalg[:, hf * 512:(hf + 1) * 512], op0=ALU.mult, op1=ALU.mult),
                        reads=[b_xs[j], b_rstd, b_finalg], writes=[b_csb[hf]])
                    out_ops.append(P.add("pool", lambda e, j=j, hf=hf, tmp=tmp, t0=t0: e.dma_start(
                        out=out_t.ap()[t0 + j * 128:t0 + (j + 1) * 128, hf * 512:(hf + 1) * 512], in_=tmp),
                        reads=[b_csb[hf]], writes=[P.buf()], dma=True))

    return dict(nc=nc, P=P, A=A, phase1=phase1, phase2a=phase2a, phase2b=phase2b, out_ops=out_ops, locals=locals())


_CACHE = {}


def kernel(**inputs):
    if "nc" not in _CACHE:
        ctx = build()
        ctx["phase1"](NT)
        ctx["phase2a"](NQ, NT)
        ctx["phase2b"](NT)
        ctx["P"].emit(final_dma_ops=ctx["out_ops"])
        _CACHE["nc"] = ctx["nc"]
    nc = _CACHE["nc"]
    f = {k: np.ascontiguousarray(np.asarray(v, dtype=np.float32)) for k, v in inputs.items()}
    in_maps = []
    for c in range(8):
        b = c // 2
        d = dict(f)
        d["x"] = np.ascontiguousarray(f["x"][b])
        for k in ("a_norm", "a_w_in", "a_conv", "a_w_out", "b_norm", "b_w_qg", "b_w_o"):
            d[k] = np.ascontiguousarray(f[k][0])
        in_maps.append(d)
    res = run_bass_kernel_spmd(nc, in_maps, core_ids=list(range(8)))
    out = np.stack([np.asarray(res.results[2 * b]["out"]) for b in range(4)], axis=0)
    return out.astype(np.float32)
```

```python
import contextlib
import numpy as np
import concourse.bass as bass
import concourse.mybir as mybir
from concourse.bass_utils import run_bass_kernel_spmd

F32 = mybir.dt.float32
BF = mybir.dt.bfloat16
ACT = mybir.ActivationFunctionType
ALU = mybir.AluOpType

D = 1024
S = 8192
FF = 2816
NH = 16
NG = 4
DH = 64
TT = 512
NT = S // TT
NQ = S // 128
NEGM = -30000.0
EPS = 1e-5

ENGS = ("pe", "act", "dve", "pool", "sp")


class Buf:
    __slots__ = ("name", "last_w", "readers")

    def __init__(self, name):
        self.name = name
        self.last_w = None
        self.readers = []


class Op:
    __slots__ = ("eng", "fn", "deps", "dma", "sig", "sem", "has_dep", "prewait")

    def __init__(self, eng, fn, dma):
        self.eng = eng
        self.fn = fn
        self.dma = dma
        self.deps = []
        self.sig = None
        self.sem = None
        self.has_dep = False
        self.prewait = None


class Prog:
    def __init__(self, nc, n_dma_sems=24):
        self.nc = nc
        self.ops = {e: [] for e in ENGS}
        self.n_dma_sems = n_dma_sems
        self.nbuf = 0
        self.dmas_since_barrier = []

    def buf(self, name=None):
        self.nbuf += 1
        return Buf(name or f"b{self.nbuf}")

    def add(self, eng, fn, reads=(), writes=(), dma=False):
        op = Op(eng, fn, dma)
        deps = {}

        def dep(p, kind):
            if p is None or p is op:
                return
            if p.eng == eng and not p.dma:
                if eng == "pe":
                    return
                if kind in ("war", "waw"):
                    return
            deps[id(p)] = p

        for r in reads:
            dep(r.last_w, "raw")
        for w in writes:
            dep(w.last_w, "waw")
            for rd in w.readers:
                dep(rd, "war")
        for r in reads:
            if dma:
                r.readers.append(op)
            else:
                r.readers = [x for x in r.readers if x.dma or x.eng != eng]
                r.readers.append(op)
        for w in writes:
            w.last_w = op
            w.readers = []
        op.deps = list(deps.values())
        for p in op.deps:
            p.has_dep = True
        self.ops[eng].append(op)
        if dma:
            self.dmas_since_barrier.append(op)
        return op

    def barrier(self):
        lasts = []
        for e in ENGS:
            for op in reversed(self.ops[e]):
                if not op.dma:
                    lasts.append(op)
                    break
        dm = list(self.dmas_since_barrier)
        self.dmas_since_barrier = []
        for e in ENGS:
            op = Op(e, lambda eng: eng.nop(), False)
            op.deps = [p for p in lasts if p.eng != e] + dm
            for p in op.deps:
                p.has_dep = True
            self.ops[e].append(op)

    def emit(self, final_dma_ops=()):
        nc = self.nc
        if final_dma_ops:
            fin = Op("sp", lambda e: e.nop(), False)
            fin.deps = list(final_dma_ops)
            for p in fin.deps:
                p.has_dep = True
            self.ops["sp"].append(fin)
        with contextlib.ExitStack() as st:
            esem = {e: st.enter_context(nc.semaphore(f"s_{e}")) for e in ENGS}
            dsem = {e: [st.enter_context(nc.semaphore(f"d_{e}{i}")) for i in range(self.n_dma_sems)]
                    for e in ("sp", "pool", "act")}
            for e in ENGS:
                cnt = 0
                dcnt = 0
                for op in self.ops[e]:
                    if op.dma:
                        R = self.n_dma_sems
                        op.sem = dsem[e][dcnt % R]
                        op.sig = 16 * (dcnt // R + 1)
                        if op.sig > 16:
                            op.prewait = (op.sem, op.sig - 16)
                        dcnt += 1
                    elif op.has_dep:
                        cnt += 1
                        op.sem = esem[e]
                        op.sig = cnt
            block = st.enter_context(nc.Block())

            def run(e, eng_obj):
                waited = {}
                for op in self.ops[e]:
                    ws = {}
                    if op.prewait:
                        ws[id(op.prewait[0])] = op.prewait
                    for p in op.deps:
                        k = id(p.sem)
                        if k not in ws or ws[k][1] < p.sig:
                            ws[k] = (p.sem, p.sig)
                    for k, (s, v) in ws.items():
                        if waited.get(k, 0) >= v:
                            continue
                        waited[k] = v
                        eng_obj.wait_ge(s, v)
                    inst = op.fn(eng_obj)
                    if op.dma:
                        inst.then_inc(op.sem, 16)
                    elif op.has_dep:
                        inst.then_inc(op.sem, 1)

            @block.sync
            def _(sync):
                run("sp", sync)

            @block.tensor
            def _(t):
                run("pe", t)

            @block.vector
            def _(v):
                run("dve", v)

            @block.scalar
            def _(a):
                run("act", a)

            @block.gpsimd
            def _(g):
                run("pool", g)


class Arena:
    def __init__(self, nc, base, top):
        self.nc = nc
        self.base = (base + 31) // 32 * 32
        self.top = top
        self.cur = self.base
        self.n = 0

    def mark(self):
        return self.cur

    def reset(self, m):
        self.cur = m

    def alloc(self, shape, dtype):
        nb = 4 if dtype == F32 else 2
        sz = nb
        for s in shape[1:]:
            sz *= s
        sz = (sz + 31) // 32 * 32
        off = self.cur
        assert off + sz <= self.top, ("SBUF overflow", off + sz - self.base, self.top - self.base)
        self.cur += sz
        self.n += 1
        return self.nc.alloc_sbuf_tensor_at(f"t{self.n}", list(shape), dtype, offset=off).ap()


def dram_ap(t, offset, pat):
    return bass.AP(t, offset, [list(p) for p in pat])


def build(debug=False):
    nc = bass.Bass("TRN2", target_bir_lowering=False)
    P = Prog(nc)
    A = Arena(nc, nc.sbuf_base, nc.sbuf_top)

    def din(name, shape):
        return nc.dram_tensor(name, list(shape), F32, kind="ExternalInput")

    x_t = din("x", [S, D])
    a_norm_t = din("a_norm", [D]); a_w_in_t = din("a_w_in", [D, 3 * D]); a_conv_t = din("a_conv", [3, D])
    a_w_out_t = din("a_w_out", [D, D]); kv_norm_t = din("kv_norm", [D]); w_kv_t = din("w_kv", [D, 1536])
    pe_k_t = din("cmp_pe_k", [32, 64]); w1_k_t = din("cmp_w1_k", [2048, 256]); w2_k_t = din("cmp_w2_k", [256, 64])
    pe_v_t = din("cmp_pe_v", [32, 64]); w1_v_t = din("cmp_w1_v", [2048, 256]); w2_v_t = din("cmp_w2_v", [256, 64])
    b_norm_t = din("b_norm", [D]); b_w_qg_t = din("b_w_qg", [D, 1072]); b_w_o_t = din("b_w_o", [D, D])
    f_norm_t = din("f_norm", [2, D]); f_w_gu_t = din("f_w_gu", [2, D, 2 * FF]); f_w_down_t = din("f_w_down", [2, FF, D])
    final_norm_t = din("final_norm", [D])
    out_t = nc.dram_tensor("out", [S, D], F32, kind="ExternalOutput")

    def scr(name, shape, dt=BF):
        return nc.dram_tensor(name, list(shape), dt)

    win_s = scr("win_s", [128, 8, 3072]); wout_s = scr("wout_s", [128, 8, 1024])
    wgu_s = [scr(f"wgu_s{l}", [128, 8, 2 * FF]) for l in range(2)]
    wdn_s = [scr(f"wdn_s{l}", [128, 22, 1024]) for l in range(2)]
    wfm_s = scr("wfm_s", [128, 8, 1536]); wtm_s = scr("wtm_s", [128, 8, 512])
    wq_s = scr("wq_s", [128, 8, 1024]); wg_s = scr("wg_s", [128, 8, 48]); wo_s = scr("wo_s", [128, 8, 1024])
    w1_s = [scr(f"w1_s{i}", [128, 16, 256]) for i in range(2)]
    w2_s = [scr(f"w2_s{i}", [128, 2, 64]) for i in range(2)]
    kT_s = scr("kT_s", [8, 64, S])
    kc2_s = scr("kc2_s", [8, 128, S // 2])
    va_s = scr("va_s", [128, 8, 64, 65])
    qT_s = scr("qT_s", [NH, 64, S])
    gate_s = scr("gate_s", [S, 48], F32)
    h1_s = scr("h1_s", [S, D], F32)
    o_s = scr("o_s", [S, D])

    NBANK = 6
    banks = [nc.alloc_psum_tensor(f"pb{i}", [128, 512], F32).ap() for i in range(NBANK)]
    bbufs = [P.buf(f"pb{i}") for i in range(NBANK)]
    tbanks = [nc.alloc_psum_tensor(f"tb{i}", [128, 1024], BF).ap() for i in range(2)]
    tbufs = [P.buf(f"tb{i}") for i in range(2)]
    bstate = {"b": 0, "t": 0}

    def bank():
        i = bstate["b"]
        bstate["b"] = (i + 1) % NBANK
        return banks[i], bbufs[i]

    def tbank():
        i = bstate["t"]
        bstate["t"] = (i + 1) % 2
        return tbanks[i], tbufs[i]

    ident = A.alloc([128, 128], BF); b_ident = P.buf("ident")
    identf = A.alloc([128, 128], F32)
    gP = A.alloc([128, 6, 8], F32); b_gP = P.buf("gP")
    convP = A.alloc([128, 8, 3], F32); b_convP = P.buf("convP")
    finalg = A.alloc([128, D], F32); b_finalg = P.buf("finalg")
    ss = A.alloc([128, 4], F32); b_ss = P.buf("ss")
    rstd = A.alloc([128, 4], F32); b_rstd = P.buf("rstd")
    cvh = A.alloc([128, 8, 2], F32); b_cvh = [P.buf(f"cvh{f}") for f in range(8)]
    cbias = A.alloc([128, 2, 2], F32); b_cbias = P.buf("cbias")
    frame0 = A.mark()

    xs = A.alloc([128, 4, D], F32); b_xs = [P.buf(f"xs{j}") for j in range(4)]
    junk = A.alloc([128, D], BF); b_junk = P.buf("junk")
    hn = A.alloc([128, 4, D], BF); b_hn = [P.buf(f"hn{j}") for j in range(4)]
    hnT = A.alloc([128, 8, TT], BF); b_hnT = [P.buf(f"hnT{j}") for j in range(4)]
    actT = A.alloc([128, 22, TT], BF); b_actT = [P.buf(f"actT{k}") for k in range(22)]
    silu_t = [A.alloc([128, TT], F32) for _ in range(2)]; b_silu = [P.buf() for _ in range(2)]
    NSLOT = 5
    ring = [A.alloc([128, 8, 512], BF) for _ in range(NSLOT)]; b_ring = [P.buf(f"ring{i}") for i in range(NSLOT)]
    c_sb = [A.alloc([128, TT], F32) for _ in range(2)]; b_csb = [P.buf() for _ in range(2)]
    cv = [A.alloc([128, TT + 2], F32) for _ in range(2)]; b_cv = [P.buf() for _ in range(2)]
    u_t = [A.alloc([128, TT], F32) for _ in range(2)]; b_u = [P.buf() for _ in range(2)]
    buT = A.alloc([128, 8, TT], BF); b_buT = [P.buf(f"buT{k}") for k in range(8)]
    st_k = [A.alloc([128, TT], BF) for _ in range(2)]; b_stk = [P.buf() for _ in range(2)]
    st_kc = A.alloc([128, 8, 256], BF); b_stkc = P.buf("stkc")
    st_v = A.alloc([128, 4, 8, 65], BF); b_stv = P.buf("stv")
    st_g = A.alloc([128, 4, 48], F32); b_stg = P.buf("stg")
    dense_end = A.mark()

    P.add("pool", lambda e: e.memset(identf, 0.0), writes=[b_ident])
    P.add("pool", lambda e: e.affine_select(out=identf, in_=identf, pattern=[[-1, 128]], compare_op=ALU.not_equal,
                                            fill=1.0, base=0, channel_multiplier=1), reads=[b_ident], writes=[b_ident])
    P.add("dve", lambda e: e.tensor_copy(out=ident, in_=identf), reads=[b_ident], writes=[b_ident])
    gsrc = [a_norm_t.ap(), f_norm_t.ap()[0, :], kv_norm_t.ap(), b_norm_t.ap(), f_norm_t.ap()[1, :]]
    for i, g in enumerate(gsrc):
        P.add("sp", lambda e, i=i, g=g: e.dma_start(out=gP[:, i, :], in_=g.rearrange("(c p) -> p c", p=128),
                                                    allow_slow_non_contiguous=True), writes=[b_gP], dma=True)
    P.add("dve", lambda e: e.tensor_scalar_mul(out=gP[:, 5, :], in0=gP[:, 3, :], scalar1=0.125),
          reads=[b_gP], writes=[b_gP])
    for c in range(8):
        P.add("sp", lambda e, c=c: e.dma_start(
            out=convP[:, c, :], in_=a_conv_t.ap()[:, c * 128:(c + 1) * 128].rearrange("k p -> p k"),
            allow_slow_non_contiguous=True), writes=[b_convP], dma=True)
    P.add("sp", lambda e: e.dma_start(out=finalg, in_=final_norm_t.ap().unsqueeze(0).to_broadcast([128, D])),
          writes=[b_finalg], dma=True)
    P.add("pool", lambda e: e.memset(cvh, 0.0), writes=b_cvh)
    P.add("pool", lambda e: e.memset(st_v, 1.0), writes=[b_stv])

    stg_f = [xs[:, 0:2, :].rearrange("p a (b c) -> p (a b) c", c=512), xs[:, 2:4, :].rearrange("p a (b c) -> p (a b) c", c=512)]
    stg_b = [actT[:, 0:4, :], actT[:, 4:8, :]]
    b_sf = [P.buf("sf0"), P.buf("sf1")]
    b_sb = [P.buf("sb0"), P.buf("sb1")]
    prep_state = {"i": 0}
    wbufs = {}

    def wbuf(t):
        k = t.name
        if k not in wbufs:
            wbufs[k] = P.buf(k)
        return wbufs[k]

    def prep(src, k0, nk, c0, ncol, dsts, gain=None):
        i = prep_state["i"]
        prep_state["i"] += 1
        sl = i % 2
        sf = stg_f[sl][:, 0:nk, 0:ncol]
        sb = stg_b[sl][:, 0:nk, 0:ncol]
        srcv = src[k0 * 128:(k0 + nk) * 128, c0:c0 + ncol].rearrange("(k p) n -> p k n", p=128)
        P.add("sp", lambda e: e.dma_start(out=sf, in_=srcv), writes=[b_sf[sl]], dma=True)
        eng = ("dve", "act")[i % 2]
        if gain is None:
            if eng == "act":
                P.add("act", lambda e: e.activation(out=sb, in_=sf, func=ACT.Copy), reads=[b_sf[sl]], writes=[b_sb[sl]])
            else:
                P.add(eng, lambda e: e.tensor_copy(out=sb, in_=sf), reads=[b_sf[sl]], writes=[b_sb[sl]])
        else:
            def cast(e):
                for k in range(nk):
                    gcol = gP[:, gain, k0 + k:k0 + k + 1]
                    if eng == "act":
                        r = e.activation(out=sb[:, k, :], in_=sf[:, k, :], func=ACT.Copy, scale=gcol)
                    else:
                        r = e.tensor_scalar(out=sb[:, k, :], in0=sf[:, k, :], scalar1=gcol, scalar2=None, op0=ALU.mult)
                return r
            P.add(eng, cast, reads=[b_sf[sl], b_gP], writes=[b_sb[sl]])
        for (dt_, dk0, dc0) in dsts:
            dv = dt_.ap()[:, dk0:dk0 + nk, dc0:dc0 + ncol]
            P.add("pool", lambda e, dv=dv: e.dma_start(out=dv, in_=sb), reads=[b_sb[sl]], writes=[wbuf(dt_)], dma=True)

    def prep_mat(src, K, c0, ncols, dst, dc0, gain=None):
        nkc = K // 128
        for cc in range(0, ncols, 512):
            w = min(512, ncols - cc)
            for k0 in range(0, nkc, 4):
                nk = min(4, nkc - k0)
                prep(src, k0, nk, c0 + cc, w, [(dst, k0, dc0 + cc)], gain)

    for f in range(8):
        for k0 in (0, 4):
            prep(a_w_in_t.ap(), k0, 4, 1024 + f * 128, 128, [(win_s, k0, f * 384)], gain=0)
            prep(a_w_in_t.ap(), k0, 4, 2048 + f * 128, 128, [(win_s, k0, f * 384 + 128)], gain=0)
            prep(a_w_in_t.ap(), k0, 4, f * 128, 128, [(win_s, k0, f * 384 + 256)], gain=0)
    prep_mat(a_w_out_t.ap(), D, 0, 1024, wout_s, 0)
    for l in range(2):
        prep_mat(f_w_gu_t.ap()[l], D, 0, 2 * FF, wgu_s[l], 0, gain=(1 if l == 0 else 4))
        prep_mat(f_w_down_t.ap()[l], FF, 0, 1024, wdn_s[l], 0)
    wkv = w_kv_t.ap()
    for st_i, src_set in enumerate((0, 1)):
        for g in range(4):
            for k0 in (0, 4):
                prep(wkv, k0, 4, src_set * 256 + g * 64, 64,
                     [(wfm_s, k0, (st_i * 4 + g) * 128), (wfm_s, k0, (st_i * 4 + g) * 128 + 64)], gain=2)
    prep_mat(wkv, D, 2 * 256, 256, wfm_s, 1024, gain=2)
    prep_mat(wkv, D, 4 * 256, 256, wfm_s, 1280, gain=2)
    prep_mat(wkv, D, 3 * 256, 256, wtm_s, 0, gain=2)
    prep_mat(wkv, D, 5 * 256, 256, wtm_s, 256, gain=2)
    prep_mat(b_w_qg_t.ap(), D, 0, 1024, wq_s, 0, gain=5)
    prep_mat(b_w_qg_t.ap(), D, 1024, 48, wg_s, 0, gain=3)
    prep_mat(b_w_o_t.ap(), D, 0, 1024, wo_s, 0)
    for i, (w1, w2) in enumerate(((w1_k_t, w2_k_t), (w1_v_t, w2_v_t))):
        prep_mat(w1.ap(), 2048, 0, 256, w1_s[i], 0)
        prep_mat(w2.ap(), 256, 0, 64, w2_s[i], 0)

    ring_state = {"i": 0}

    def wload(dt_, k0, nk, c0, ncol):
        i = ring_state["i"]
        ring_state["i"] += 1
        sl = i % NSLOT
        dst = ring[sl][:, 0:nk, 0:ncol]
        srcv = dt_.ap()[:, k0:k0 + nk, c0:c0 + ncol]
        P.add("sp", lambda e: e.dma_start(out=dst, in_=srcv), reads=[wbuf(dt_)], writes=[b_ring[sl]], dma=True)
        return ring[sl], b_ring[sl]

    class WStream:
        def __init__(self, plan, depth=NSLOT - 1):
            self.plan = plan
            self.depth = depth
            self.loaded = []
            self.pos = 0
            for _ in range(min(depth, len(plan))):
                self._issue()

        def _issue(self):
            p = self.plan[len(self.loaded)]
            self.loaded.append(wload(*p))

        def get(self, expect=None):
            r = self.loaded[self.pos]
            if expect is not None:
                assert self.plan[self.pos][0] is expect, (self.plan[self.pos][0].name, expect.name)
            self.pos += 1
            return r

        def advance(self):
            if len(self.loaded) < len(self.plan):
                self._issue()

    def ffn_plan(l):
        pl = []
        for i in range(6):
            w = 512 if i < 5 else 256
            pl.append((wgu_s[l], 0, 8, i * 512, w))
            pl.append((wgu_s[l], 0, 8, FF + i * 512, w))
        for nh in range(2):
            for (k0, nk) in ((0, 8), (8, 8), (16, 6)):
                pl.append((wdn_s[l], k0, nk, nh * 512, 512))
        return pl

    def tile_plan1():
        pl = [(win_s, 0, 8, f * 384, 384) for f in range(8)]
        pl += [(wout_s, 0, 8, i * 512, 512) for i in range(2)]
        pl += ffn_plan(0)
        pl += [(wfm_s, 0, 8, i * 512, 512) for i in range(3)]
        pl += [(wtm_s, 0, 8, 0, 512)]
        pl += [(wq_s, 0, 8, i * 512, 512) for i in range(2)]
        pl += [(wg_s, 0, 8, 0, 48)]
        return pl

    def mm_group(ps, pb, lhs_fn, rhs_fn, nk, reads, n=None):
        def f(e):
            for k in range(nk):
                r = e.matmul(ps, lhsT=lhs_fn(k), rhs=rhs_fn(k), start=(k == 0), stop=(k == nk - 1))
            return r
        P.add("pe", f, reads=reads, writes=[pb])

    evict_rr = {"i": 0}

    def rstd_only():
        P.add("pool", lambda e: e.memset(ss, 0.0), writes=[b_ss])
        for j in range(4):
            P.add("act", lambda e, j=j: e.activation(out=junk, in_=xs[:, j, :], func=ACT.Square,
                                                      accum_out=ss[:, j:j + 1]),
                  reads=[b_xs[j]], writes=[b_ss, b_junk])
        P.add("dve", lambda e: e.tensor_scalar(out=rstd, in0=ss, scalar1=1.0 / D, scalar2=EPS, op0=ALU.mult, op1=ALU.add),
              reads=[b_ss], writes=[b_rstd])
        P.add("act", lambda e: e.activation(out=rstd, in_=rstd, func=ACT.Sqrt), reads=[b_rstd], writes=[b_rstd])
        P.add("dve", lambda e: e.reciprocal(out=rstd, in_=rstd), reads=[b_rstd], writes=[b_rstd])

    def transpose_hn():
        for j in range(4):
            tb, tbb = tbank()

            def tr(e, j=j, tb=tb):
                for k in range(8):
                    r = e.transpose(out=tb[:, k * 128:(k + 1) * 128], in_=hn[:, j, k * 128:(k + 1) * 128], identity=ident)
                return r
            P.add("pe", tr, reads=[b_hn[j], b_ident], writes=[tbb])
            eng = "act" if j % 2 == 0 else "dve"
            src = tb.rearrange("p (k t) -> p k t", k=8)
            dst = hnT[:, :, j * 128:(j + 1) * 128]
            if eng == "act":
                P.add("act", lambda e, src=src, dst=dst: e.activation(out=dst, in_=src, func=ACT.Copy),
                      reads=[tbb], writes=[b_hnT[j]])
            else:
                P.add("dve", lambda e, src=src, dst=dst: e.tensor_copy(out=dst, in_=src), reads=[tbb], writes=[b_hnT[j]])

    def norm_T():
        rstd_only()
        for j in range(4):
            if j % 2 == 0:
                P.add("act", lambda e, j=j: e.activation(out=hn[:, j, :], in_=xs[:, j, :], func=ACT.Copy, scale=rstd[:, j:j + 1]),
                      reads=[b_xs[j], b_rstd], writes=[b_hn[j]])
            else:
                P.add("dve", lambda e, j=j: e.tensor_scalar(out=hn[:, j, :], in0=xs[:, j, :], scalar1=rstd[:, j:j + 1],
                                                            scalar2=None, op0=ALU.mult),
                      reads=[b_xs[j], b_rstd], writes=[b_hn[j]])
        transpose_hn()

    def ffn(l, ws):
        for i in range(6):
            nch = 4 if i < 5 else 2
            gw, gb = ws.get(wgu_s[l])
            uw, ub = ws.get(wgu_s[l])
            for c in range(nch):
                fc = i * 4 + c
                pg, pgb = bank()
                mm_group(pg, pgb, lambda k, c=c, gw=gw: gw[:, k, c * 128:(c + 1) * 128], lambda k: hnT[:, k, :], 8,
                         [gb] + b_hnT)
                pu, pub = bank()
                mm_group(pu, pub, lambda k, c=c, uw=uw: uw[:, k, c * 128:(c + 1) * 128], lambda k: hnT[:, k, :], 8,
                         [ub] + b_hnT)
                sl = fc % 2
                P.add("act", lambda e, pg=pg, sl=sl: e.activation(out=silu_t[sl], in_=pg, func=ACT.Silu),
                      reads=[pgb], writes=[b_silu[sl]])
                P.add("dve", lambda e, pu=pu, sl=sl, fc=fc: e.tensor_tensor(out=actT[:, fc, :], in0=pu, in1=silu_t[sl], op=ALU.mult),
                      reads=[pub, b_silu[sl]], writes=[b_actT[fc]])
            ws.advance()
            ws.advance()
        for nh in range(2):
            pss = [bank() for _ in range(4)]
            for (k0, nk) in ((0, 8), (8, 8), (16, 6)):
                dw, db = ws.get(wdn_s[l]); ws.advance()
                for j in range(4):
                    ps, pb = pss[j]

                    def f(e, j=j, ps=ps, dw=dw, k0=k0, nk=nk):
                        for k in range(nk):
                            r = e.matmul(ps, lhsT=actT[:, k0 + k, j * 128:(j + 1) * 128], rhs=dw[:, k, :],
                                         start=(k0 + k == 0), stop=(k0 + k == 21))
                        return r
                    P.add("pe", f, reads=[db] + b_actT[k0:k0 + nk], writes=[pb])
            for j in range(4):
                ps, pb = pss[j]
                P.add("dve", lambda e, j=j, ps=ps, nh=nh: e.tensor_tensor(
                    out=xs[:, j, nh * 512:(nh + 1) * 512], in0=ps, in1=xs[:, j, nh * 512:(nh + 1) * 512], op=ALU.add),
                    reads=[pb, b_xs[j]], writes=[b_xs[j]])

    def down_proj_tm(ws, wt, srcT, b_src):
        for nh in range(2):
            w, wb = ws.get(wt); ws.advance()
            for j in range(4):
                ps, pb = bank()
                mm_group(ps, pb, lambda k, j=j: srcT[:, k, j * 128:(j + 1) * 128], lambda k, w=w: w[:, k, :], 8,
                         [wb] + b_src)
                P.add("dve", lambda e, j=j, ps=ps, nh=nh: e.tensor_tensor(
                    out=xs[:, j, nh * 512:(nh + 1) * 512], in0=ps, in1=xs[:, j, nh * 512:(nh + 1) * 512], op=ALU.add),
                    reads=[pb, b_xs[j]], writes=[b_xs[j]])

    b_kT = [P.buf(f"kT{t}") for t in range(NT)]
    b_kc2 = [P.buf(f"kc2{t}") for t in range(NT)]
    b_va = [P.buf(f"va{t}") for t in range(NT)]
    b_qT = [P.buf(f"qT{t}") for t in range(NT)]
    b_gate = [P.buf(f"gate{t}") for t in range(NT)]
    b_h1 = [P.buf(f"h1{t}") for t in range(NT)]
    b_o = [P.buf(f"o{t}") for t in range(NT)]
    out_ops = []

    x_ap = x_t.ap()

    P.barrier()

    kT_flat = kT_s.ap().rearrange("s d t -> (s d) t")
    qT_flat = qT_s.ap().rearrange("h d t -> (h d) t")

    def phase1(ntiles, final_stub=False):
        ws = WStream([p for _ in range(ntiles) for p in tile_plan1()])
        for t in range(ntiles):
            t0 = t * TT
            for j in range(4):
                P.add("sp", lambda e, t=t, t0=t0, j=j: e.dma_start(out=xs[:, j, :], in_=x_ap[t0 + j * 128:t0 + (j + 1) * 128, :]),
                      writes=[b_xs[j]], dma=True)
            norm_T()
            if debug and t == 0:
                d1 = nc.dram_tensor("dbg_hnT", [128, 8, TT], BF, kind="ExternalOutput")
                P.add("pool", lambda e: e.dma_start(out=d1.ap(), in_=hnT), reads=b_hnT, writes=[P.buf()], dma=True)
                d0 = nc.dram_tensor("dbg_rstd", [128, 4], F32, kind="ExternalOutput")
                P.add("pool", lambda e: e.dma_start(out=d0.ap(), in_=rstd), reads=[b_rstd], writes=[P.buf()], dma=True)
            for f in range(8):
                w, wb = ws.get(win_s); ws.advance()
                pc, pcb = bank()
                mm_group(pc, pcb, lambda k, w=w: w[:, k, 0:128], lambda k: hnT[:, k, :], 8, [wb] + b_hnT)
                pv, pvb = bank()
                mm_group(pv, pvb, lambda k, w=w: w[:, k, 128:256], lambda k: hnT[:, k, :], 8, [wb] + b_hnT)
                pq, pqb = bank()
                mm_group(pq, pqb, lambda k, w=w: w[:, k, 256:384], lambda k: hnT[:, k, :], 8, [wb] + b_hnT)
                sl = f % 2
                P.add("act", lambda e, pc=pc, sl=sl: e.activation(out=c_sb[sl], in_=pc, func=ACT.Copy),
                      reads=[pcb], writes=[b_csb[sl]])
                P.add("dve", lambda e, sl=sl, f=f: e.tensor_copy(out=cv[sl][:, 0:2], in_=cvh[:, f, :]),
                      reads=[b_cvh[f]], writes=[b_cv[sl]])
                P.add("dve", lambda e, pv=pv, sl=sl: e.tensor_tensor(out=cv[sl][:, 2:TT + 2], in0=pv, in1=c_sb[sl], op=ALU.mult),
                      reads=[pvb, b_csb[sl], b_cv[sl]], writes=[b_cv[sl]])
                P.add("dve", lambda e, sl=sl, f=f: e.tensor_copy(out=cvh[:, f, :], in_=cv[sl][:, TT:TT + 2]),
                      reads=[b_cv[sl]], writes=[b_cvh[f]])
                P.add("act", lambda e, sl=sl, f=f: e.activation(out=u_t[sl], in_=cv[sl][:, 2:TT + 2], func=ACT.Copy, scale=convP[:, f, 2:3]),
                      reads=[b_cv[sl], b_convP], writes=[b_u[sl]])
                P.add("dve", lambda e, sl=sl, f=f: e.scalar_tensor_tensor(out=u_t[sl], in0=cv[sl][:, 1:TT + 1], scalar=convP[:, f, 1:2],
                                                                           in1=u_t[sl], op0=ALU.mult, op1=ALU.add),
                      reads=[b_cv[sl], b_convP, b_u[sl]], writes=[b_u[sl]])
                P.add("dve", lambda e, sl=sl, f=f: e.scalar_tensor_tensor(out=u_t[sl], in0=cv[sl][:, 0:TT], scalar=convP[:, f, 0:1],
                                                                           in1=u_t[sl], op0=ALU.mult, op1=ALU.add),
                      reads=[b_cv[sl], b_convP, b_u[sl]], writes=[b_u[sl]])
                P.add("dve", lambda e, pq=pq, sl=sl, f=f: e.tensor_tensor(out=buT[:, f, :], in0=pq, in1=u_t[sl], op=ALU.mult),
                      reads=[pqb, b_u[sl]], writes=[b_buT[f]])
            if debug and t == 0:
                d6 = nc.dram_tensor("dbg_cvh", [128, 8, 2], F32, kind="ExternalOutput")
                P.add("pool", lambda e: e.dma_start(out=d6.ap(), in_=cvh), reads=b_cvh, writes=[P.buf()], dma=True)
            if debug and t == 1:
                d7 = nc.dram_tensor("dbg_buT1", [128, 8, TT], BF, kind="ExternalOutput")
                P.add("pool", lambda e: e.dma_start(out=d7.ap(), in_=buT), reads=b_buT, writes=[P.buf()], dma=True)
            if debug and t == 0:
                d2 = nc.dram_tensor("dbg_buT", [128, 8, TT], BF, kind="ExternalOutput")
                P.add("pool", lambda e: e.dma_start(out=d2.ap(), in_=buT), reads=b_buT, writes=[P.buf()], dma=True)
            down_proj_tm(ws, wout_s, buT, b_buT)
            if debug and t == 0:
                d3 = nc.dram_tensor("dbg_ha", [128, 4, D], F32, kind="ExternalOutput")
                P.add("pool", lambda e: e.dma_start(out=d3.ap(), in_=xs), reads=b_xs, writes=[P.buf()], dma=True)
            norm_T()
            if debug and t == 0:
                d4 = nc.dram_tensor("dbg_hnT2", [128, 8, TT], BF, kind="ExternalOutput")
                P.add("pool", lambda e: e.dma_start(out=d4.ap(), in_=hnT), reads=b_hnT, writes=[P.buf()], dma=True)
            ffn(0, ws)
            if debug and t == 0:
                d5 = nc.dram_tensor("dbg_actT", [128, 22, TT], BF, kind="ExternalOutput")
                P.add("pool", lambda e: e.dma_start(out=d5.ap(), in_=actT), reads=b_actT, writes=[P.buf()], dma=True)
            norm_T()
            for si in range(2):
                w, wb = ws.get(wfm_s); ws.advance()
                for g in range(4):
                    ps, pb = bank()
                    mm_group(ps, pb, lambda k, w=w, g=g: w[:, k, g * 128:(g + 1) * 128], lambda k: hnT[:, k, :], 8, [wb] + b_hnT)
                    pv2 = ps.rearrange("p (t two) -> p t two", two=2)
                    sg = si * 4 + g
                    P.add("act", lambda e, pv2=pv2, sg=sg: e.activation(out=st_kc[0:64, sg, :], in_=pv2[0:64, :, 0], func=ACT.Copy),
                          reads=[pb], writes=[b_stkc])
                    P.add("dve", lambda e, pv2=pv2, sg=sg: e.tensor_copy(out=st_kc[64:128, sg, :], in_=pv2[64:128, :, 1]),
                          reads=[pb], writes=[b_stkc])
            P.add("pool", lambda e, t=t, t0=t0: e.dma_start(out=kc2_s.ap()[:, :, t * 256:(t + 1) * 256].rearrange("s p c -> p s c"), in_=st_kc),
                  reads=[b_stkc], writes=[b_kc2[t]], dma=True)
            w, wb = ws.get(wfm_s); ws.advance()
            for pi in range(4):
                ps, pb = bank()
                mm_group(ps, pb, lambda k, w=w, pi=pi: w[:, k, pi * 128:(pi + 1) * 128], lambda k: hnT[:, k, :], 8, [wb] + b_hnT)
                sl = pi % 2
                if sl == 0:
                    P.add("act", lambda e, ps=ps, sl=sl: e.activation(out=st_k[sl], in_=ps, func=ACT.Copy), reads=[pb], writes=[b_stk[sl]])
                else:
                    P.add("dve", lambda e, ps=ps, sl=sl: e.tensor_copy(out=st_k[sl], in_=ps), reads=[pb], writes=[b_stk[sl]])
                P.add("pool", lambda e, t=t, t0=t0, pi=pi, sl=sl: e.dma_start(out=kT_flat[pi * 128:(pi + 1) * 128, t0:t0 + TT], in_=st_k[sl]),
                      reads=[b_stk[sl]], writes=[b_kT[t]], dma=True)
            w, wb = ws.get(wtm_s); ws.advance()
            for j in range(4):
                ps, pb = bank()
                mm_group(ps, pb, lambda k, j=j: hnT[:, k, j * 128:(j + 1) * 128], lambda k, w=w: w[:, k, :], 8, [wb] + b_hnT)
                src = ps.rearrange("p (s d) -> p s d", d=64)
                if j % 2 == 0:
                    P.add("act", lambda e, j=j, src=src: e.activation(out=st_v[:, j, :, 0:64], in_=src, func=ACT.Copy),
                          reads=[pb], writes=[b_stv])
                else:
                    P.add("dve", lambda e, j=j, src=src: e.tensor_copy(out=st_v[:, j, :, 0:64], in_=src), reads=[pb], writes=[b_stv])
            for sg in range(8):
                P.add("pool", lambda e, t=t, t0=t0, sg=sg: e.dma_start(out=va_s.ap()[:, sg, 4 * t:4 * t + 4, :], in_=st_v[:, :, sg, :]),
                      reads=[b_stv], writes=[b_va[t]], dma=True)
            for half in range(2):
                w, wb = ws.get(wq_s); ws.advance()
                for c4 in range(4):
                    c = half * 4 + c4
                    ps, pb = bank()
                    mm_group(ps, pb, lambda k, w=w, c4=c4: w[:, k, c4 * 128:(c4 + 1) * 128], lambda k: hnT[:, k, :], 8, [wb] + b_hnT)
                    sl = c % 2
                    if sl == 0:
                        P.add("act", lambda e, ps=ps, sl=sl: e.activation(out=st_k[sl], in_=ps, func=ACT.Copy), reads=[pb], writes=[b_stk[sl]])
                    else:
                        P.add("dve", lambda e, ps=ps, sl=sl: e.tensor_copy(out=st_k[sl], in_=ps), reads=[pb], writes=[b_stk[sl]])
                    P.add("pool", lambda e, t=t, t0=t0, c=c, sl=sl: e.dma_start(out=qT_flat[c * 128:(c + 1) * 128, t0:t0 + TT], in_=st_k[sl]),
                          reads=[b_stk[sl]], writes=[b_qT[t]], dma=True)
            w, wb = ws.get(wg_s); ws.advance()
            for j in range(4):
                ps, pb = bank()
                mm_group(ps[:, 0:48], pb, lambda k, j=j: hnT[:, k, j * 128:(j + 1) * 128], lambda k, w=w: w[:, k, 0:48], 8, [wb] + b_hnT)
                P.add("act", lambda e, j=j, ps=ps: e.activation(out=st_g[:, j, :], in_=ps[:, 0:48], func=ACT.Sigmoid),
                      reads=[pb], writes=[b_stg])
            P.add("pool", lambda e, t=t, t0=t0: e.dma_start(out=gate_s.ap()[t0:t0 + TT, :].rearrange("(j p) c -> p j c", p=128), in_=st_g),
                  reads=[b_stg], writes=[b_gate[t]], dma=True)
            for j in range(4):
                P.add("pool", lambda e, t=t, t0=t0, j=j: e.dma_start(out=h1_s.ap()[t0 + j * 128:t0 + (j + 1) * 128, :], in_=xs[:, j, :]),
                      reads=[b_xs[j]], writes=[b_h1[t]], dma=True)

            if final_stub:
                for j in range(4):
                    for hf in range(2):
                        tmp = c_sb[hf]
                        P.add("dve", lambda e, j=j, hf=hf, tmp=tmp: e.scalar_tensor_tensor(
                            out=tmp, in0=xs[:, j, hf * 512:(hf + 1) * 512], scalar=rstd[:, j:j + 1],
                            in1=finalg[:, hf * 512:(hf + 1) * 512], op0=ALU.mult, op1=ALU.mult),
                            reads=[b_xs[j], b_rstd, b_finalg], writes=[b_csb[hf]])
                        out_ops.append(P.add("pool", lambda e, j=j, hf=hf, tmp=tmp, t0=t0: e.dma_start(
                            out=out_t.ap()[t0 + j * 128:t0 + (j + 1) * 128, hf * 512:(hf + 1) * 512], in_=tmp),
                            reads=[b_csb[hf]], writes=[P.buf()], dma=True))

    SLOPES = [2.0 ** (-(h + 1) / 2.0) for h in range(NH)]

    def phase2a(nqt, ntiles_avail):
        P.barrier()
        A.reset(frame0)
        nkeys = ntiles_avail * TT
        nchunks_av = nkeys // 128
        b_const = P.buf("const2a")
        VMw = A.alloc([128, 2304], BF)
        Cm = A.alloc([128, 128], BF); Wm = A.alloc([128, 128], BF)
        AB = A.alloc([128, NH, 64], F32); CB = A.alloc([128, NH, 64], F32)
        Ttab = A.alloc([128, 255], F32)
        ctmp = A.alloc([128, 1024], F32); b_ctmp = P.buf("ctmp")
        kcT = A.alloc([64, 4, 512], BF); b_kcT = P.buf("kcT")
        cvr = A.alloc([128, 4, 4, 193], BF); b_cvr = P.buf("cvr")
        w1sb = A.alloc([128, 16, 256], BF); b_w1sb = P.buf("w1sb")
        w2sb = A.alloc([128, 2, 64], BF); b_w2sb = P.buf("w2sb")
        pe2f = A.alloc([128, 16], F32); pe2b = A.alloc([128, 16], BF); b_pe2 = P.buf("pe2")
        Xb = A.alloc([128, S // 2], BF); b_X = P.buf("X")
        H1 = A.alloc([128, 2, 512], BF); b_H1 = P.buf("H1")
        ks = A.alloc([128, S], BF); b_ks = P.buf("ks"); b_oh = P.buf("onehot")
        vs = A.alloc([128, 64, 65], BF); b_vs = P.buf("vs")
        kw = [A.alloc([64, 640], BF) for _ in range(2)]; b_kw = [P.buf() for _ in range(2)]
        vw = [A.alloc([128, 5, 65], BF) for _ in range(2)]; b_vw = [P.buf() for _ in range(2)]
        qt = [A.alloc([128, 512], BF) for _ in range(2)]; b_qt = [P.buf() for _ in range(2)]
        qtB = [A.alloc([128, 512], BF) for _ in range(2)]; b_qtB = [P.buf() for _ in range(2)]
        b_mA = [P.buf() for _ in range(2)]; b_mB = [P.buf() for _ in range(2)]
        selmW = A.alloc([128, 192], BF)
        gt = [A.alloc([128, 48], F32) for _ in range(2)]; b_gt = [P.buf() for _ in range(2)]
        pc = A.alloc([128, 4, 512], BF); b_pc = [P.buf() for _ in range(4)]
        NPP = 4
        pP = [A.alloc([128, 512], BF) for _ in range(NPP)]; b_pP = [P.buf() for _ in range(NPP)]
        Mb = [A.alloc([128, 512], BF) for _ in range(2)]; b_Mb = [P.buf() for _ in range(2)]
        acc = [A.alloc([128, 128], F32) for _ in range(2)]; b_acc = [P.buf() for _ in range(2)]
        wk = A.alloc([128, 128], F32); b_wk = P.buf("wk")
        m8 = A.alloc([128, 16], F32); b_m8 = P.buf("m8")
        selm = A.alloc([128, 128], BF); b_selm = P.buf("selm")
        rsum = A.alloc([128, 12], F32); b_rsum = P.buf("rsum")
        rinv = A.alloc([128, 12], F32); b_rinv = P.buf("rinv")
        fac = A.alloc([128, 12], F32); b_fac = P.buf("fac")
        t1 = A.alloc([128, 256], F32); t2 = A.alloc([128, 256], F32); t3 = A.alloc([128, 256], F32)
        b_t1 = P.buf("t1"); b_t2 = P.buf("t2"); b_t3 = P.buf("t3")
        ot = [A.alloc([128, 256], BF) for _ in range(2)]; b_ot = [P.buf() for _ in range(2)]
        zt = A.alloc([128, 512], BF); b_zt = P.buf("zt")
        P.add("pool", lambda e: e.memset(zt, 0.0), writes=[b_zt])

        def zero_acc(bk, ncol):
            P.add("pe", lambda e: e.matmul(bk[0][:, 0:ncol], lhsT=zt[:, 0:128], rhs=zt[:, 0:ncol], start=True, stop=False),
                  reads=[b_zt], writes=[bk[1]])

        def zbuild(e):
            r = None
            return r
        ohv = ctmp[64:128, 0:128]
        for half in range(2):
            P.add("pool", lambda e: e.memset(ctmp[64:128, 0:256], 1.0), writes=[b_ctmp])
            P.add("pool", lambda e, half=half: e.affine_select(out=ohv, in_=ohv, pattern=[[1, 128]], compare_op=ALU.is_equal,
                                                               fill=0.0, base=-64 * half, channel_multiplier=-1),
                  reads=[b_ctmp], writes=[b_ctmp])
            lo, hi = half * 64, half * 64 + 64
            P.add("dve", lambda e, lo=lo, hi=hi: e.tensor_copy(
                out=ks[64:128, lo * 64:hi * 64].rearrange("p (b k) -> p b k", k=64),
                in_=ctmp[64:128, lo:hi].unsqueeze(2).to_broadcast([64, 64, 64])), reads=[b_ctmp], writes=[b_oh])
        P.add("pool", lambda e: e.memset(selmW, 0.0), writes=[b_selm])
        for pz in range(3):
            x0 = pz * 768
            P.add("pool", lambda e: e.memset(ctmp[:, 0:768], 0.0), writes=[b_ctmp])
            P.add("pool", lambda e, x0=x0: e.affine_select(out=ctmp[:, 0:768], in_=ctmp[:, 0:768], pattern=[[1, 768]],
                                                           compare_op=ALU.is_ge, fill=NEGM, base=x0 - 31, channel_multiplier=-16),
                  reads=[b_ctmp], writes=[b_ctmp])
            P.add("dve", lambda e, x0=x0: e.tensor_copy(out=VMw[:, x0:x0 + 768], in_=ctmp[:, 0:768]), reads=[b_ctmp], writes=[b_const])
        for (M_, pat, base, cm) in ((Cm, [[1, 128]], 0, -1), (Wm, [[-1, 128]], -1, 1)):
            P.add("pool", lambda e: e.memset(ctmp[:, 0:128], 0.0), writes=[b_ctmp])
            P.add("pool", lambda e, pat=pat, base=base, cm=cm: e.affine_select(
                out=ctmp[:, 0:128], in_=ctmp[:, 0:128], pattern=pat, compare_op=ALU.is_ge, fill=NEGM, base=base, channel_multiplier=cm),
                reads=[b_ctmp], writes=[b_ctmp])
            P.add("dve", lambda e, M_=M_: e.tensor_copy(out=M_, in_=ctmp[:, 0:128]), reads=[b_ctmp], writes=[b_const])
        ov = ctmp[:, 0:512].rearrange("p (c j) -> p c j", c=4)
        P.add("pool", lambda e: e.memset(ctmp[:, 0:512], 1.0), writes=[b_ctmp])
        P.add("pool", lambda e: e.affine_select(out=ov, in_=ov, pattern=[[128, 4], [-4, 128]], compare_op=ALU.is_ge,
                                                fill=0.0, base=1, channel_multiplier=1), reads=[b_ctmp], writes=[b_ctmp])
        P.add("pool", lambda e: e.affine_select(out=ov, in_=ov, pattern=[[-128, 4], [4, 128]], compare_op=ALU.is_ge,
                                                fill=0.0, base=3, channel_multiplier=-1), reads=[b_ctmp], writes=[b_ctmp])
        P.add("pool", lambda e: e.memset(cvr, 0.0), writes=[b_cvr])
        for g in range(4):
            P.add("dve", lambda e, g=g: e.tensor_copy(out=cvr[:, g, :, 65:193], in_=ov), reads=[b_ctmp], writes=[b_cvr])
            P.add("dve", lambda e, g=g: e.memset(cvr[:, g, :, 64:65], 1.0), writes=[b_cvr])
        def tt(e):
            e.memset(Ttab[0:64, 0:126], 0.0); e.memset(Ttab[0:64, 126:128], 1e4); e.memset(Ttab[0:64, 128:255], -1e30)
            e.memset(Ttab[64:128, 0:127], 0.0); e.memset(Ttab[64:128, 127:129], 1e4)
            return e.memset(Ttab[64:128, 129:255], -1e30)
        P.add("pool", tt, writes=[b_const])
        P.add("pool", lambda e: e.iota(ctmp[:, 0:64], pattern=[[-128, 64]], base=-64, channel_multiplier=1,
                                       allow_small_or_imprecise_dtypes=True), writes=[b_ctmp])
        P.add("pool", lambda e: e.iota(ctmp[:, 64:128], pattern=[[-128, 64]], base=-48, channel_multiplier=16,
                                       allow_small_or_imprecise_dtypes=True), reads=[b_ctmp], writes=[b_ctmp])
        for h in range(NH):
            P.add("dve", lambda e, h=h: e.tensor_scalar(out=AB[:, h, :], in0=ctmp[:, 0:64], scalar1=SLOPES[h], scalar2=None, op0=ALU.mult),
                  reads=[b_ctmp], writes=[b_const])
            P.add("dve", lambda e, h=h: e.tensor_scalar(out=CB[:, h, :], in0=ctmp[:, 64:128], scalar1=-0.5, scalar2=SLOPES[h],
                                                        op0=ALU.add, op1=ALU.mult), reads=[b_ctmp], writes=[b_const])

        P.add("pool", lambda e: e.memset(kcT, 0.0), writes=[b_kcT])
        P.add("pool", lambda e: e.memset(H1, 0.0), writes=[b_H1])
        if nkeys < S:
            P.add("pool", lambda e: e.memset(Xb, 0.0), writes=[b_X])
        for si in range(2):
            pe_t = pe_k_t if si == 0 else pe_v_t
            P.add("sp", lambda e, si=si: e.dma_start(out=w1sb, in_=w1_s[si].ap()), reads=[wbuf(w1_s[si])], writes=[b_w1sb], dma=True)
            P.add("sp", lambda e, si=si: e.dma_start(out=w2sb, in_=w2_s[si].ap()), reads=[wbuf(w2_s[si])], writes=[b_w2sb], dma=True)
            pe_src = bass.AP(pe_t, 0, [[1, 128], [128, 16]])
            P.add("sp", lambda e, pe_src=pe_src: e.dma_start(out=pe2f, in_=pe_src, allow_slow_non_contiguous=True), writes=[b_pe2], dma=True)
            P.add("dve", lambda e: e.tensor_copy(out=pe2b, in_=pe2f), reads=[b_pe2], writes=[b_pe2])
            for hc in range(2):
                ps, pb = bank()

                def bm(e, ps=ps, hc=hc):
                    for lp in range(16):
                        r = e.matmul(ps[:, 0:1], lhsT=w1sb[:, lp, hc * 128:(hc + 1) * 128], rhs=pe2b[:, lp:lp + 1],
                                     start=(lp == 0), stop=(lp == 15))
                    return r
                P.add("pe", bm, reads=[b_w1sb, b_pe2], writes=[pb])
                P.add("dve", lambda e, ps=ps, hc=hc, si=si: e.tensor_copy(out=cbias[:, si, hc:hc + 1], in_=ps[:, 0:1]),
                      reads=[pb], writes=[b_cbias])
            for g in range(4):
                npair = nkeys // 2
                P.add("sp", lambda e, si=si, g=g, npair=npair: e.dma_start(out=Xb[:, 0:npair], in_=kc2_s.ap()[si * 4 + g, :, 0:npair]),
                      reads=b_kc2[:ntiles_avail], writes=[b_X], dma=True)
                for hc in range(2):
                    ps, pb = bank()

                    def cm_(e, ps=ps, hc=hc):
                        for lp in range(16):
                            r = e.matmul(ps[:, 0:511], lhsT=w1sb[:, lp, hc * 128:(hc + 1) * 128], rhs=Xb[:, lp:lp + 4081:8],
                                         start=(lp == 0), stop=(lp == 15))
                        return r
                    P.add("pe", cm_, reads=[b_w1sb, b_X], writes=[pb])
                    P.add("act", lambda e, ps=ps, hc=hc, si=si: e.activation(out=H1[:, hc, 0:511], in_=ps[:, 0:511], func=ACT.Silu,
                                                                             bias=cbias[:, si, hc:hc + 1], scale=1.0),
                          reads=[pb, b_cbias], writes=[b_H1])
                if si == 0:
                    ps, pb = bank()

                    def k2(e, ps=ps):
                        for hc in range(2):
                            r = e.matmul(ps[0:64, 0:511], lhsT=w2sb[:, hc, :], rhs=H1[:, hc, 0:511], start=(hc == 0), stop=(hc == 1))
                        return r
                    P.add("pe", k2, reads=[b_w2sb, b_H1], writes=[pb])
                    P.add("dve", lambda e, ps=ps, g=g: e.tensor_copy(out=kcT[:, g, 0:511], in_=ps[0:64, 0:511]), reads=[pb], writes=[b_kcT])
                else:
                    ps, pb = bank()

                    def v2(e, ps=ps):
                        for c in range(4):
                            for hc in range(2):
                                r = e.matmul(ps[:, c * 64:(c + 1) * 64], lhsT=H1[:, hc, c * 128:(c + 1) * 128], rhs=w2sb[:, hc, :],
                                             start=(hc == 0), stop=(hc == 1))
                        return r
                    P.add("pe", v2, reads=[b_w2sb, b_H1], writes=[pb])
                    P.add("dve", lambda e, ps=ps, g=g: e.tensor_copy(out=cvr[:, g, :, 0:64], in_=ps[:, 0:256].rearrange("p (c d) -> p c d", c=4)),
                          reads=[pb], writes=[b_cvr])

        bS = [(banks[0], bbufs[0]), (banks[1], bbufs[1])]
        bOA, bOB, bOs, bOw = (banks[2], bbufs[2]), (banks[3], bbufs[3]), (banks[4], bbufs[4]), (banks[5], bbufs[5])
        st2 = {"s": 0, "p": 0, "it": 0}

        def sbank():
            i = st2["s"]; st2["s"] = (i + 1) % 2
            return bS[i]

        def pslot():
            i = st2["p"]; st2["p"] = (i + 1) % NPP
            return pP[i], b_pP[i]

        def mask_mm(e, ps, M_):
            for h in range(4):
                r = e.matmul(ps[:, h * 128:(h + 1) * 128], lhsT=ident, rhs=M_, start=False, stop=True)
            return r

        RNG = []
        for g_ in range(4):
            smin = SLOPES[4 * g_ + 3]
            r_ = 1
            while smin * (128 * r_ - 127) < 100.0:
                r_ += 1
            RNG.append(r_)
        RNGH = []
        for h_ in range(NH):
            r_ = 1
            while SLOPES[h_] * (128 * r_ - 127) < 100.0:
                r_ += 1
            RNGH.append(r_)

        def qtile(g, qi):
            it = st2["it"]; st2["it"] += 1
            sl = it % 2
            tl = qi // 4
            c0 = max(0, qi - 4)
            nwc = qi - c0 + 1
            P.add("sp", lambda e, g=g, qi=qi, sl=sl: e.dma_start(
                out=qt[sl][0:64, :].rearrange("d (h t) -> d h t", h=4), in_=qT_s.ap()[4 * g:4 * g + 4, :, qi * 128:(qi + 1) * 128].rearrange("h d t -> d h t")),
                reads=[b_qT[tl]], writes=[b_qt[sl]], dma=True)
            if qi >= 32:
                P.add("sp", lambda e, g=g, qi=qi, sl=sl: e.dma_start(
                    out=qtB[sl][0:64, :].rearrange("d (h t) -> d h t", h=4), in_=qT_s.ap()[4 * g:4 * g + 4, :, qi * 128:(qi + 1) * 128].rearrange("h d t -> d h t")),
                    reads=[b_qT[tl]], writes=[b_qtB[sl]], dma=True)
            P.add("sp", lambda e, qi=qi, sl=sl: e.dma_start(out=gt[sl], in_=gate_s.ap()[qi * 128:(qi + 1) * 128, :]),
                  reads=[b_gate[tl]], writes=[b_gt[sl]], dma=True)
            P.add("sp", lambda e, g=g, qi=qi, sl=sl, c0=c0, nwc=nwc: e.dma_start(
                out=kw[sl][:, 0:nwc * 128], in_=kT_s.ap()[4 + g, :, c0 * 128:(qi + 1) * 128]),
                reads=b_kT[c0 // 4:tl + 1], writes=[b_kw[sl]], dma=True)
            P.add("sp", lambda e, g=g, qi=qi, sl=sl, c0=c0, nwc=nwc: e.dma_start(
                out=vw[sl][:, 0:nwc, :], in_=va_s.ap()[:, 4 + g, c0:qi + 1, :]),
                reads=b_va[c0 // 4:tl + 1], writes=[b_vw[sl]], dma=True)
            qv = qt[sl][0:64, :]
            zero_acc(bOA, 386); zero_acc(bOB, 386); zero_acc(bOw, 260); zero_acc(bOs, 260)
            mb = Mb[sl]; bmb = b_Mb[sl]
            ncc = qi // 16 + 1
            tasks = []

            def mk_cmp(c):
                dl = qi - 16 * c
                bk = {}

                def S_():
                    ps, pb = sbank(); bk["ps"] = ps; bk["pb"] = pb

                    def smm(e):
                        r = e.matmul(ps, lhsT=kcT[:, g, c * 128:(c + 1) * 128], rhs=qv, start=True, stop=(dl > 16))
                        if dl <= 16:
                            r = mask_mm(e, ps, VMw[:, 128 * dl:128 * dl + 128])
                        return r
                    P.add("pe", smm, reads=[b_kcT, b_qt[sl], b_const, b_ident], writes=[pb])

                def A_():
                    ps, pb = bk["ps"], bk["pb"]
                    for h in range(4):
                        P.add("act", lambda e, h=h: e.activation(
                            out=pc[:, c, h * 128:(h + 1) * 128], in_=ps[:, h * 128:(h + 1) * 128], func=ACT.Exp,
                            bias=CB[:, 4 * g + h, dl:dl + 1], scale=1.0), reads=[pb, b_const], writes=[b_pc[c]])

                def V_():
                    def omm(e):
                        for h in range(4):
                            ob = bOA[0] if h < 2 else bOB[0]
                            col = (h % 2) * 193
                            r = e.matmul(ob[:, col:col + 193], lhsT=pc[:, c, h * 128:(h + 1) * 128], rhs=cvr[:, g, c, :],
                                         start=False, stop=(c == ncc - 1))
                        return r
                    P.add("pe", omm, reads=[b_pc[c], b_cvr], writes=[bOA[1], bOB[1]])
                    if c == ncc - 1:
                        selection()
                return (S_, A_, V_)

            def selection():
                P.add("dve", lambda e: e.tensor_scalar_max(out=rsum[:, 0:2], in0=bOA[0][:, 64:258:193], scalar1=1e-30),
                      reads=[bOA[1]], writes=[b_rsum])
                P.add("dve", lambda e: e.tensor_scalar_max(out=rsum[:, 2:4], in0=bOB[0][:, 64:258:193], scalar1=1e-30),
                      reads=[bOB[1]], writes=[b_rsum])
                P.add("dve", lambda e: e.reciprocal(out=rinv[:, 0:4], in_=rsum[:, 0:4]), reads=[b_rsum], writes=[b_rinv])
                tsl = Ttab[:, 127 - 2 * qi:255 - 2 * qi]
                for h in range(4):
                    ob, obb = (bOA if h < 2 else bOB)
                    col = (h % 2) * 193 + 65
                    src1 = tsl if h == 0 else acc[(h + 1) % 2]
                    rb = [obb, b_rinv] + ([b_const] if h == 0 else [b_acc[(h + 1) % 2]])
                    P.add("dve", lambda e, ob=ob, col=col, h=h, src1=src1: e.scalar_tensor_tensor(
                        out=acc[h % 2], in0=ob[:, col:col + 128], scalar=rinv[:, h:h + 1], in1=src1, op0=ALU.mult, op1=ALU.add),
                        reads=rb, writes=[b_acc[h % 2]])
                af = acc[1]; baf = b_acc[1]
                P.add("dve", lambda e: e.tensor_scalar_add(out=af[:, 0:1], in0=af[:, 0:1], scalar1=1e4), reads=[baf], writes=[baf])
                P.add("dve", lambda e: e.max(out=m8[:, 0:8], in_=af), reads=[baf], writes=[b_m8])
                P.add("dve", lambda e: e.match_replace(out=wk, in_to_replace=m8[:, 0:8], in_values=af, imm_value=-3e38),
                      reads=[baf, b_m8], writes=[b_wk])
                P.add("dve", lambda e: e.max(out=m8[:, 8:16], in_=wk), reads=[b_wk], writes=[b_m8])
                P.add("dve", lambda e: e.tensor_scalar(out=selmW[:, 64:192], in0=af, scalar1=m8[:, 15:16], scalar2=-1.0, op0=ALU.is_ge, op1=ALU.add),
                      reads=[baf, b_m8], writes=[b_selm])
            def sel_finish():
                tb, tbb = tbank()

                def trm(e):
                    r = e.transpose(out=tb[:, 0:128], in_=selmW[:, 0:128], identity=ident)
                    if qi >= 32:
                        r = e.transpose(out=tb[:, 128:256], in_=selmW[:, 64:192], identity=ident)
                    return r
                P.add("pe", trm, reads=[b_selm, b_ident], writes=[tbb])
                P.add("act", lambda e: e.activation(out=qt[sl][64:128, :].rearrange("p (h t) -> p h t", h=4),
                                                    in_=tb[64:128, 0:128].unsqueeze(1).to_broadcast([64, 4, 128]),
                                                    func=ACT.Copy, scale=30000.0), reads=[tbb], writes=[b_mA[sl]])
                if qi >= 32:
                    P.add("act", lambda e: e.activation(out=qtB[sl][64:128, :].rearrange("p (h t) -> p h t", h=4),
                                                        in_=tb[64:128, 128:256].unsqueeze(1).to_broadcast([64, 4, 128]),
                                                        func=ACT.Copy, scale=30000.0), reads=[tbb], writes=[b_mB[sl]])

            def mk_kv(c, kind):
                bk = {}
                rc = qi - c

                def S_():
                    ps, pb = sbank(); bk["ps"] = ps; bk["pb"] = pb
                    if kind == "win":
                        masked = (c == qi) or (c == qi - 4)

                        def wmm(e):
                            r = e.matmul(ps, lhsT=kw[sl][:, (c - c0) * 128:(c - c0 + 1) * 128], rhs=qv, start=True, stop=not masked)
                            if c == qi:
                                r = mask_mm(e, ps, Cm)
                            elif c == qi - 4:
                                r = mask_mm(e, ps, Wm)
                            return r
                        P.add("pe", wmm, reads=[b_kw[sl], b_qt[sl], b_const, b_ident], writes=[pb])
                    else:
                        rhs_t = qt[sl] if c < 32 else qtB[sl]
                        rb_ = [b_qt[sl], b_mA[sl]] if c < 32 else [b_qtB[sl], b_mB[sl]]

                        def s2(e):
                            r = e.matmul(ps, lhsT=ks[:, c * 128:(c + 1) * 128], rhs=rhs_t, start=True, stop=(c != qi))
                            if c == qi:
                                r = mask_mm(e, ps, Cm)
                            return r
                        P.add("pe", s2, reads=[b_ks, b_oh, b_const, b_ident] + rb_, writes=[pb])

                def A_():
                    ps, pb = bk["ps"], bk["pb"]
                    pt_, ptb = pslot(); bk["pt"] = pt_; bk["ptb"] = ptb
                    for h in range(4):
                        if rc >= RNGH[4 * g + h]:
                            continue
                        P.add("act", lambda e, h=h: e.activation(
                            out=pt_[:, h * 128:(h + 1) * 128], in_=ps[:, h * 128:(h + 1) * 128], func=ACT.Exp,
                            bias=AB[:, 4 * g + h, rc:rc + 1], scale=1.0), reads=[pb, b_const], writes=[ptb])

                def V_():
                    pt_, ptb = bk["pt"], bk["ptb"]
                    if kind == "win":
                        def wpv(e):
                            for h in range(4):
                                if rc >= RNGH[4 * g + h]:
                                    continue
                                r = e.matmul(bOw[0][:, h * 65:(h + 1) * 65], lhsT=pt_[:, h * 128:(h + 1) * 128], rhs=vw[sl][:, c - c0, :],
                                             start=False, stop=(c == qi))
                            return r
                        P.add("pe", wpv, reads=[ptb, b_vw[sl]], writes=[bOw[1]])
                    else:
                        def spv(e):
                            for h in range(4):
                                if rc >= RNGH[4 * g + h]:
                                    continue
                                r = e.matmul(bOs[0][:, h * 65:(h + 1) * 65], lhsT=pt_[:, h * 128:(h + 1) * 128], rhs=vs[:, c, :],
                                             start=False, stop=(c == qi))
                            return r
                        P.add("pe", spv, reads=[ptb, b_vs], writes=[bOs[1]])
                return (S_, A_, V_)

            for c in range(ncc):
                tasks.append(mk_cmp(c))
            for c in range(c0, qi + 1):
                tasks.append(mk_kv(c, "win"))
            first_slc_task = len(tasks)
            for c in range(max(0, qi + 1 - RNG[g]), qi + 1):
                tasks.append(mk_kv(c, "slc"))
            tasks[0][0]()
            for ti in range(len(tasks)):
                if ti + 1 < len(tasks):
                    if ti + 1 == first_slc_task:
                        sel_finish()
                    tasks[ti + 1][0]()
                tasks[ti][1]()
                tasks[ti][2]()
            P.add("dve", lambda e: e.reciprocal(out=rinv[:, 4:8], in_=bOs[0][:, 64:260:65]), reads=[bOs[1]], writes=[b_rinv])
            P.add("dve", lambda e: e.reciprocal(out=rinv[:, 8:12], in_=bOw[0][:, 64:260:65]), reads=[bOw[1]], writes=[b_rinv])
            gv = gt[sl].rearrange("p (h b) -> p h b", b=3)
            for b_ in range(3):
                P.add("dve", lambda e, b_=b_, gv=gv, g=g: e.tensor_tensor(out=fac[:, 4 * b_:4 * b_ + 4], in0=gv[:, 4 * g:4 * g + 4, b_],
                                                                          in1=rinv[:, 4 * b_:4 * b_ + 4], op=ALU.mult),
                      reads=[b_gt[sl], b_rinv], writes=[b_fac])
            for hb, (ob, obb) in enumerate((bOA, bOB)):
                P.add("dve", lambda e, hb=hb, ob=ob: e.tensor_tensor(
                    out=t1[:, hb * 128:(hb + 1) * 128].rearrange("p (h d) -> p h d", h=2),
                    in0=ob[:, 0:386].rearrange("p (h x) -> p h x", h=2)[:, :, 0:64],
                    in1=fac[:, 2 * hb:2 * hb + 2].unsqueeze(2).to_broadcast([128, 2, 64]), op=ALU.mult),
                    reads=[obb, b_fac], writes=[b_t1])
            P.add("dve", lambda e: e.tensor_tensor(out=t2.rearrange("p (h d) -> p h d", h=4),
                                                   in0=bOs[0][:, 0:260].rearrange("p (h x) -> p h x", h=4)[:, :, 0:64],
                                                   in1=fac[:, 4:8].unsqueeze(2).to_broadcast([128, 4, 64]), op=ALU.mult),
                  reads=[bOs[1], b_fac], writes=[b_t2])
            P.add("dve", lambda e: e.tensor_tensor(out=t3.rearrange("p (h d) -> p h d", h=4),
                                                   in0=bOw[0][:, 0:260].rearrange("p (h x) -> p h x", h=4)[:, :, 0:64],
                                                   in1=fac[:, 8:12].unsqueeze(2).to_broadcast([128, 4, 64]), op=ALU.mult),
                  reads=[bOw[1], b_fac], writes=[b_t3])
            P.add("pool", lambda e: e.tensor_tensor(out=t2, in0=t2, in1=t3, op=ALU.add), reads=[b_t2, b_t3], writes=[b_t2])
            o_ = ot[sl]
            P.add("pool", lambda e, o_=o_: e.tensor_tensor(out=o_, in0=t1, in1=t2, op=ALU.add), reads=[b_t1, b_t2], writes=[b_ot[sl]])
            P.add("pool", lambda e, o_=o_, qi=qi, g=g: e.dma_start(out=o_s.ap()[qi * 128:(qi + 1) * 128, g * 256:(g + 1) * 256], in_=o_),
                  reads=[b_ot[sl]], writes=[b_o[tl]], dma=True)

        for g in range(4):
            P.add("sp", lambda e, g=g: e.dma_start(out=ks[0:64, 0:nkeys], in_=kT_s.ap()[g, :, 0:nkeys]), reads=b_kT[:ntiles_avail], writes=[b_ks], dma=True)
            P.add("sp", lambda e, g=g: e.dma_start(out=vs[:, 0:nchunks_av, :], in_=va_s.ap()[:, g, 0:nchunks_av, :]),
                  reads=b_va[:ntiles_avail], writes=[b_vs], dma=True)
            for qi in range(nqt):
                qtile(g, qi)

    def phase2b(ntiles):
        P.barrier()
        pl = []
        for _ in range(ntiles):
            pl += [(wo_s, 0, 8, i * 512, 512) for i in range(2)]
            pl += ffn_plan(1)
        ws = WStream(pl)
        for t in range(ntiles):
            t0 = t * TT
            for j in range(4):
                P.add("sp", lambda e, j=j, t0=t0: e.dma_start(out=xs[:, j, :], in_=h1_s.ap()[t0 + j * 128:t0 + (j + 1) * 128, :]),
                      reads=[b_h1[t]], writes=[b_xs[j]], dma=True)
                P.add("sp", lambda e, j=j, t0=t0: e.dma_start(out=hn[:, j, :], in_=o_s.ap()[t0 + j * 128:t0 + (j + 1) * 128, :]),
                      reads=[b_o[t]], writes=[b_hn[j]], dma=True)
            transpose_hn()
            down_proj_tm(ws, wo_s, hnT, b_hnT)
            norm_T()
            ffn(1, ws)
            rstd_only()
            for j in range(4):
                for hf in range(2):
                    tmp = c_sb[hf]
                    P.add("dve", lambda e, j=j, hf=hf, tmp=tmp: e.scalar_tensor_tensor(
                        out=tmp, in0=xs[:, j, hf * 512:(hf + 1) * 512], scalar=rstd[:, j:j + 1],
                        in1=finalg[:, hf * 512:(hf + 1) * 512], op0=ALU.mult, op1=ALU.mult),
                        reads=[b_xs[j], b_rstd, b_finalg], writes=[b_csb[hf]])
                    out_ops.append(P.add("pool", lambda e, j=j, hf=hf, tmp=tmp, t0=t0: e.dma_start(
                        out=out_t.ap()[t0 + j * 128:t0 + (j + 1) * 128, hf * 512:(hf + 1) * 512], in_=tmp),
                        reads=[b_csb[hf]], writes=[P.buf()], dma=True))

    return dict(nc=nc, P=P, A=A, phase1=phase1, phase2a=phase2a, phase2b=phase2b, out_ops=out_ops, locals=locals())


_CACHE = {}


def kernel(**inputs):
    if "nc" not in _CACHE:
        ctx = build()
        ctx["phase1"](NT)
        ctx["phase2a"](NQ, NT)
        ctx["phase2b"](NT)
        ctx["P"].emit(final_dma_ops=ctx["out_ops"])
        _CACHE["nc"] = ctx["nc"]
    nc = _CACHE["nc"]
    f = {k: np.ascontiguousarray(np.asarray(v, dtype=np.float32)) for k, v in inputs.items()}
    in_maps = []
    for c in range(8):
        b = c // 2
        d = dict(f)
        d["x"] = np.ascontiguousarray(f["x"][b])
        for k in ("a_norm", "a_w_in", "a_conv", "a_w_out", "b_norm", "b_w_qg", "b_w_o"):
            d[k] = np.ascontiguousarray(f[k][0])
        in_maps.append(d)
    res = run_bass_kernel_spmd(nc, in_maps, core_ids=list(range(8)))
    out = np.stack([np.asarray(res.results[2 * b]["out"]) for b in range(4)], axis=0)
    return out.astype(np.float32)
```
